# Optimizing a Trainium2 kernel written in Bass

```python
import math
import jax, jax.numpy as jnp
from jax import lax
import numpy as np

D_MODEL = 1024
BATCH = 8
SEQ = 2048
DEPTH = 4
DEC_BATCH = 128
DEC_SEQ = 1
PAST_LEN = 2048
PAGE_SIZE = 128

N_A = DEPTH // 2
N_B = DEPTH - N_A
D_LRU = D_MODEL
LRU_BLOCK = 128
N_LRU_BLOCKS = D_LRU // LRU_BLOCK
LRU_CONV = 4
LRU_C = 8.0
N_HEADS = 8
HEAD_DIM = D_MODEL // (2 * N_HEADS)
D_FF = 3 * D_MODEL
FFN_CONV = 3
Q_BLOCK = 128
EPS = 1e-6

kernel_name = "yoco_rglru_diffattn_step"


def rms_norm(x, g):
    xf = x.astype(jnp.float32)
    y = xf * lax.rsqrt(jnp.mean(xf * xf, axis=-1, keepdims=True) + EPS)
    return (y * g.astype(jnp.float32)).astype(x.dtype)


def modulated_norm(x, g, shift, scale):
    return rms_norm(x, g) * (1 + scale) + shift


def causal_conv(x, buf, w, b):
    width = w.shape[0]
    t = x.shape[1]
    xp = jnp.concatenate([buf.astype(x.dtype), x], axis=1)
    y = b
    for j in range(width):
        y = y + xp[:, j:j + t] * w[j]
    return y, xp[:, t:]


def linear_scan(a, b, h0):
    def combine(left, right):
        a1, b1 = left
        a2, b2 = right
        return a1 * a2, a2 * b1 + b2
    a_cum, b_cum = lax.associative_scan(combine, (a, b), axis=1)
    return a_cum * h0[:, None] + b_cum


def rglru_block(h, p, i, h0, conv0):
    bsz, t, _ = h.shape
    f32 = jnp.float32
    xb, gb = jnp.split(h @ p['w_lru_in'][i], 2, axis=-1)
    xc, conv_new = causal_conv(xb, conv0, p['w_lru_conv'][i], p['b_lru_conv'][i])
    xblk = xc.reshape(bsz, t, N_LRU_BLOCKS, LRU_BLOCK)
    gate_x = jax.nn.sigmoid(jnp.einsum('btni,nij->btnj', xblk, p['w_gate_x'][i]).reshape(bsz, t, D_LRU) + p['b_gate_x'][i])
    gate_a = jax.nn.sigmoid(jnp.einsum('btni,nij->btnj', xblk, p['w_gate_a'][i]).reshape(bsz, t, D_LRU) + p['b_gate_a'][i])
    log_a = LRU_C * gate_a.astype(f32) * jax.nn.log_sigmoid(p['lru_log_param'][i].astype(f32))
    a = jnp.exp(log_a)
    b = jnp.sqrt(-jnp.expm1(2.0 * log_a)) * (gate_x * xc).astype(f32)
    hs = linear_scan(a, b, h0.astype(f32))
    y = hs.astype(h.dtype) * jax.nn.gelu(gb)
    return y @ p['w_lru_out'][i], hs[:, -1].astype(h.dtype), conv_new


def conv_ffn(h, p, l, buf):
    u = h @ p['w_up'][l]
    u, buf_new = causal_conv(u, buf, p['w_ffn_conv'][l], p['b_ffn_conv'][l])
    a, g = jnp.split(u, 2, axis=-1)
    return (jax.nn.gelu(a) * g) @ p['w_down'][l], buf_new


def alibi_slopes():
    return 2.0 ** (-8.0 * jnp.arange(1, N_HEADS + 1, dtype=jnp.float32) / N_HEADS)


def diff_attention(q, q_pos, segments, lam, slopes):
    scale = HEAD_DIM ** -0.5
    scores = []
    for k, _, k_pos in segments:
        s = jnp.einsum('bqhcd,bkhcd->bhcqk', q, k, preferred_element_type=jnp.float32) * scale
        dist = (q_pos[:, None] - k_pos[None, :]).astype(jnp.float32)
        s = s - slopes[:, None, None, None] * dist
        scores.append(jnp.where(dist >= 0, s, -jnp.inf))
    probs = jax.nn.softmax(jnp.concatenate(scores, axis=-1), axis=-1)
    attn = probs[:, :, 0] - lam * probs[:, :, 1]
    out = None
    off = 0
    for _, v, k_pos in segments:
        n = k_pos.shape[0]
        part = jnp.einsum('bhqk,bkhe->bqhe', attn[..., off:off + n].astype(v.dtype), v,
                          preferred_element_type=jnp.float32)
        out = part if out is None else out + part
        off += n
    return out.astype(q.dtype)


def prompt_attention(q, k, v, lam, slopes):
    bsz, s = q.shape[:2]
    nb = s // Q_BLOCK
    qb = jnp.moveaxis(q.reshape(bsz, nb, Q_BLOCK, N_HEADS, 2, HEAD_DIM), 1, 0)
    k_pos = jnp.arange(s)

    def one_block(args):
        qi, bi = args
        q_pos = bi * Q_BLOCK + jnp.arange(Q_BLOCK)
        return diff_attention(qi, q_pos, [(k, v, k_pos)], lam, slopes)

    out = lax.map(one_block, (qb, jnp.arange(nb)))
    return jnp.moveaxis(out, 0, 1).reshape(bsz, s, N_HEADS, 2 * HEAD_DIM)


def shared_kv(x, sc, p):
    bsz, t, _ = x.shape
    shift, scale = jnp.split((sc @ p['w_ada_kv'] + p['b_ada_kv'])[:, None], 2, axis=-1)
    h = modulated_norm(x, p['g_norm_kv'], shift, scale)
    k, v = jnp.split(h @ p['w_kv'], 2, axis=-1)
    k = rms_norm(k.reshape(bsz, t, N_HEADS, 2, HEAD_DIM), p['g_k_norm'])
    v = v.reshape(bsz, t, N_HEADS, 2 * HEAD_DIM)
    return k, v


def diff_attn_layer(h, p, j, lam_init, attend):
    bsz, t, _ = h.shape
    f32 = jnp.float32
    q = rms_norm((h @ p['w_q'][j]).reshape(bsz, t, N_HEADS, 2, HEAD_DIM), p['g_q_norm'][j])
    lam = (jnp.exp(jnp.sum(p['lam_q1'][j].astype(f32) * p['lam_k1'][j].astype(f32)))
           - jnp.exp(jnp.sum(p['lam_q2'][j].astype(f32) * p['lam_k2'][j].astype(f32))) + lam_init)
    o = attend(q, lam)
    o = rms_norm(o, p['g_subln'][j]) * (1.0 - lam_init)
    return o.reshape(bsz, t, N_HEADS * 2 * HEAD_DIM) @ p['w_o'][j]


def run_group(x, c, pos0, lru_h0, lru_conv0, ffn_conv0, kv_past, p):
    t = x.shape[1]
    slopes = alibi_slopes()
    q_pos = pos0 + jnp.arange(t)
    sc = jax.nn.silu(c)
    lru_h, lru_conv, ffn_conv = [], [], []
    k_new = v_new = None
    attend = None
    for l in range(DEPTH):
        mod = (sc @ p['w_ada'][l] + p['b_ada'][l])[:, None]
        sh1, s1, g1, sh2, s2, g2 = jnp.split(mod, 6, axis=-1)
        h = modulated_norm(x, p['g_norm_mix'][l], sh1, s1)
        if l < N_A:
            out, h_last, conv_new = rglru_block(h, p, l, lru_h0[l], lru_conv0[l])
            lru_h.append(h_last)
            lru_conv.append(conv_new)
        else:
            lam_init = 0.8 - 0.6 * math.exp(-0.3 * l)
            out = diff_attn_layer(h, p, l - N_A, lam_init, attend)
        x = x + g1 * out
        h = modulated_norm(x, p['g_norm_ffn'][l], sh2, s2)
        out, fb = conv_ffn(h, p, l, ffn_conv0[l])
        ffn_conv.append(fb)
        x = x + g2 * out
        if l == N_A - 1:
            k_new, v_new = shared_kv(x, sc, p)
            if kv_past is None:
                def attend(q, lam, k=k_new, v=v_new):
                    return prompt_attention(q, k, v, lam, slopes)
            else:
                k_past, v_past = kv_past
                past_pos = jnp.arange(k_past.shape[1])

                def attend(q, lam, k=k_new, v=v_new):
                    return diff_attention(q, q_pos, [(k_past, v_past, past_pos), (k, v, q_pos)], lam, slopes)
    return x, k_new, v_new, jnp.stack(lru_h), jnp.stack(lru_conv), jnp.stack(ffn_conv)


def setup_inputs(seed: int = 0) -> dict:
    key = jax.random.key(seed)
    ks = iter(jax.random.split(key, 64))

    def nrm(shape, scale):
        return jax.random.normal(next(ks), shape, jnp.float32) * scale

    def gain(shape):
        return 1.0 + nrm(shape, 0.02)

    n_pages = PAST_LEN // PAGE_SIZE
    n_used = DEC_BATCH * n_pages
    n_pool = n_used + n_used // 4
    page_table = jax.random.permutation(next(ks), n_pool)[:n_used].reshape(DEC_BATCH, n_pages).astype(jnp.int32)
    a0 = jax.random.uniform(next(ks), (N_A, D_LRU), jnp.float32, 0.9, 0.999)
    s0 = a0 ** (1.0 / LRU_C)
    lru_log_param = jnp.log(s0) - jnp.log1p(-s0)
    d = D_MODEL
    attn_w = 2 * N_HEADS * HEAD_DIM
    return {
        'x_prompt': nrm((BATCH, SEQ, d), 1.0),
        'x_sample': nrm((DEC_BATCH, DEC_SEQ, d), 1.0),
        'c_prompt': nrm((BATCH, d), 1.0),
        'c_sample': nrm((DEC_BATCH, d), 1.0),
        'cache_k': nrm((n_pool, PAGE_SIZE, N_HEADS, 2, HEAD_DIM), 1.0),
        'cache_v': nrm((n_pool, PAGE_SIZE, N_HEADS, 2 * HEAD_DIM), 1.0),
        'page_table': page_table,
        'state_lru_h': nrm((N_A, DEC_BATCH, D_LRU), 0.5),
        'state_lru_conv': nrm((N_A, DEC_BATCH, LRU_CONV - 1, D_LRU), 1.0),
        'state_ffn_conv': nrm((DEPTH, DEC_BATCH, FFN_CONV - 1, 2 * D_FF), 1.0),
        'w_ada': nrm((DEPTH, d, 6 * d), 0.5 * d ** -0.5),
        'b_ada': nrm((DEPTH, 6 * d), 0.02),
        'g_norm_mix': gain((DEPTH, d)),
        'g_norm_ffn': gain((DEPTH, d)),
        'w_lru_in': nrm((N_A, d, 2 * D_LRU), d ** -0.5),
        'w_lru_conv': nrm((N_A, LRU_CONV, D_LRU), LRU_CONV ** -0.5),
        'b_lru_conv': nrm((N_A, D_LRU), 0.02),
        'w_gate_x': nrm((N_A, N_LRU_BLOCKS, LRU_BLOCK, LRU_BLOCK), LRU_BLOCK ** -0.5),
        'b_gate_x': nrm((N_A, D_LRU), 0.02),
        'w_gate_a': nrm((N_A, N_LRU_BLOCKS, LRU_BLOCK, LRU_BLOCK), LRU_BLOCK ** -0.5),
        'b_gate_a': nrm((N_A, D_LRU), 0.02),
        'lru_log_param': lru_log_param,
        'w_lru_out': nrm((N_A, D_LRU, d), D_LRU ** -0.5),
        'w_ada_kv': nrm((d, 2 * d), 0.5 * d ** -0.5),
        'b_ada_kv': nrm((2 * d,), 0.02),
        'g_norm_kv': gain((d,)),
        'w_kv': nrm((d, 2 * attn_w), d ** -0.5),
        'g_k_norm': gain((HEAD_DIM,)),
        'w_q': nrm((N_B, d, attn_w), d ** -0.5),
        'g_q_norm': gain((N_B, HEAD_DIM)),
        'lam_q1': nrm((N_B, HEAD_DIM), 0.1),
        'lam_k1': nrm((N_B, HEAD_DIM), 0.1),
        'lam_q2': nrm((N_B, HEAD_DIM), 0.1),
        'lam_k2': nrm((N_B, HEAD_DIM), 0.1),
        'g_subln': gain((N_B, 2 * HEAD_DIM)),
        'w_o': nrm((N_B, attn_w, d), attn_w ** -0.5),
        'w_up': nrm((DEPTH, d, 2 * D_FF), d ** -0.5),
        'w_ffn_conv': nrm((DEPTH, FFN_CONV, 2 * D_FF), FFN_CONV ** -0.5),
        'b_ffn_conv': nrm((DEPTH, 2 * D_FF), 0.02),
        'w_down': nrm((DEPTH, D_FF, d), D_FF ** -0.5),
    }


def reference(x_prompt, x_sample, c_prompt, c_sample, cache_k, cache_v, page_table,
              state_lru_h, state_lru_conv, state_ffn_conv,
              w_ada, b_ada, g_norm_mix, g_norm_ffn,
              w_lru_in, w_lru_conv, b_lru_conv, w_gate_x, b_gate_x, w_gate_a, b_gate_a,
              lru_log_param, w_lru_out,
              w_ada_kv, b_ada_kv, g_norm_kv, w_kv, g_k_norm,
              w_q, g_q_norm, lam_q1, lam_k1, lam_q2, lam_k2, g_subln, w_o,
              w_up, w_ffn_conv, b_ffn_conv, w_down):
    p = dict(w_ada=w_ada, b_ada=b_ada, g_norm_mix=g_norm_mix, g_norm_ffn=g_norm_ffn,
             w_lru_in=w_lru_in, w_lru_conv=w_lru_conv, b_lru_conv=b_lru_conv,
             w_gate_x=w_gate_x, b_gate_x=b_gate_x, w_gate_a=w_gate_a, b_gate_a=b_gate_a,
             lru_log_param=lru_log_param, w_lru_out=w_lru_out,
             w_ada_kv=w_ada_kv, b_ada_kv=b_ada_kv, g_norm_kv=g_norm_kv, w_kv=w_kv, g_k_norm=g_k_norm,
             w_q=w_q, g_q_norm=g_q_norm, lam_q1=lam_q1, lam_k1=lam_k1, lam_q2=lam_q2, lam_k2=lam_k2,
             g_subln=g_subln, w_o=w_o, w_up=w_up, w_ffn_conv=w_ffn_conv, b_ffn_conv=b_ffn_conv,
             w_down=w_down)
    bp = x_prompt.shape[0]
    dt = x_prompt.dtype
    h0_p = jnp.zeros((N_A, bp, D_LRU), dt)
    lc0_p = jnp.zeros((N_A, bp, LRU_CONV - 1, D_LRU), dt)
    fc0_p = jnp.zeros((DEPTH, bp, FFN_CONV - 1, 2 * D_FF), dt)
    y_prompt, k_prompt, v_prompt, h_p, lc_p, fc_p = run_group(
        x_prompt, c_prompt, 0, h0_p, lc0_p, fc0_p, None, p)
    bs = x_sample.shape[0]
    past_len = page_table.shape[1] * PAGE_SIZE
    k_past = cache_k[page_table].reshape(bs, past_len, N_HEADS, 2, HEAD_DIM)
    v_past = cache_v[page_table].reshape(bs, past_len, N_HEADS, 2 * HEAD_DIM)
    y_sample, k_sample, v_sample, h_s, lc_s, fc_s = run_group(
        x_sample, c_sample, past_len, state_lru_h, state_lru_conv, state_ffn_conv, (k_past, v_past), p)
    return (y_prompt, y_sample, k_prompt, v_prompt, k_sample, v_sample,
            h_p, h_s, lc_p, lc_s, fc_p, fc_s)
```

```python
import math
import contextlib
import numpy as np
import concourse.bass as bass
import concourse.mybir as mybir
from concourse.bass_utils import run_bass_kernel_spmd

F32 = mybir.dt.float32
BF16 = mybir.dt.bfloat16
I32 = mybir.dt.int32
AF = mybir.ActivationFunctionType
ALU = mybir.AluOpType
AX = mybir.AxisListType

NCORES = 8
D = 1024
T = 2048
NS = 16
TA = T + NS
H = 8
DEPTH = 4
N_A = 2
NPG = 16
TILES = [(0, 512), (512, 512), (1024, 512), (1536, 512), (2048, 16)]
EPS = 1e-6
NEG = -30000.0
SLOPES = [2.0 ** (-(h + 1)) for h in range(H)]

ENGS = ["pe", "act", "dve", "pool", "sp"]
EPOCH = 20000


class Buf:
    __slots__ = ("name", "w", "r", "dkey", "dcount")

    def __init__(self, name=""):
        self.name = name
        self.w = None
        self.r = {}
        self.dkey = None
        self.dcount = 0


class _Rec:
    def __getattr__(self, name):
        def f(*a, **k):
            self.call = (name, a, k)
            return self
        return f


class Prog:
    def __init__(self, nc):
        self.nc = nc
        self.q = {e: [] for e in ENGS}
        self.cnt = {e: 0 for e in ENGS}
        self.waited = {e: {} for e in ENGS}
        self.keys = []
        self.keyset = set()
        self.final = {}
        self.nd = 0
        self.free_dkeys = []
        self.dbufs = []

    def _key(self, k):
        if k not in self.keyset:
            self.keyset.add(k)
            self.keys.append(k)
        return k

    def _collect(self, eng, reads, writes):
        need = {}

        def add(ev, raw):
            if ev is None:
                return
            k, v = ev
            if (not raw) and k[0] == eng:
                return
            if need.get(k, 0) < v:
                need[k] = v

        for b in reads:
            add(b.w, True)
        for b in writes:
            add(b.w, False)
            for k, v in b.r.items():
                add((k, v), False)
        waits = []
        wd = self.waited[eng]
        for k, v in need.items():
            if wd.get(k, 0) < v:
                wd[k] = v
                waits.append((k, v))
        return waits

    def _commit(self, ev, reads, writes):
        k, v = ev
        for b in reads:
            if b.r.get(k, 0) < v:
                b.r[k] = v
        for b in writes:
            b.w = ev
            b.r = {}
        if self.final.get(k, 0) < v:
            self.final[k] = v

    def op(self, eng, fn, reads=(), writes=()):
        rec = _Rec()
        fn(rec)
        name, a, k = rec.call

        def fn(e, name=name, a=a, k=k):
            return getattr(e, name)(*a, **k)
        waits = self._collect(eng, reads, writes)
        self.cnt[eng] += 1
        c = self.cnt[eng]
        key = self._key((eng, (c - 1) // EPOCH))
        val = (c - 1) % EPOCH + 1
        self._commit((key, val), reads, writes)
        self.q[eng].append((waits, fn, key, 1))

    def I(self, eng, method, reads, writes, *args, **kw):
        def fn(e, method=method, args=args, kw=kw):
            return getattr(e, method)(*args, **kw)
        self.op(eng, fn, reads=reads, writes=writes)

    def dma(self, eng, out, in_, reads=(), writes=(), fn=None):
        waits = self._collect(eng, reads, writes)
        prim = writes[0] if len(writes) else reads[0]
        if prim.dkey is None:
            if self.free_dkeys:
                prim.dkey = self.free_dkeys.pop()
            else:
                prim.dkey = self._key(("d", self.nd))
                self.nd += 1
            prim.dcount = self.final.get(prim.dkey, 0)
            self.dbufs.append(prim)
        prim.dcount += 16
        ev = (prim.dkey, prim.dcount)
        self._commit(ev, reads, writes)
        if fn is None:
            def fn(e, out=out, in_=in_):
                return e.dma_start(out=out, in_=in_)
        self.q[eng].append((waits, fn, prim.dkey, 16))

    def barrier(self):
        for b in self.dbufs:
            if b.dkey is not None:
                self.free_dkeys.append(b.dkey)
                b.dkey = None
        self.dbufs = []
        snap = dict(self.final)
        for eng in ENGS:
            wd = self.waited[eng]
            waits = []
            for k, v in snap.items():
                if wd.get(k, 0) < v:
                    wd[k] = v
                    waits.append((k, v))
            if waits:
                self.q[eng].append((waits, None, None, 0))

    def finish(self):
        wd = self.waited["sp"]
        waits = []
        for k, v in self.final.items():
            if wd.get(k, 0) < v:
                wd[k] = v
                waits.append((k, v))
        self.q["sp"].append((waits, None, None, 0))

    def emit(self):
        nc = self.nc
        with contextlib.ExitStack() as st:
            sems = {}
            for i, k in enumerate(self.keys):
                sems[k] = st.enter_context(nc.semaphore("s%d" % i))
            block = st.enter_context(nc.Block())

            def run(e, lst):
                for waits, fn, key, inc in lst:
                    for k, v in waits:
                        e.wait_ge(sems[k], v)
                    if fn is not None:
                        fn(e).then_inc(sems[key], inc)

            @block.tensor
            def _(e):
                run(e, self.q["pe"])

            @block.scalar
            def _(e):
                run(e, self.q["act"])

            @block.vector
            def _(e):
                run(e, self.q["dve"])

            @block.gpsimd
            def _(e):
                run(e, self.q["pool"])

            @block.sync
            def _(e):
                run(e, self.q["sp"])


class Arena:
    def __init__(self, tensor, nfloats):
        self.t = tensor
        self.n = nfloats
        self.off = 0

    def reset(self):
        self.off = 0

    def alloc(self, shape, dt=F32, parts=128):
        n = 1
        for s in shape:
            n *= s
        nbytes = n * (2 if dt == BF16 else 4)
        nf = (nbytes + 3) // 4
        nf = (nf + 7) // 8 * 8
        assert self.off + nf <= self.n, ("arena overflow", self.off, nf, self.n)
        ap = self.t[0:parts, self.off:self.off + nf]
        if dt != F32:
            ap = ap.bitcast(dt)
        ap = ap[:, 0:n]
        if len(shape) == 2:
            ap = ap.rearrange("p (a b) -> p a b", a=shape[0])
        elif len(shape) == 3:
            ap = ap.rearrange("p (a b c) -> p a b c", a=shape[0], b=shape[1])
        self.last_off = self.off
        self.off += nf
        return ap


def _vec_layout():
    ent = []

    def add(name, ncols):
        ent.append((name, ncols))

    for l in range(DEPTH):
        add("b_ada%d" % l, 48)
        add("g_mix%d" % l, 8)
        add("g_ffn%d" % l, 8)
        for j in range(3):
            add("w_fc%d_%d" % (l, j), 48)
        add("b_fc%d" % l, 48)
    for i in range(N_A):
        for j in range(4):
            add("w_lc%d_%d" % (i, j), 8)
        add("b_lc%d" % i, 8)
        add("b_gx%d" % i, 8)
        add("b_ga%d" % i, 8)
        add("lrul%d" % i, 8)
    add("b_ada_kv", 16)
    add("g_kv", 8)
    add("gk2", 1)
    for j in range(2):
        add("gq2_%d" % j, 1)
        add("gsub%d" % j, 1)
    base = {}
    off = 0
    for name, n in ent:
        base[name] = off
        off += n
    return ent, base, off


VENT, VBASE, NV = _vec_layout()


def _pack_vecs(inp):
    cols = {}
    for l in range(DEPTH):
        cols["b_ada%d" % l] = inp["b_ada"][l]
        cols["g_mix%d" % l] = inp["g_norm_mix"][l]
        cols["g_ffn%d" % l] = inp["g_norm_ffn"][l]
        for j in range(3):
            cols["w_fc%d_%d" % (l, j)] = inp["w_ffn_conv"][l, j]
        cols["b_fc%d" % l] = inp["b_ffn_conv"][l]
    for i in range(N_A):
        for j in range(4):
            cols["w_lc%d_%d" % (i, j)] = inp["w_lru_conv"][i, j]
        cols["b_lc%d" % i] = inp["b_lru_conv"][i]
        cols["b_gx%d" % i] = inp["b_gate_x"][i]
        cols["b_ga%d" % i] = inp["b_gate_a"][i]
        cols["lrul%d" % i] = inp["lru_log_param"][i]
    cols["b_ada_kv"] = inp["b_ada_kv"]
    cols["g_kv"] = inp["g_norm_kv"]
    cols["gk2"] = np.concatenate([inp["g_k_norm"], inp["g_k_norm"]])
    for j in range(2):
        cols["gq2_%d" % j] = np.concatenate([inp["g_q_norm"][j], inp["g_q_norm"][j]])
        cols["gsub%d" % j] = inp["g_subln"][j]
    out = np.zeros((128, NV), np.float32)
    for name, n in VENT:
        v = np.asarray(cols[name], np.float32).reshape(n, 128)
        out[:, VBASE[name]:VBASE[name] + n] = v.T
    return out


def _consts():
    c = {}
    c["identf"] = np.eye(128, dtype=np.float32)
    k = np.arange(128)[:, None]
    q = np.arange(128)[None, :]
    c["maskT"] = np.where(k <= q, 0.0, NEG).astype(np.float32)
    bo = np.zeros((128, 128), np.float32)
    bo[:64, :64] = 1.0 / 64
    bo[64:, 64:] = 1.0 / 64
    c["bones"] = bo
    ka = np.zeros((4, 16, 128), np.float32)
    for v in range(16):
        ka[0, v] = np.arange(128)
        ka[1, v] = 1.0
        ka[2, v] = 1.0
        ka[3, v] = 128.0 * (v - 12)
    c["kaug"] = ka
    qa = np.zeros((4, H, 512), np.float32)
    ql = np.arange(512)
    qhi = (ql // 128) * 128
    qlo = ql - qhi
    for h in range(H):
        qa[0, h] = SLOPES[h]
        qa[1, h] = -SLOPES[h] * qhi
        qa[2, h] = -SLOPES[h] * qlo
        qa[3, h] = SLOPES[h]
    c["qaug"] = qa
    al = np.zeros((128, NPG, H), np.float32)
    for pg in range(NPG):
        for h in range(H):
            al[:, pg, h] = -SLOPES[h] * (T - (pg * 128 + np.arange(128)))
    c["alis"] = al
    nm = np.full((128, NS), NEG, np.float32)
    for s in range(NS):
        nm[s, s] = 0.0
    c["newmask"] = nm
    c["iotap"] = np.arange(128, dtype=np.float32).reshape(128, 1)
    bm = np.zeros((16, 1024), np.float32)
    for hc in range(16):
        bm[hc, (hc // 2) * 128:(hc // 2 + 1) * 128] = 1.0
    c["blockm"] = bm
    cm = np.zeros((16, 2), np.float32)
    cm[0::2, 0] = 1.0
    cm[1::2, 1] = 1.0
    c["coefm"] = cm
    return c


def build_program(n_pool):
    nc = bass.Bass("TRN2", target_bir_lowering=False)

    def din(name, shape, dt=F32):
        return nc.dram_tensor(name, list(shape), dt, kind="ExternalInput").ap()

    def dout(name, shape):
        return nc.dram_tensor(name, list(shape), F32, kind="ExternalOutput").ap()

    xT_d = din("xT", [128, 8, TA])
    cT_d = din("cT", [128, 8, 17])
    vecs_d = din("vecs", [128, NV])
    lamv_d = din("lamv", [128, 2, 4, 64])
    h0T_d = din("h0T", [128, N_A, 8, NS])
    c0T_d = din("c0T", [128, N_A, 8, 3, NS])
    f0T_d = din("f0T", [128, DEPTH, 48, 2, NS])
    pt_d = din("ptb", [128, NS * NPG], I32)
    ck_d = din("cache_k", [n_pool * 128, 1024])
    cv_d = din("cache_v", [n_pool * 128, 1024])
    w_ada_d = din("w_ada", [DEPTH, D, 6 * D])
    w_lin_d = din("w_lru_in", [N_A, D, 2 * D])
    w_gx_d = din("w_gate_x", [N_A, 8, 128, 128])
    w_ga_d = din("w_gate_a", [N_A, 8, 128, 128])
    w_lout_d = din("w_lru_out", [N_A, D, D])
    w_adakv_d = din("w_ada_kv", [D, 2 * D])
    w_kv_d = din("w_kv", [D, 2 * D])
    w_q_d = din("w_q", [2, D, D])
    w_o_d = din("w_o", [2, D, D])
    w_up_d = din("w_up", [DEPTH, D, 6 * D])
    w_dn_d = din("w_down", [DEPTH, 3 * D, D])
    gsubrow_d = din("gsubrow", [1, 2, 128])
    cst = _consts()
    cst_d = {k: din("c_" + k, v.shape) for k, v in cst.items()}
    kstok_o = dout("kstok", [NS, 1024])

    yT_o = dout("yT", [128, 8, TA])
    kT_o = dout("kT", [128, 8, TA])
    v_o = dout("vtok", [TA, 1024])
    lh_o = dout("lruh", [128, N_A, 8, 17])
    lcp_o = dout("lrucp", [128, N_A, 8, 3])
    lcs_o = dout("lrucs", [128, N_A, 8, 3, NS])
    fcp_o = dout("ffncp", [128, DEPTH, 48, 2])
    fcs_o = dout("ffncs", [128, DEPTH, 48, 2, NS])

    P = Prog(nc)
    st = contextlib.ExitStack()
    with st:
        def sb(name, shape, dt=F32):
            return st.enter_context(nc.sbuf_tensor("sb_" + name, list(shape), dt))

        xres = sb("xres", [128, 8, TA])
        bx = [Buf("x%d" % i) for i in range(5)]
        vecs = sb("vecs", [128, NV])
        bvec = Buf("vecs")
        mods = sb("mods", [128, 48, 17])
        bmods = Buf("mods")
        gsm = sb("gsm", [128, 3, 8, 17])
        bgsm = Buf("gsm")
        modkv = sb("modkv", [128, 16, 17])
        bmodkv = Buf("modkv")
        scT = sb("scT", [128, 8, 17], BF16)
        bscT = Buf("scT")
        identf = sb("identf", [128, 128])
        identb = sb("identb", [128, 128], BF16)
        maskTb = sb("maskTb", [128, 128], BF16)
        bonesb = sb("bonesb", [128, 128], BF16)
        onesd = sb("onesd", [128, 128], BF16)
        onesh = sb("onesh", [128, 128], BF16)
        ones1 = sb("ones1", [128, 128], BF16)
        ones1f = sb("ones1f", [128, 128])
        kaugb = sb("kaugb", [4, 16, 128], BF16)
        gqs = sb("gqs", [128, 2])
        gsubs = sb("gsubs", [128, 2])
        alis = sb("alis", [128, NPG, H])
        newmask = sb("newmask", [128, NS])
        iotap = sb("iotap", [128, 1])
        cL = sb("cL", [128, N_A * 8])
        neglam = sb("neglam", [128, 2])
        epsc = sb("epsc", [128, 1])
        bconst = Buf("const")
        bcL = Buf("cL")
        blam = Buf("lam")
        ARENA_F = 32000
        arena_t = sb("arena", [128, ARENA_F])
        A = Arena(arena_t, ARENA_F)

        banks = [st.enter_context(nc.psum_tensor("bank%d" % i, [128, 512], F32)) for i in range(8)]
        bbank = [Buf("bank%d" % i) for i in range(8)]
        rot = {"list": list(range(8)), "i": 0}

        def bank():
            i = rot["list"][rot["i"] % len(rot["list"])]
            rot["i"] += 1
            return banks[i], bbank[i]

        def V(name, j=0):
            c = VBASE[name] + j
            return vecs[:, c:c + 1]

        P.dma("sp", vecs[:], vecs_d, writes=[bvec])
        P.dma("sp", identf[:], cst_d["identf"], writes=[bconst])
        P.dma("pool", identb[:], cst_d["identf"], writes=[bconst])
        P.dma("pool", maskTb[:], cst_d["maskT"], writes=[bconst])
        P.dma("pool", bonesb[:], cst_d["bones"], writes=[bconst])
        P.dma("pool", kaugb[:], cst_d["kaug"], writes=[bconst])
        P.dma("sp", alis[:], cst_d["alis"], writes=[bconst])
        P.dma("sp", newmask[:], cst_d["newmask"], writes=[bconst])
        P.dma("sp", iotap[:], cst_d["iotap"], writes=[bconst])
        P.op("dve", lambda e: e.memset(onesd[:], 1.0 / 1024), writes=[bconst])
        P.op("dve", lambda e: e.memset(onesh[:], 1.0 / 128), writes=[bconst])
        P.op("dve", lambda e: e.memset(ones1[:], 1.0), writes=[bconst])
        P.op("dve", lambda e: e.memset(ones1f[:], 1.0), writes=[bconst])
        P.op("dve", lambda e: e.memset(epsc[:], EPS), writes=[bconst])
        for ti, (t0, n) in enumerate(TILES):
            P.dma("sp", xres[:, :, t0:t0 + n], xT_d[:, :, t0:t0 + n], writes=[bx[ti]])

        A.reset()
        cTs = A.alloc([8, 17])
        bcTs = Buf("cTs")
        P.dma("sp", cTs, cT_d, writes=[bcTs])
        P.op("act", lambda e: e.activation(out=scT[:], in_=cTs, func=AF.Silu), reads=[bcTs], writes=[bscT])
        for i in range(N_A):
            src = vecs[:, VBASE["lrul%d" % i]:VBASE["lrul%d" % i] + 8]
            dst = cL[:, i * 8:(i + 1) * 8]
            P.op("act", lambda e, src=src, dst=dst: e.activation(out=dst, in_=src, func=AF.Exp, scale=-1.0), reads=[bvec], writes=[bcL])
            P.op("act", lambda e, dst=dst: e.activation(out=dst, in_=dst, func=AF.Ln, bias=ones1f[:, 0:1], scale=1.0), reads=[bcL, bconst], writes=[bcL])
            P.op("act", lambda e, dst=dst: e.activation(out=dst, in_=dst, func=AF.Copy, scale=-8.0), reads=[bcL], writes=[bcL])
        lamv = A.alloc([2, 4, 64])
        blamv = Buf("lamv")
        lamp = A.alloc([2, 2, 64])
        lams = A.alloc([2, 2])
        P.dma("sp", lamv, lamv_d, writes=[blamv])
        for j in range(2):
            for pr in range(2):
                P.op("dve", lambda e, j=j, pr=pr: e.tensor_tensor(out=lamp[:, j, pr, :], in0=lamv[:, j, 2 * pr, :], in1=lamv[:, j, 2 * pr + 1, :], op=ALU.mult), reads=[blamv], writes=[blam])
                P.op("dve", lambda e, j=j, pr=pr: e.tensor_reduce(out=lams[:, j, pr:pr + 1], in_=lamp[:, j, pr, :], axis=AX.X, op=ALU.add), reads=[blam], writes=[blam])
        P.op("act", lambda e: e.activation(out=lams, in_=lams, func=AF.Exp), reads=[blam], writes=[blam])
        for j in range(2):
            lam_init = 0.8 - 0.6 * math.exp(-0.3 * (j + N_A))
            P.op("dve", lambda e, j=j, li=lam_init: e.scalar_tensor_tensor(out=neglam[:, j:j + 1], in0=lams[:, j, 1:2], scalar=-li, in1=lams[:, j, 0:1], op0=ALU.add, op1=ALU.subtract), reads=[blam], writes=[blam])

        def wview(w2d, c0, ncols):
            return w2d.rearrange("(kc p) n -> p kc n", p=128)[:, :, c0:c0 + ncols]

        def ada_mods(wd, bname, nj, dst, bdst):
            A.reset()
            P.barrier()
            NSL = 4
            slots = [A.alloc([8, 512], BF16) for _ in range(NSL)]
            bsl = [Buf("adaw%d" % i) for i in range(NSL)]
            ngrp = (nj * 128) // 512
            for g in range(ngrp):
                s = g % NSL
                P.dma("pool", slots[s], wview(wd, g * 512, 512), writes=[bsl[s]])
                pt, pb = bank()
                for jj in range(4):
                    j = g * 4 + jj
                    for k in range(8):
                        P.op("pe", lambda e, s=s, jj=jj, k=k, pt=pt: e.matmul(pt[:, jj * 32:jj * 32 + 17], lhsT=slots[s][:, k, jj * 128:(jj + 1) * 128], rhs=scT[:, k, :], start=(k == 0), stop=(k == 7)), reads=[bsl[s], bscT], writes=[pb])
                for jj in range(4):
                    j = g * 4 + jj
                    P.op("dve", lambda e, j=j, jj=jj, pt=pt: e.tensor_scalar(out=dst[:, j, :], in0=pt[:, jj * 32:jj * 32 + 17], scalar1=V(bname, j), scalar2=None, op0=ALU.add), reads=[pb, bvec], writes=[bdst])

        def make_gs(slot, gname, sc_off, src, bsrc):
            for c in range(8):
                P.op("dve", lambda e, c=c: e.tensor_scalar(out=gsm[:, slot, c, :], in0=src[:, sc_off + c, :], scalar1=1.0, scalar2=V(gname, c), op0=ALU.add, op1=ALU.mult), reads=[bsrc, bvec], writes=[bgsm])

        def mod_norm(hT, bh, slot, shift_src, bshift, shift_off, scr, tiles=None, loc=False):
            sq = scr[:, 0:2048].bitcast(BF16).rearrange("p (a b) -> p a b", a=8)
            bsq = Buf("sq")
            rstd = scr[:, 2048:2560]
            brstd = Buf("rstd")
            tmps = [scr[:, 2560:3072], scr[:, 3072:3584]]
            btmps = [Buf("tmp0"), Buf("tmp1")]
            tcnt = 0
            for ti, (t0, n) in enumerate(TILES):
                if tiles is not None and ti not in tiles:
                    continue
                h0c = 0 if loc else t0
                bhh = bh[0] if loc else bh[ti]
                P.op("act", lambda e, t0=t0, n=n: e.activation(out=sq[:, :, 0:n], in_=xres[:, :, t0:t0 + n], func=AF.Square), reads=[bx[ti]], writes=[bsq])
                pt, pb = bank()
                for c in range(8):
                    P.op("pe", lambda e, c=c, n=n, pt=pt: e.matmul(pt[:, 0:n], lhsT=onesd[:], rhs=sq[:, c, 0:n], start=(c == 0), stop=(c == 7)), reads=[bsq, bconst], writes=[pb])
                P.op("act", lambda e, n=n, pt=pt: e.activation(out=rstd[:, 0:n], in_=pt[:, 0:n], func=AF.Sqrt, bias=epsc[:, 0:1], scale=1.0), reads=[pb, bconst], writes=[brstd])
                P.op("dve", lambda e, n=n: e.reciprocal(out=rstd[:, 0:n], in_=rstd[:, 0:n]), reads=[brstd], writes=[brstd])
                for c in range(8):
                    tmp = tmps[tcnt % 2]
                    btmp = btmps[tcnt % 2]
                    tcnt += 1
                    P.op("dve", lambda e, c=c, t0=t0, n=n: e.tensor_tensor(out=tmp[:, 0:n], in0=xres[:, c, t0:t0 + n], in1=rstd[:, 0:n], op=ALU.mult), reads=[bx[ti], brstd], writes=[btmp])
                    if ti < 4:
                        P.op("act", lambda e, c=c, t0=t0, n=n: e.activation(out=hT[:, c, h0c:h0c + n], in_=tmp[:, 0:n], func=AF.Identity, scale=gsm[:, slot, c, 0:1], bias=shift_src[:, shift_off + c, 0:1]), reads=[btmp, bgsm, bshift], writes=[bhh])
                    else:
                        P.op("dve", lambda e, c=c, n=n: e.tensor_tensor(out=tmp[:, 0:n], in0=tmp[:, 0:n], in1=gsm[:, slot, c, 1:17], op=ALU.mult), reads=[btmp, bgsm], writes=[btmp])
                        P.op("dve", lambda e, c=c, t0=t0, n=n: e.tensor_tensor(out=hT[:, c, h0c:h0c + n], in0=tmp[:, 0:n], in1=shift_src[:, shift_off + c, 1:17], op=ALU.add), reads=[btmp, bshift], writes=[bhh])

        tmpg_holder = {}

        def new_tmpg():
            tmpg_holder["ap"] = A.alloc([16])
            tmpg_holder["buf"] = Buf("tmpg")

        def resid_update(ti, t0, n, oc, pt, pb, gate_off):
            if ti < 4:
                P.op("dve", lambda e: e.scalar_tensor_tensor(out=xres[:, oc, t0:t0 + n], in0=pt[:, 0:n], scalar=mods[:, gate_off + oc, 0:1], in1=xres[:, oc, t0:t0 + n], op0=ALU.mult, op1=ALU.add), reads=[pb, bmods, bx[ti]], writes=[bx[ti]])
            else:
                tmpg = tmpg_holder["ap"]
                bt = tmpg_holder["buf"]
                P.op("dve", lambda e: e.tensor_tensor(out=tmpg, in0=pt[:, 0:n], in1=mods[:, gate_off + oc, 1:17], op=ALU.mult), reads=[pb, bmods], writes=[bt])
                P.op("dve", lambda e: e.tensor_tensor(out=xres[:, oc, t0:t0 + n], in0=xres[:, oc, t0:t0 + n], in1=tmpg, op=ALU.add), reads=[bt, bx[ti]], writes=[bx[ti]])

        def lru_layer(i):
            A.reset()
            P.barrier()
            hT = A.alloc([8, TA], BF16)
            bh = [Buf("h%d" % t) for t in range(5)]
            yb = A.alloc([8, TA], BF16)
            by = [Buf("y%d" % t) for t in range(5)]
            wg = A.alloc([2, 8, 128], BF16)
            bwg = Buf("wg")
            P.dma("pool", wg[:, 0], w_gx_d[i].rearrange("n k j -> k n j"), writes=[bwg])
            P.dma("pool", wg[:, 1], w_ga_d[i].rearrange("n k j -> k n j"), writes=[bwg])
            wsl = [A.alloc([8, 256], BF16) for _ in range(2)]
            bws = [Buf("wlin%d" % s) for s in range(2)]
            h0 = A.alloc([8, NS])
            c0 = A.alloc([8, 3, NS])
            bst = Buf("lrustate")
            P.dma("sp", h0, h0T_d[:, i], writes=[bst])
            P.dma("sp", c0, c0T_d[:, i], writes=[bst])
            xbuf = A.alloc([T + 3])
            xs = A.alloc([NS])
            xc = A.alloc([TA])
            xcb = A.alloc([512], BF16)
            ta = A.alloc([TA])
            off_ta = A.last_off
            tom = A.alloc([TA])
            tgx = A.alloc([TA])
            hout = A.alloc([8, 17])
            new_tmpg()
            mod_norm(hT, bh, 0, mods, bmods, 0, arena_t[:, off_ta:off_ta + 3584])
            P.barrier()
            bxb, bxc, bxcb, bta, btom, btgx, bho = (Buf("xb"), Buf("xc"), Buf("xcb"), Buf("ta"), Buf("tom"), Buf("tgx"), Buf("hout"))
            P.op("dve", lambda e: e.memset(xbuf[:, 0:3], 0.0), writes=[bxb])
            for nch in range(8):
                s = nch % 2
                P.dma("pool", wsl[s][:, :, 0:128], wview(w_lin_d[i], nch * 128, 128), writes=[bws[s]])
                P.dma("pool", wsl[s][:, :, 128:256], wview(w_lin_d[i], D + nch * 128, 128), writes=[bws[s]])
                for ti, (t0, n) in enumerate(TILES):
                    pt, pb = bank()
                    for k in range(8):
                        P.op("pe", lambda e, k=k, t0=t0, n=n, pt=pt, s=s: e.matmul(pt[:, 0:n], lhsT=wsl[s][:, k, 0:128], rhs=hT[:, k, t0:t0 + n], start=(k == 0), stop=(k == 7)), reads=[bws[s], bh[ti]], writes=[pb])
                    if ti < 4:
                        P.op("act", lambda e, t0=t0, n=n, pt=pt: e.activation(out=xbuf[:, 3 + t0:3 + t0 + n], in_=pt[:, 0:n], func=AF.Copy), reads=[pb], writes=[bxb])
                    else:
                        P.op("act", lambda e, n=n, pt=pt: e.activation(out=xs, in_=pt[:, 0:n], func=AF.Copy), reads=[pb], writes=[bxb])
                P.dma("sp", lcp_o[:, i, nch, :], xbuf[:, T:T + 3], reads=[bxb])
                P.dma("sp", lcs_o[:, i, nch, 0:2, :], c0[:, nch, 1:3, :], reads=[bst])
                P.dma("sp", lcs_o[:, i, nch, 2, :], xs, reads=[bxb])
                wl = lambda j: V("w_lc%d_%d" % (i, j), nch)
                P.op("act", lambda e: e.activation(out=xc[:, 0:T], in_=xbuf[:, 0:T], func=AF.Identity, scale=wl(0), bias=V("b_lc%d" % i, nch)), reads=[bxb, bvec], writes=[bxc])
                for j in range(1, 4):
                    P.op("dve", lambda e, j=j: e.scalar_tensor_tensor(out=xc[:, 0:T], in0=xbuf[:, j:j + T], scalar=wl(j), in1=xc[:, 0:T], op0=ALU.mult, op1=ALU.add), reads=[bxb, bxc, bvec], writes=[bxc])
                P.op("act", lambda e: e.activation(out=xc[:, T:TA], in_=c0[:, nch, 0, :], func=AF.Identity, scale=wl(0), bias=V("b_lc%d" % i, nch)), reads=[bst, bvec], writes=[bxc])
                for j in range(1, 3):
                    P.op("dve", lambda e, j=j: e.scalar_tensor_tensor(out=xc[:, T:TA], in0=c0[:, nch, j, :], scalar=wl(j), in1=xc[:, T:TA], op0=ALU.mult, op1=ALU.add), reads=[bst, bxc, bvec], writes=[bxc])
                P.op("dve", lambda e: e.scalar_tensor_tensor(out=xc[:, T:TA], in0=xs, scalar=wl(3), in1=xc[:, T:TA], op0=ALU.mult, op1=ALU.add), reads=[bxb, bxc, bvec], writes=[bxc])
                for ti, (t0, n) in enumerate(TILES):
                    P.op("act", lambda e, t0=t0, n=n: e.activation(out=xcb[:, 0:n], in_=xc[:, t0:t0 + n], func=AF.Copy), reads=[bxc], writes=[bxcb])
                    pgx, pbx = bank()
                    P.op("pe", lambda e, n=n, pgx=pgx: e.matmul(pgx[:, 0:n], lhsT=wg[:, 0, nch, :], rhs=xcb[:, 0:n], start=True, stop=True), reads=[bwg, bxcb], writes=[pbx])
                    pga, pba = bank()
                    P.op("pe", lambda e, n=n, pga=pga: e.matmul(pga[:, 0:n], lhsT=wg[:, 1, nch, :], rhs=xcb[:, 0:n], start=True, stop=True), reads=[bwg, bxcb], writes=[pba])
                    P.op("act", lambda e, t0=t0, n=n, pgx=pgx: e.activation(out=tgx[:, t0:t0 + n], in_=pgx[:, 0:n], func=AF.Sigmoid, bias=V("b_gx%d" % i, nch)), reads=[pbx, bvec], writes=[btgx])
                    P.op("act", lambda e, t0=t0, n=n, pga=pga: e.activation(out=ta[:, t0:t0 + n], in_=pga[:, 0:n], func=AF.Sigmoid, bias=V("b_ga%d" % i, nch)), reads=[pba, bvec], writes=[bta])
                P.op("act", lambda e: e.activation(out=ta, in_=ta, func=AF.Exp, scale=cL[:, i * 8 + nch:i * 8 + nch + 1]), reads=[bta, bcL], writes=[bta])
                P.op("pool", lambda e: e.tensor_tensor(out=tom, in0=ta, in1=ta, op=ALU.mult), reads=[bta], writes=[btom])
                P.op("act", lambda e: e.activation(out=tom, in_=tom, func=AF.Sqrt, scale=-1.0, bias=ones1f[:, 0:1]), reads=[btom, bconst], writes=[btom])
                P.op("dve", lambda e: e.tensor_tensor(out=tgx, in0=tgx, in1=tom, op=ALU.mult), reads=[btgx, btom], writes=[btgx])
                P.op("dve", lambda e: e.tensor_tensor(out=tgx, in0=tgx, in1=xc, op=ALU.mult), reads=[btgx, bxc], writes=[btgx])
                P.op("dve", lambda e: e.tensor_tensor_scan(out=tom[:, 0:T], data0=ta[:, 0:T], data1=tgx[:, 0:T], initial=0.0, op0=ALU.mult, op1=ALU.add), reads=[bta, btgx, btom], writes=[btom])
                P.op("dve", lambda e: e.tensor_tensor(out=tom[:, T:TA], in0=ta[:, T:TA], in1=h0[:, nch, :], op=ALU.mult), reads=[bta, bst, btom], writes=[btom])
                P.op("dve", lambda e: e.tensor_tensor(out=tom[:, T:TA], in0=tom[:, T:TA], in1=tgx[:, T:TA], op=ALU.add), reads=[btgx, btom], writes=[btom])
                P.op("act", lambda e: e.activation(out=hout[:, nch, :], in_=tom[:, T - 1:TA], func=AF.Copy), reads=[btom], writes=[bho])
                for ti, (t0, n) in enumerate(TILES):
                    pt, pb = bank()
                    for k in range(8):
                        P.op("pe", lambda e, k=k, t0=t0, n=n, pt=pt, s=s: e.matmul(pt[:, 0:n], lhsT=wsl[s][:, k, 128:256], rhs=hT[:, k, t0:t0 + n], start=(k == 0), stop=(k == 7)), reads=[bws[s], bh[ti]], writes=[pb])
                    P.op("act", lambda e, t0=t0, n=n, pt=pt: e.activation(out=xc[:, t0:t0 + n], in_=pt[:, 0:n], func=AF.Gelu_apprx_tanh), reads=[pb, bxc], writes=[bxc])
                    P.op("dve", lambda e, t0=t0, n=n: e.tensor_tensor(out=yb[:, nch, t0:t0 + n], in0=xc[:, t0:t0 + n], in1=tom[:, t0:t0 + n], op=ALU.mult), reads=[bxc, btom], writes=[by[ti]])
            P.dma("sp", lh_o[:, i], hout, reads=[bho])
            wo = [A.alloc([8, 128], BF16) for _ in range(2)]
            bwo = [Buf("wlo%d" % s) for s in range(2)]
            for oc in range(8):
                s = oc % 2
                P.dma("pool", wo[s], wview(w_lout_d[i], oc * 128, 128), writes=[bwo[s]])
                for ti, (t0, n) in enumerate(TILES):
                    pt, pb = bank()
                    for k in range(8):
                        P.op("pe", lambda e, k=k, t0=t0, n=n, pt=pt, s=s: e.matmul(pt[:, 0:n], lhsT=wo[s][:, k, :], rhs=yb[:, k, t0:t0 + n], start=(k == 0), stop=(k == 7)), reads=[bwo[s], by[ti]], writes=[pb])
                    resid_update(ti, t0, n, oc, pt, pb, 16)

        def ffn_layer(l):
            A.reset()
            P.barrier()
            hT = A.alloc([8, TA], BF16)
            bh = [Buf("hf%d" % t) for t in range(5)]
            NG = 8
            GC = 3
            actb = [A.alloc([GC, TA], BF16) for _ in range(2)]
            bact = [[Buf("act%d_%d" % (s, t)) for t in range(5)] for s in range(2)]
            wup = [A.alloc([8, 256], BF16) for _ in range(3)]
            bwup = [Buf("wup%d" % s) for s in range(3)]
            wcnt = 0
            wdn = [A.alloc([GC, 1024], BF16) for _ in range(2)]
            bwdn = [Buf("wdn%d" % s) for s in range(2)]
            f0 = A.alloc([48, 2, NS])
            bf0 = Buf("f0")
            P.dma("sp", f0, f0T_d[:, l], writes=[bf0])
            P.dma("sp", fcs_o[:, l, :, 0, :], f0[:, :, 1, :], reads=[bf0])
            ub = [A.alloc([T + 2]) for _ in range(2)]
            us = [A.alloc([NS]) for _ in range(2)]
            bub = [Buf("ub%d" % s) for s in range(2)]
            uc = [A.alloc([TA]) for _ in range(2)]
            off_uc = A.last_off - TA
            assert off_uc % 8 == 0
            buc = [Buf("uc%d" % s) for s in range(2)]
            new_tmpg()
            mod_norm(hT, bh, 1, mods, bmods, 24, arena_t[:, off_uc:off_uc + 3584])
            P.barrier()
            fout = A.alloc([48, 2])
            fouts = A.alloc([48, NS])
            bfo = Buf("fout")
            for s in range(2):
                P.op("dve", lambda e, s=s: e.memset(ub[s][:, 0:2], 0.0), writes=[bub[s]])
            for g in range(NG):
                s = g % 2
                P.dma("pool", wdn[s], w_dn_d[l].rearrange("(kc p) n -> p kc n", p=128)[:, g * GC:(g + 1) * GC, :], writes=[bwdn[s]])
                for jj in range(GC):
                    ws = wcnt % 3
                    wcnt += 1
                    P.dma("pool", wup[ws][:, :, 0:128], wview(w_up_d[l], (g * GC + jj) * 128, 128), writes=[bwup[ws]])
                    P.dma("pool", wup[ws][:, :, 128:256], wview(w_up_d[l], 3 * D + (g * GC + jj) * 128, 128), writes=[bwup[ws]])
                    for part in range(2):
                        ch = part * 24 + g * GC + jj
                        wc0 = part * 128
                        for ti, (t0, n) in enumerate(TILES):
                            pt, pb = bank()
                            for k in range(8):
                                P.op("pe", lambda e, k=k, t0=t0, n=n, pt=pt, s=s, wc0=wc0: e.matmul(pt[:, 0:n], lhsT=wup[ws][:, k, wc0:wc0 + 128], rhs=hT[:, k, t0:t0 + n], start=(k == 0), stop=(k == 7)), reads=[bwup[ws], bh[ti]], writes=[pb])
                            if ti < 4:
                                P.op("act", lambda e, t0=t0, n=n, pt=pt, part=part: e.activation(out=ub[part][:, 2 + t0:2 + t0 + n], in_=pt[:, 0:n], func=AF.Copy), reads=[pb], writes=[bub[part]])
                            else:
                                P.op("act", lambda e, n=n, pt=pt, part=part: e.activation(out=us[part], in_=pt[:, 0:n], func=AF.Copy), reads=[pb], writes=[bub[part]])
                        P.op("act", lambda e, ch=ch, part=part: e.activation(out=fout[:, ch, :], in_=ub[part][:, T:T + 2], func=AF.Copy), reads=[bub[part]], writes=[bfo])
                        P.op("act", lambda e, ch=ch, part=part: e.activation(out=fouts[:, ch, :], in_=us[part], func=AF.Copy), reads=[bub[part]], writes=[bfo])
                        wf = lambda j, ch=ch: V("w_fc%d_%d" % (l, j), ch)
                        bfc = V("b_fc%d" % l, ch)
                        P.op("act", lambda e, part=part, wf=wf, bfc=bfc: e.activation(out=uc[part][:, 0:T], in_=ub[part][:, 0:T], func=AF.Identity, scale=wf(0), bias=bfc), reads=[bub[part], bvec], writes=[buc[part]])
                        for j in range(1, 3):
                            P.op("dve", lambda e, j=j, part=part, wf=wf: e.scalar_tensor_tensor(out=uc[part][:, 0:T], in0=ub[part][:, j:j + T], scalar=wf(j), in1=uc[part][:, 0:T], op0=ALU.mult, op1=ALU.add), reads=[bub[part], buc[part], bvec], writes=[buc[part]])
                        P.op("act", lambda e, part=part, wf=wf, bfc=bfc, ch=ch: e.activation(out=uc[part][:, T:TA], in_=f0[:, ch, 0, :], func=AF.Identity, scale=wf(0), bias=bfc), reads=[bf0, bvec], writes=[buc[part]])
                        P.op("dve", lambda e, part=part, wf=wf, ch=ch: e.scalar_tensor_tensor(out=uc[part][:, T:TA], in0=f0[:, ch, 1, :], scalar=wf(1), in1=uc[part][:, T:TA], op0=ALU.mult, op1=ALU.add), reads=[bf0, buc[part], bvec], writes=[buc[part]])
                        P.op("dve", lambda e, part=part, wf=wf: e.scalar_tensor_tensor(out=uc[part][:, T:TA], in0=us[part], scalar=wf(2), in1=uc[part][:, T:TA], op0=ALU.mult, op1=ALU.add), reads=[bub[part], buc[part], bvec], writes=[buc[part]])
                    P.op("act", lambda e: e.activation(out=uc[0], in_=uc[0], func=AF.Gelu_apprx_tanh), reads=[buc[0]], writes=[buc[0]])
                    for ti, (t0, n) in enumerate(TILES):
                        P.op("pool", lambda e, t0=t0, n=n, jj=jj, s=s: e.tensor_tensor(out=actb[s][:, jj, t0:t0 + n], in0=uc[0][:, t0:t0 + n], in1=uc[1][:, t0:t0 + n], op=ALU.mult), reads=[buc[0], buc[1]], writes=[bact[s][ti]])
                for oc in range(8):
                    for ti, (t0, n) in enumerate(TILES):
                        pt, pb = bank()
                        for k in range(GC):
                            P.op("pe", lambda e, k=k, t0=t0, n=n, pt=pt, s=s, oc=oc: e.matmul(pt[:, 0:n], lhsT=wdn[s][:, k, oc * 128:(oc + 1) * 128], rhs=actb[s][:, k, t0:t0 + n], start=(k == 0), stop=(k == GC - 1)), reads=[bwdn[s], bact[s][ti]], writes=[pb])
                        resid_update(ti, t0, n, oc, pt, pb, 40)
            P.dma("sp", fcp_o[:, l], fout, reads=[bfo])
            P.dma("sp", fcs_o[:, l, :, 1, :], fouts, reads=[bfo])


        bkTd = Buf("kT_dram")
        bvd = Buf("v_dram")

        def group_rstd(pt, pb, n, sqb, bsqb, rr, brr, ones_ap, p2b=None):
            P.op("act", lambda e: e.activation(out=sqb[:, 0:n], in_=pt[:, 0:n], func=AF.Square), reads=[pb], writes=[bsqb])
            p2, pb2 = p2b if p2b is not None else bank()
            P.op("pe", lambda e: e.matmul(p2[:, 0:n], lhsT=ones_ap, rhs=sqb[:, 0:n], start=True, stop=True), reads=[bsqb, bconst], writes=[pb2])
            P.op("act", lambda e: e.activation(out=rr[:, 0:n], in_=p2[:, 0:n], func=AF.Sqrt, bias=epsc[:, 0:1], scale=1.0), reads=[pb2, bconst], writes=[brr])
            P.op("dve", lambda e: e.reciprocal(out=rr[:, 0:n], in_=rr[:, 0:n]), reads=[brr], writes=[brr])

        def kv_phase():
            ada_mods(w_adakv_d, "b_ada_kv", 16, modkv, bmodkv)
            make_gs(2, "g_kv", 8, modkv, bmodkv)
            A.reset()
            P.barrier()
            hT = A.alloc([8, TA], BF16)
            bh = [Buf("hkv%d" % t) for t in range(5)]
            scr = A.alloc([3584])
            mod_norm(hT, bh, 2, modkv, bmodkv, 0, scr)
            wk = [A.alloc([8, 128], BF16) for _ in range(2)]
            bwk = [Buf("wk%d" % s) for s in range(2)]
            wv = A.alloc([8, 1024], BF16)
            bwv = Buf("wv")
            P.dma("pool", wv, wview(w_kv_d, D, 1024), writes=[bwv])
            sqb = A.alloc([512], BF16)
            bsqb = Buf("sqb")
            rr = A.alloc([512])
            brr = Buf("rr")
            kst = [A.alloc([512]) for _ in range(2)]
            bkst = [Buf("kst%d" % s) for s in range(2)]
            vst = [A.alloc([1024]) for _ in range(2)]
            bvst = [Buf("vst%d" % s) for s in range(2)]
            kstk = A.alloc([1024])
            bkstk = Buf("kstk")
            cnt = 0
            for hc in range(8):
                s = hc % 2
                P.dma("pool", wk[s], wview(w_kv_d, hc * 128, 128), writes=[bwk[s]])
                for ti, (t0, n) in enumerate(TILES):
                    pt, pb = bank()
                    for k in range(8):
                        P.op("pe", lambda e: e.matmul(pt[:, 0:n], lhsT=wk[s][:, k, :], rhs=hT[:, k, t0:t0 + n], start=(k == 0), stop=(k == 7)), reads=[bwk[s], bh[ti]], writes=[pb])
                    group_rstd(pt, pb, n, sqb, bsqb, rr, brr, bonesb[:])
                    ks = cnt % 2
                    cnt += 1
                    P.op("dve", lambda e: e.scalar_tensor_tensor(out=kst[ks][:, 0:n], in0=pt[:, 0:n], scalar=V("gk2"), in1=rr[:, 0:n], op0=ALU.mult, op1=ALU.mult), reads=[pb, brr, bvec], writes=[bkst[ks]])
                    P.dma("sp", kT_o[:, hc, t0:t0 + n], kst[ks][:, 0:n], reads=[bkst[ks]], writes=[bkTd])
                    if ti == 4:
                        ptt, pbt = bank()
                        P.op("pe", lambda e: e.transpose(out=ptt[0:NS, 0:128], in_=kst[ks][:, 0:NS], identity=identf[:]), reads=[bkst[ks], bconst], writes=[pbt])
                        P.op("act", lambda e: e.activation(out=kstk[0:NS, hc * 128:(hc + 1) * 128], in_=ptt[0:NS, 0:128], func=AF.Copy), reads=[pbt], writes=[bkstk])
            P.dma("sp", kstok_o, kstk[0:NS, :], reads=[bkstk], writes=[bkTd])
            vtiles = [(t * 128, 128) for t in range(16)] + [(T, NS)]
            for vi, (t0, m) in enumerate(vtiles):
                ti = min(t0 // 512, 4)
                vs = vi % 2
                for half in range(2):
                    pt, pb = bank()
                    for k in range(8):
                        P.op("pe", lambda e: e.matmul(pt[0:m, :], lhsT=hT[:, k, t0:t0 + m], rhs=wv[:, k, half * 512:(half + 1) * 512], start=(k == 0), stop=(k == 7)), reads=[bwv, bh[ti]], writes=[pb])
                    P.op("act", lambda e: e.activation(out=vst[vs][0:m, half * 512:(half + 1) * 512], in_=pt[0:m, :], func=AF.Copy), reads=[pb], writes=[bvst[vs]])
                P.dma("sp", v_o[t0:t0 + m, :], vst[vs][0:m, :], reads=[bvst[vs]], writes=[bvd])

        def attn_layer(j):
            A.reset()
            P.barrier()
            KT = A.alloc([8, TA], BF16)
            bKT = Buf("KT")
            P.dma("pool", KT, kT_o, reads=[bkTd], writes=[bKT])
            Vb = A.alloc([16, 1024], BF16)
            bVb = Buf("Vb")
            P.dma("pool", Vb, v_o[0:T, :].rearrange("(kt p) f -> p kt f", p=128), reads=[bvd], writes=[bVb])
            qaugb = A.alloc([H, 512], BF16, parts=4)
            bqa = Buf("qaug")
            P.dma("pool", qaugb, cst_d["qaug"], writes=[bqa])
            hT = A.alloc([8, 512], BF16)
            bh = [Buf("hat")]
            scr = A.alloc([3584])
            to = [A.alloc([512]) for _ in range(2)]
            bto = [Buf("to%d" % s) for s in range(2)]
            sqb2 = A.alloc([512], BF16)
            bsqb2 = Buf("sqb2")
            rr2 = A.alloc([512])
            brr2 = Buf("rr2")
            wq = [A.alloc([8, 128], BF16) for _ in range(2)]
            bwq = [Buf("wq%d" % s) for s in range(2)]
            wo = wq
            bwo = bwq
            qz = [[A.alloc([512], BF16) for _ in range(2)] for _ in range(2)]
            bqz = [Buf("qz0"), Buf("qz1")]
            pbuf = [A.alloc([512], BF16) for _ in range(4)]
            bpb = [Buf("pbuf%d" % s) for s in range(4)]
            oT = A.alloc([8, 512], BF16)
            boT = Buf("oT")
            sqb = A.alloc([512], BF16)
            bsqb = Buf("sqb")
            rr = scr[:, 0:512]
            brr = Buf("rr")
            rc = [scr[:, 512:1024], scr[:, 1024:1536]]
            brc = [Buf("rc%d" % s) for s in range(2)]
            tc = [scr[:, 1536:2048], scr[:, 2048:2560]]
            btc = [Buf("tc%d" % s) for s in range(2)]
            od = scr[:, 2560:3072]
            bod = Buf("od")
            new_tmpg()
            for sl in range(2):
                P.op("dve", lambda e: e.memset(qz[sl][0][64:128, :], 0.0), writes=[bqz[sl]])
                P.op("dve", lambda e: e.memset(qz[sl][1][0:64, :], 0.0), writes=[bqz[sl]])
            lam_init = 0.8 - 0.6 * math.exp(-0.3 * (j + N_A))
            P.op("dve", lambda e: e.tensor_scalar(out=gqs[:, j:j + 1], in0=V("gq2_%d" % j), scalar1=0.125, scalar2=None, op0=ALU.mult), reads=[bvec], writes=[bconst])
            P.op("dve", lambda e: e.tensor_scalar(out=gsubs[:, j:j + 1], in0=V("gsub%d" % j), scalar1=1.0 - lam_init, scalar2=None, op0=ALU.mult), reads=[bvec], writes=[bconst])
            rot["list"] = [0, 1, 2, 3]
            accO = [(banks[4], bbank[4]), (banks[5], bbank[5])]
            accD = [(banks[6], bbank[6]), (banks[7], bbank[7])]
            cnts = {"p": 0, "w": 0}

            def qproj(h, sl):
                s = cnts["w"] % 2
                cnts["w"] += 1
                P.dma("pool", wq[s], wview(w_q_d[j], h * 128, 128), writes=[bwq[s]])
                pq, pbq = banks[1], bbank[1]
                for k in range(8):
                    P.op("pe", lambda e: e.matmul(pq[:, :], lhsT=wq[s][:, k, :], rhs=hT[:, k, :], start=(k == 0), stop=(k == 7)), reads=[bwq[s], bh[0]], writes=[pbq])
                group_rstd(pq, pbq, 512, sqb, bsqb, rr, brr, bonesb[:], p2b=(banks[2], bbank[2]))
                P.op("dve", lambda e: e.scalar_tensor_tensor(out=qz[sl][0][0:64, :], in0=pq[0:64, :], scalar=gqs[0:64, j:j + 1], in1=rr[0:64, :], op0=ALU.mult, op1=ALU.mult), reads=[pbq, brr, bconst], writes=[bqz[sl]])
                P.op("dve", lambda e: e.scalar_tensor_tensor(out=qz[sl][1][64:128, :], in0=pq[64:128, :], scalar=gqs[64:128, j:j + 1], in1=rr[64:128, :], op0=ALU.mult, op1=ALU.mult), reads=[pbq, brr, bconst], writes=[bqz[sl]])

            for qb in range(4):
                P.barrier()
                mod_norm(hT, bh, 0, mods, bmods, 0, scr, tiles=[qb], loc=True)
                P.barrier()
                qproj(0, 0)
                for h in range(H):
                    sl = h % 2
                    nkt = 4 * qb + 4
                    Sb = {}

                    def QK(kt):
                        jd = kt - 4 * qb
                        clo = 128 * jd if jd > 0 else 0
                        n = 512 - clo
                        var = jd + 12
                        for c in range(2):
                            bi = 2 * (kt % 2) + c
                            ps, pbs = banks[bi], bbank[bi]
                            Sb[(kt, c)] = (ps, pbs)
                            P.op("pe", lambda e: e.matmul(ps[:, 0:n], lhsT=KT[:, h, kt * 128:(kt + 1) * 128], rhs=qz[sl][c][:, clo:512], start=True, stop=False, skip_group_check=True), reads=[bKT, bqz[sl]], writes=[pbs])
                            P.op("pe", lambda e: e.matmul(ps[:, 0:n], lhsT=kaugb[0:4, var, :], rhs=qaugb[0:4, h, clo:512], start=False, stop=(jd < 0), skip_group_check=True), reads=[bconst, bqa], writes=[pbs])
                            if jd >= 0:
                                P.op("pe", lambda e: e.matmul(ps[:, 0:128], lhsT=identb[:], rhs=maskTb[:], start=False, stop=True, skip_group_check=True), reads=[bconst], writes=[pbs])

                    def PVD(kt):
                        jd = kt - 4 * qb
                        clo = 128 * jd if jd > 0 else 0
                        n = 512 - clo
                        for c in range(2):
                            ps, pbs = Sb[(kt, c)]
                            pi = cnts["p"] % 4
                            cnts["p"] += 1
                            P.op("act", lambda e: e.activation(out=pbuf[pi][:, 0:n], in_=ps[:, 0:n], func=AF.Exp), reads=[pbs], writes=[bpb[pi]])
                            P.op("pe", lambda e: e.matmul(accO[c][0][:, clo:512], lhsT=Vb[:, kt, h * 128:(h + 1) * 128], rhs=pbuf[pi][:, 0:n], start=(kt == 0), stop=(kt == nkt - 1), skip_group_check=True), reads=[bVb, bpb[pi]], writes=[accO[c][1]])
                            P.op("pe", lambda e: e.matmul(accD[c][0][:, clo:512], lhsT=ones1[:], rhs=pbuf[pi][:, 0:n], start=(kt == 0), stop=(kt == nkt - 1), skip_group_check=True), reads=[bconst, bpb[pi]], writes=[accD[c][1]])

                    QK(0)
                    for kt in range(nkt):
                        if kt + 1 < nkt:
                            QK(kt + 1)
                        PVD(kt)
                    if h + 1 < H:
                        qproj(h + 1, 1 - sl)
                    for c in range(2):
                        P.op("dve", lambda e: e.reciprocal(out=rc[c], in_=accD[c][0][:, :]), reads=[accD[c][1]], writes=[brc[c]])
                        P.op("act", lambda e: e.activation(out=to[c], in_=accO[c][0][:, :], func=AF.Copy), reads=[accO[c][1]], writes=[bto[c]])
                    for c in range(2):
                        P.op("dve", lambda e: e.tensor_tensor(out=tc[c], in0=to[c], in1=rc[c], op=ALU.mult), reads=[bto[c], brc[c]], writes=[btc[c]])
                    P.op("dve", lambda e: e.scalar_tensor_tensor(out=od, in0=tc[1], scalar=neglam[:, j:j + 1], in1=tc[0], op0=ALU.mult, op1=ALU.add), reads=[btc[0], btc[1], blam], writes=[bod])
                    P.op("act", lambda e: e.activation(out=sqb2, in_=od, func=AF.Square), reads=[bod], writes=[bsqb2])
                    p2, pb2 = banks[3], bbank[3]
                    P.op("pe", lambda e: e.matmul(p2[:, :], lhsT=onesh[:], rhs=sqb2, start=True, stop=True), reads=[bsqb2, bconst], writes=[pb2])
                    P.op("act", lambda e: e.activation(out=rr2, in_=p2[:, :], func=AF.Sqrt, bias=epsc[:, 0:1], scale=1.0), reads=[pb2, bconst], writes=[brr2])
                    P.op("dve", lambda e: e.reciprocal(out=rr2, in_=rr2), reads=[brr2], writes=[brr2])
                    P.op("dve", lambda e: e.scalar_tensor_tensor(out=oT[:, h, :], in0=od, scalar=gsubs[:, j:j + 1], in1=rr2, op0=ALU.mult, op1=ALU.mult), reads=[bod, brr2, bconst], writes=[boT])
                t0 = 512 * qb
                for oc in range(8):
                    s = cnts["w"] % 2
                    cnts["w"] += 1
                    P.dma("pool", wo[s], wview(w_o_d[j], oc * 128, 128), writes=[bwo[s]])
                    pt, pb = bank()
                    for k in range(8):
                        P.op("pe", lambda e: e.matmul(pt[:, :], lhsT=wo[s][:, k, :], rhs=oT[:, k, :], start=(k == 0), stop=(k == 7)), reads=[bwo[s], boT], writes=[pb])
                    resid_update(qb, t0, 512, oc, pt, pb, 16)
            rot["list"] = list(range(8))

            A.reset()
            P.barrier()
            hTs = A.alloc([8, NS], BF16)
            bhs = [Buf("hs")]
            scr = A.alloc([3584])
            wq = [A.alloc([8, 128], BF16) for _ in range(2)]
            bwq = [Buf("wqs%d" % s) for s in range(2)]
            wo = [A.alloc([8, 128], BF16) for _ in range(2)]
            bwo = [Buf("wos%d" % s) for s in range(2)]
            qs_all = A.alloc([8, NS])
            bqs = Buf("qs_all")
            sqs = A.alloc([NS], BF16)
            bsqs = Buf("sqs")
            rrs = A.alloc([NS])
            brrs = Buf("rrs")
            Rm = A.alloc([8, 128], BF16)
            bRm = Buf("Rm")
            qbc = A.alloc([1024], BF16)
            bqbc = Buf("qbc")
            NPF = 4
            kpg = [A.alloc([1024]) for _ in range(NPF)]
            bkpg = [Buf("kpg%d" % s) for s in range(NPF)]
            vpg = [A.alloc([1024]) for _ in range(NPF)]
            bvpg = [Buf("vpg%d" % s) for s in range(NPF)]
            vpb = [A.alloc([1024], BF16) for _ in range(NPF)]
            bvpb = [Buf("vpb%d" % s) for s in range(NPF)]
            prod = A.alloc([1024])
            bprod = Buf("prod")
            idxf = A.alloc([NS * NPG])
            idx = A.alloc([NS * NPG], I32)
            ptl = A.alloc([NS * NPG], I32)
            bidx = Buf("idx")
            Knew = A.alloc([1024])
            Vnewb = A.alloc([1024], BF16)
            Vnewf = prod
            bnew = Buf("new")
            On = A.alloc([1024])
            bOn = Buf("On")
            ods = A.alloc([1024])
            bods = Buf("ods")
            sq2 = prod
            bsq2 = bprod
            ms8 = A.alloc([8])
            bms8 = Buf("ms8")
            gsr = A.alloc([128])
            bgsr = Buf("gsr")
            oTs = A.alloc([8, NS], BF16)
            boTs = Buf("oTs")
            new_tmpg()
            P.dma("sp", ptl, pt_d, writes=[bidx])
            P.op("dve", lambda e: e.tensor_copy(out=idxf, in_=ptl), reads=[bidx], writes=[bidx])
            P.op("dve", lambda e: e.tensor_scalar(out=idxf, in0=idxf, scalar1=128.0, scalar2=iotap[:, 0:1], op0=ALU.mult, op1=ALU.add), reads=[bidx, bconst], writes=[bidx])
            P.op("dve", lambda e: e.tensor_copy(out=idx, in_=idxf), reads=[bidx], writes=[bidx])
            P.op("dve", lambda e: e.memset(Knew, 0.0), writes=[bnew])
            P.op("dve", lambda e: e.memset(Vnewf, 0.0), writes=[bprod])
            P.dma("sp", Knew[0:NS, :], kstok_o, reads=[bkTd], writes=[bnew])
            P.dma("sp", Vnewf[0:NS, :], v_o[T:TA, :], reads=[bvd], writes=[bprod])
            P.op("act", lambda e: e.activation(out=Vnewb, in_=Vnewf, func=AF.Copy), reads=[bprod], writes=[bnew])
            P.dma("sp", gsr[0:1, :], gsubrow_d[:, j, :], writes=[bgsr])
            P.op("dve", lambda e: e.tensor_scalar(out=gsr[0:1, :], in0=gsr[0:1, :], scalar1=1.0 - lam_init, scalar2=None, op0=ALU.mult), reads=[bgsr], writes=[bgsr])
            mod_norm(hTs, bhs, 0, mods, bmods, 0, scr, tiles=[4], loc=True)
            for h in range(H):
                s = h % 2
                P.dma("pool", wq[s], wview(w_q_d[j], h * 128, 128), writes=[bwq[s]])
                pq, pbq = bank()
                for k in range(8):
                    P.op("pe", lambda e: e.matmul(pq[:, 0:NS], lhsT=wq[s][:, k, :], rhs=hTs[:, k, :], start=(k == 0), stop=(k == 7)), reads=[bwq[s], bhs[0]], writes=[pbq])
                group_rstd(pq, pbq, NS, sqs, bsqs, rrs, brrs, bonesb[:])
                P.op("dve", lambda e: e.scalar_tensor_tensor(out=qs_all[:, h, :], in0=pq[:, 0:NS], scalar=gqs[:, j:j + 1], in1=rrs, op0=ALU.mult, op1=ALU.mult), reads=[pbq, brrs, bconst], writes=[bqs])
            rot["list"] = [0, 1]
            blockm = A.alloc([1024])
            coefm = A.alloc([2])
            coefc = A.alloc([1])
            bepi = Buf("epi")
            P.dma("sp", blockm[0:16, :], cst_d["blockm"], writes=[bepi])
            P.dma("sp", coefm[0:16, :], cst_d["coefm"], writes=[bepi])
            P.op("dve", lambda e: e.scalar_tensor_tensor(out=coefc[0:16, :], in0=coefm[0:16, 1:2], scalar=neglam[0:16, j:j + 1], in1=coefm[0:16, 0:1], op0=ALU.mult, op1=ALU.add), reads=[bepi, blam], writes=[bepi])
            qbcs = [qbc, A.alloc([1024], BF16)]
            bqbcs = [bqbc, Buf("qbc1")]
            Sp = [A.alloc([16]) for _ in range(3)]
            bSp = [Buf("Sp%d" % i) for i in range(3)]
            Pp = [A.alloc([16], BF16) for _ in range(3)]
            bPp = [Buf("Pp%d" % i) for i in range(3)]
            wcol = A.alloc([1])
            bwcol = Buf("wcol")
            masked = A.alloc([1024])
            bmasked = Buf("masked")
            Oacc = [[(banks[2], bbank[2]), (banks[3], bbank[3])], [(banks[4], bbank[4]), (banks[5], bbank[5])]]
            rowb = [(banks[6], bbank[6]), (banks[7], bbank[7])]
            gcnt = 0
            rcnt = 0
            for sm in range(NS):
                qb_ = qbcs[sm % 2]
                bqb_ = bqbcs[sm % 2]
                for h in range(H):
                    P.op("dve", lambda e: e.tensor_scalar(out=Rm[:, h, :], in0=identb[:], scalar1=qs_all[:, h, sm:sm + 1], scalar2=None, op0=ALU.mult), reads=[bconst, bqs], writes=[bRm])
                for half in range(2):
                    pb_, pbb_ = bank()
                    P.op("pe", lambda e: e.matmul(pb_[:, :], lhsT=ones1[:], rhs=Rm[:, 4 * half:4 * half + 4, :], start=True, stop=True), reads=[bRm, bconst], writes=[pbb_])
                    P.op("act", lambda e: e.activation(out=qb_[:, half * 512:(half + 1) * 512], in_=pb_[:, :], func=AF.Copy), reads=[pbb_], writes=[bqb_])
                Oa = Oacc[sm % 2]
                pden, pbden = bank()
                for pg in range(NPG + 1):
                    r = rcnt % 3
                    rcnt += 1
                    if pg < NPG:
                        g = gcnt % NPF
                        gcnt += 1
                        col = sm * NPG + pg
                        P.dma("pool", None, None, reads=[bidx], writes=[bkpg[g]], fn=(lambda e, g=g, col=col: e.indirect_dma_start(out=kpg[g], out_offset=None, in_=ck_d, in_offset=bass.IndirectOffsetOnAxis(ap=idx[:, col:col + 1], axis=0))))
                        P.dma("pool", None, None, reads=[bidx], writes=[bvpg[g]], fn=(lambda e, g=g, col=col: e.indirect_dma_start(out=vpg[g], out_offset=None, in_=cv_d, in_offset=bass.IndirectOffsetOnAxis(ap=idx[:, col:col + 1], axis=0))))
                        vb_ = vpb[g]
                        bvb_ = bvpb[g]
                        P.op("act", lambda e: e.activation(out=vb_, in_=vpg[g], func=AF.Copy), reads=[bvpg[g]], writes=[bvb_])
                        ksrc, bks_ = kpg[g], bkpg[g]
                    else:
                        ksrc, bks_ = Knew, bnew
                        vb_, bvb_ = Vnewb, bnew
                    P.op("dve", lambda e: e.tensor_tensor(out=prod, in0=ksrc, in1=qb_, op=ALU.mult), reads=[bks_, bqb_], writes=[bprod])
                    P.op("dve", lambda e: e.tensor_reduce(out=Sp[r], in_=prod.rearrange("p (g d) -> p g d", d=64), axis=AX.X, op=ALU.add), reads=[bprod], writes=[bSp[r]])
                    if pg < NPG:
                        P.op("dve", lambda e: e.tensor_tensor(out=Sp[r].rearrange("p (h c) -> p h c", c=2), in0=Sp[r].rearrange("p (h c) -> p h c", c=2), in1=alis[:, pg, :].unsqueeze(2).to_broadcast([128, H, 2]), op=ALU.add), reads=[bSp[r], bconst], writes=[bSp[r]])
                    else:
                        P.op("dve", lambda e: e.tensor_scalar(out=Sp[r], in0=Sp[r], scalar1=newmask[:, sm:sm + 1], scalar2=None, op0=ALU.add), reads=[bSp[r], bconst], writes=[bSp[r]])
                    P.op("act", lambda e: e.activation(out=Pp[r], in_=Sp[r], func=AF.Exp), reads=[bSp[r]], writes=[bPp[r]])
                    for half in range(2):
                        P.op("pe", lambda e: e.matmul(Oa[half][0][0:16, :], lhsT=Pp[r], rhs=vb_[:, half * 512:(half + 1) * 512], start=(pg == 0), stop=(pg == NPG)), reads=[bPp[r], bvb_], writes=[Oa[half][1]])
                    P.op("pe", lambda e: e.matmul(pden[0:16, 0:1], lhsT=Pp[r], rhs=ones1[:, 0:1], start=(pg == 0), stop=(pg == NPG)), reads=[bPp[r], bconst], writes=[pbden])
                P.op("dve", lambda e: e.reciprocal(out=wcol[0:16, :], in_=pden[0:16, 0:1]), reads=[pbden], writes=[bwcol])
                P.op("dve", lambda e: e.tensor_tensor(out=wcol[0:16, :], in0=wcol[0:16, :], in1=coefc[0:16, :], op=ALU.mult), reads=[bwcol, bepi], writes=[bwcol])
                for half in range(2):
                    P.op("dve", lambda e: e.tensor_tensor(out=masked[0:16, half * 512:(half + 1) * 512], in0=Oa[half][0][0:16, :], in1=blockm[0:16, half * 512:(half + 1) * 512], op=ALU.mult), reads=[Oa[half][1], bepi], writes=[bmasked])
                for half in range(2):
                    P.op("pe", lambda e: e.matmul(rowb[half][0][0:1, :], lhsT=wcol[0:16, 0:1], rhs=masked[0:16, half * 512:(half + 1) * 512], start=True, stop=True), reads=[bwcol, bmasked], writes=[rowb[half][1]])
                    P.op("act", lambda e: e.activation(out=ods[0:1, half * 512:(half + 1) * 512], in_=rowb[half][0][0:1, :], func=AF.Copy), reads=[rowb[half][1]], writes=[bods])
                P.op("dve", lambda e: e.tensor_tensor(out=On[0:1, 0:1024], in0=ods[0:1, :], in1=ods[0:1, :], op=ALU.mult), reads=[bods], writes=[bOn])
                P.op("dve", lambda e: e.tensor_reduce(out=ms8[0:1, :], in_=On[0:1, 0:1024].rearrange("p (h e) -> p h e", e=128), axis=AX.X, op=ALU.add), reads=[bOn], writes=[bms8])
                P.op("act", lambda e: e.activation(out=ms8[0:1, :], in_=ms8[0:1, :], func=AF.Sqrt, bias=epsc[0:1, 0:1], scale=1.0 / 128), reads=[bms8, bconst], writes=[bms8])
                P.op("dve", lambda e: e.reciprocal(out=ms8[0:1, :], in_=ms8[0:1, :]), reads=[bms8], writes=[bms8])
                P.op("dve", lambda e: e.tensor_tensor(out=ods[0:1, :].rearrange("p (h e) -> p h e", e=128), in0=ods[0:1, :].rearrange("p (h e) -> p h e", e=128), in1=ms8[0:1, :].unsqueeze(2).to_broadcast([1, 8, 128]), op=ALU.mult), reads=[bods, bms8], writes=[bods])
                P.op("dve", lambda e: e.tensor_tensor(out=ods[0:1, :].rearrange("p (h e) -> p h e", e=128), in0=ods[0:1, :].rearrange("p (h e) -> p h e", e=128), in1=gsr[0:1, :].unsqueeze(1).to_broadcast([1, 8, 128]), op=ALU.mult), reads=[bods, bgsr], writes=[bods])
                pc, pbc = bank()
                for h in range(H):
                    P.op("pe", lambda e: e.matmul(pc[:, h:h + 1], lhsT=ods[0:1, h * 128:(h + 1) * 128], rhs=ones1f[0:1, 0:1], start=True, stop=True, skip_group_check=True), reads=[bods, bconst], writes=[pbc])
                P.op("act", lambda e: e.activation(out=oTs[:, :, sm], in_=pc[:, 0:8], func=AF.Copy), reads=[pbc], writes=[boTs])
            rot["list"] = list(range(8))
            for oc in range(8):
                s = oc % 2
                P.dma("pool", wo[s], wview(w_o_d[j], oc * 128, 128), writes=[bwo[s]])
                pt, pb = bank()
                for k in range(8):
                    P.op("pe", lambda e: e.matmul(pt[:, 0:NS], lhsT=wo[s][:, k, :], rhs=oTs[:, k, :], start=(k == 0), stop=(k == 7)), reads=[bwo[s], boTs], writes=[pb])
                resid_update(4, T, NS, oc, pt, pb, 16)

        for l in range(DEPTH):
            ada_mods(w_ada_d[l], "b_ada%d" % l, 48, mods, bmods)
            make_gs(0, "g_mix%d" % l, 8, mods, bmods)
            make_gs(1, "g_ffn%d" % l, 32, mods, bmods)
            if l < N_A:
                lru_layer(l)
            else:
                attn_layer(l - N_A)
            ffn_layer(l)
            if l == N_A - 1:
                kv_phase()

        for ti, (t0, n) in enumerate(TILES):
            P.dma("sp", yT_o[:, :, t0:t0 + n], xres[:, :, t0:t0 + n], reads=[bx[ti]])
        P.finish()
        P.emit()
    return nc


_CACHE = {}


def kernel(**inp):
    inp = {k: np.asarray(v) for k, v in inp.items()}
    n_pool = inp["cache_k"].shape[0]
    if n_pool not in _CACHE:
        _CACHE[n_pool] = build_program(n_pool)
    nc = _CACHE[n_pool]
    vecs = _pack_vecs(inp)
    cst = _consts()
    lamv = np.stack([np.stack([inp["lam_q1"][j], inp["lam_k1"][j], inp["lam_q2"][j], inp["lam_k2"][j]]) for j in range(2)])
    lamv = np.ascontiguousarray(np.broadcast_to(lamv[None], (128, 2, 4, 64))).astype(np.float32)
    ck = np.ascontiguousarray(inp["cache_k"]).reshape(n_pool * 128, 1024)
    cv = np.ascontiguousarray(inp["cache_v"]).reshape(n_pool * 128, 1024)

    def fm(a):
        rows, F = a.shape
        return np.ascontiguousarray(a.reshape(rows, F // 128, 128).transpose(2, 1, 0))

    in_maps = []
    for i in range(NCORES):
        ss = slice(NS * i, NS * (i + 1))
        xa = np.concatenate([inp["x_prompt"][i], inp["x_sample"][ss, 0]], axis=0)
        ca = np.concatenate([inp["c_prompt"][i:i + 1], inp["c_sample"][ss]], axis=0)
        h0 = np.stack([fm(inp["state_lru_h"][a, ss]) for a in range(N_A)], axis=1)
        c0 = np.stack([np.stack([fm(inp["state_lru_conv"][a, ss, j]) for j in range(3)], axis=2) for a in range(N_A)], axis=1)
        f0 = np.stack([np.stack([fm(inp["state_ffn_conv"][l, ss, j]) for j in range(2)], axis=2) for l in range(DEPTH)], axis=1)
        ptb = np.ascontiguousarray(np.broadcast_to(inp["page_table"][ss].reshape(1, NS * NPG), (128, NS * NPG))).astype(np.int32)
        m = {
            "xT": fm(xa), "cT": fm(ca), "vecs": vecs, "lamv": lamv,
            "h0T": np.ascontiguousarray(h0), "c0T": np.ascontiguousarray(c0), "f0T": np.ascontiguousarray(f0),
            "ptb": ptb, "cache_k": ck, "cache_v": cv, "gsubrow": np.ascontiguousarray(inp["g_subln"].reshape(1, 2, 128)).astype(np.float32),
            "w_ada": inp["w_ada"], "w_lru_in": inp["w_lru_in"], "w_gate_x": inp["w_gate_x"], "w_gate_a": inp["w_gate_a"],
            "w_lru_out": inp["w_lru_out"], "w_ada_kv": inp["w_ada_kv"], "w_kv": inp["w_kv"], "w_q": inp["w_q"], "w_o": inp["w_o"],
            "w_up": inp["w_up"], "w_down": inp["w_down"],
        }
        for k, v in cst.items():
            m["c_" + k] = v
        in_maps.append(m)
    res = run_bass_kernel_spmd(nc, in_maps, core_ids=list(range(NCORES)))
    R = res.results

    def tm(a):
        return np.ascontiguousarray(a.transpose(2, 1, 0).reshape(a.shape[2], a.shape[1] * 128))

    B = NCORES
    y_p = np.zeros((B, T, D), np.float32)
    y_s = np.zeros((B * NS, 1, D), np.float32)
    k_p = np.zeros((B, T, H, 2, 64), np.float32)
    v_p = np.zeros((B, T, H, 128), np.float32)
    k_s = np.zeros((B * NS, 1, H, 2, 64), np.float32)
    v_s = np.zeros((B * NS, 1, H, 128), np.float32)
    lh_p = np.zeros((N_A, B, D), np.float32)
    lh_s = np.zeros((N_A, B * NS, D), np.float32)
    lc_p = np.zeros((N_A, B, 3, D), np.float32)
    lc_s = np.zeros((N_A, B * NS, 3, D), np.float32)
    fc_p = np.zeros((DEPTH, B, 2, 6 * D), np.float32)
    fc_s = np.zeros((DEPTH, B * NS, 2, 6 * D), np.float32)
    for i in range(NCORES):
        r = R[i]
        ss = slice(NS * i, NS * (i + 1))
        yt = tm(r["yT"])
        y_p[i] = yt[:T]
        y_s[ss, 0] = yt[T:]
        kt = tm(r["kT"])
        k_p[i] = kt[:T].reshape(T, H, 2, 64)
        k_s[ss, 0] = kt[T:].reshape(NS, H, 2, 64)
        v_p[i] = r["vtok"][:T].reshape(T, H, 128)
        v_s[ss, 0] = r["vtok"][T:].reshape(NS, H, 128)
        for a in range(N_A):
            hh = tm(r["lruh"][:, a])
            lh_p[a, i] = hh[0]
            lh_s[a, ss] = hh[1:]
            lc_p[a, i] = tm(r["lrucp"][:, a])
            for j in range(3):
                lc_s[a, ss, j] = tm(r["lrucs"][:, a, :, j, :])
        for l in range(DEPTH):
            fc_p[l, i] = tm(r["ffncp"][:, l])
            for j in range(2):
                fc_s[l, ss, j] = tm(r["ffncs"][:, l, :, j, :])
    return (y_p, y_s, k_p, v_p, k_s, v_s, lh_p, lh_s, lc_p, lc_s, fc_p, fc_s)
```

```python
import math
import contextlib
import numpy as np
import concourse.bass as bass
import concourse.mybir as mybir
from concourse.bass_utils import run_bass_kernel_spmd

F32 = mybir.dt.float32
BF16 = mybir.dt.bfloat16
I32 = mybir.dt.int32
AF = mybir.ActivationFunctionType
ALU = mybir.AluOpType
AX = mybir.AxisListType

NCORES = 8
D = 1024
T = 2048
NS = 16
TA = T + NS
H = 8
DEPTH = 4
N_A = 2
NPG = 16
TILES = [(0, 512), (512, 512), (1024, 512), (1536, 512), (2048, 16)]
EPS = 1e-6
NEG = -30000.0
SLOPES = [2.0 ** (-(h + 1)) for h in range(H)]

ENGS = ["pe", "act", "dve", "pool", "sp"]
EPOCH = 20000


class Buf:
    __slots__ = ("name", "w", "r", "dkey", "dcount")

    def __init__(self, name=""):
        self.name = name
        self.w = None
        self.r = {}
        self.dkey = None
        self.dcount = 0


class _Rec:
    def __getattr__(self, name):
        def f(*a, **k):
            self.call = (name, a, k)
            return self
        return f


class Prog:
    def __init__(self, nc):
        self.nc = nc
        self.q = {e: [] for e in ENGS}
        self.cnt = {e: 0 for e in ENGS}
        self.waited = {e: {} for e in ENGS}
        self.keys = []
        self.keyset = set()
        self.final = {}
        self.nd = 0
        self.free_dkeys = []
        self.dbufs = []

    def _key(self, k):
        if k not in self.keyset:
            self.keyset.add(k)
            self.keys.append(k)
        return k

    def _collect(self, eng, reads, writes):
        need = {}

        def add(ev, raw):
            if ev is None:
                return
            k, v = ev
            if (not raw) and k[0] == eng:
                return
            if need.get(k, 0) < v:
                need[k] = v

        for b in reads:
            add(b.w, True)
        for b in writes:
            add(b.w, False)
            for k, v in b.r.items():
                add((k, v), False)
        waits = []
        wd = self.waited[eng]
        for k, v in need.items():
            if wd.get(k, 0) < v:
                wd[k] = v
                waits.append((k, v))
        return waits

    def _commit(self, ev, reads, writes):
        k, v = ev
        for b in reads:
            if b.r.get(k, 0) < v:
                b.r[k] = v
        for b in writes:
            b.w = ev
            b.r = {}
        if self.final.get(k, 0) < v:
            self.final[k] = v

    def op(self, eng, fn, reads=(), writes=()):
        rec = _Rec()
        fn(rec)
        name, a, k = rec.call

        def fn(e, name=name, a=a, k=k):
            return getattr(e, name)(*a, **k)
        waits = self._collect(eng, reads, writes)
        self.cnt[eng] += 1
        c = self.cnt[eng]
        key = self._key((eng, (c - 1) // EPOCH))
        val = (c - 1) % EPOCH + 1
        self._commit((key, val), reads, writes)
        self.q[eng].append((waits, fn, key, 1))

    def I(self, eng, method, reads, writes, *args, **kw):
        def fn(e, method=method, args=args, kw=kw):
            return getattr(e, method)(*args, **kw)
        self.op(eng, fn, reads=reads, writes=writes)

    def dma(self, eng, out, in_, reads=(), writes=(), fn=None):
        waits = self._collect(eng, reads, writes)
        prim = writes[0] if len(writes) else reads[0]
        if prim.dkey is None:
            if self.free_dkeys:
                prim.dkey = self.free_dkeys.pop()
            else:
                prim.dkey = self._key(("d", self.nd))
                self.nd += 1
            prim.dcount = self.final.get(prim.dkey, 0)
            self.dbufs.append(prim)
        prim.dcount += 16
        ev = (prim.dkey, prim.dcount)
        self._commit(ev, reads, writes)
        if fn is None:
            def fn(e, out=out, in_=in_):
                return e.dma_start(out=out, in_=in_)
        self.q[eng].append((waits, fn, prim.dkey, 16))

    def barrier(self):
        for b in self.dbufs:
            if b.dkey is not None:
                self.free_dkeys.append(b.dkey)
                b.dkey = None
        self.dbufs = []
        snap = dict(self.final)
        for eng in ENGS:
            wd = self.waited[eng]
            waits = []
            for k, v in snap.items():
                if wd.get(k, 0) < v:
                    wd[k] = v
                    waits.append((k, v))
            if waits:
                self.q[eng].append((waits, None, None, 0))

    def finish(self):
        wd = self.waited["sp"]
        waits = []
        for k, v in self.final.items():
            if wd.get(k, 0) < v:
                wd[k] = v
                waits.append((k, v))
        self.q["sp"].append((waits, None, None, 0))

    def emit(self):
        nc = self.nc
        with contextlib.ExitStack() as st:
            sems = {}
            for i, k in enumerate(self.keys):
                sems[k] = st.enter_context(nc.semaphore("s%d" % i))
            block = st.enter_context(nc.Block())

            def run(e, lst):
                for waits, fn, key, inc in lst:
                    for k, v in waits:
                        e.wait_ge(sems[k], v)
                    if fn is not None:
                        fn(e).then_inc(sems[key], inc)

            @block.tensor
            def _(e):
                run(e, self.q["pe"])

            @block.scalar
            def _(e):
                run(e, self.q["act"])

            @block.vector
            def _(e):
                run(e, self.q["dve"])

            @block.gpsimd
            def _(e):
                run(e, self.q["pool"])

            @block.sync
            def _(e):
                run(e, self.q["sp"])


class Arena:
    def __init__(self, tensor, nfloats):
        self.t = tensor
        self.n = nfloats
        self.off = 0

    def reset(self):
        self.off = 0

    def alloc(self, shape, dt=F32, parts=128):
        n = 1
        for s in shape:
            n *= s
        nbytes = n * (2 if dt == BF16 else 4)
        nf = (nbytes + 3) // 4
        nf = (nf + 7) // 8 * 8
        assert self.off + nf <= self.n, ("arena overflow", self.off, nf, self.n)
        ap = self.t[0:parts, self.off:self.off + nf]
        if dt != F32:
            ap = ap.bitcast(dt)
        ap = ap[:, 0:n]
        if len(shape) == 2:
            ap = ap.rearrange("p (a b) -> p a b", a=shape[0])
        elif len(shape) == 3:
            ap = ap.rearrange("p (a b c) -> p a b c", a=shape[0], b=shape[1])
        self.last_off = self.off
        self.off += nf
        return ap


def _vec_layout():
    ent = []

    def add(name, ncols):
        ent.append((name, ncols))

    for l in range(DEPTH):
        add("b_ada%d" % l, 48)
        add("g_mix%d" % l, 8)
        add("g_ffn%d" % l, 8)
        for j in range(3):
            add("w_fc%d_%d" % (l, j), 48)
        add("b_fc%d" % l, 48)
    for i in range(N_A):
        for j in range(4):
            add("w_lc%d_%d" % (i, j), 8)
        add("b_lc%d" % i, 8)
        add("b_gx%d" % i, 8)
        add("b_ga%d" % i, 8)
        add("lrul%d" % i, 8)
    add("b_ada_kv", 16)
    add("g_kv", 8)
    add("gk2", 1)
    for j in range(2):
        add("gq2_%d" % j, 1)
        add("gsub%d" % j, 1)
    base = {}
    off = 0
    for name, n in ent:
        base[name] = off
        off += n
    return ent, base, off


VENT, VBASE, NV = _vec_layout()


def _pack_vecs(inp):
    cols = {}
    for l in range(DEPTH):
        cols["b_ada%d" % l] = inp["b_ada"][l]
        cols["g_mix%d" % l] = inp["g_norm_mix"][l]
        cols["g_ffn%d" % l] = inp["g_norm_ffn"][l]
        for j in range(3):
            cols["w_fc%d_%d" % (l, j)] = inp["w_ffn_conv"][l, j]
        cols["b_fc%d" % l] = inp["b_ffn_conv"][l]
    for i in range(N_A):
        for j in range(4):
            cols["w_lc%d_%d" % (i, j)] = inp["w_lru_conv"][i, j]
        cols["b_lc%d" % i] = inp["b_lru_conv"][i]
        cols["b_gx%d" % i] = inp["b_gate_x"][i]
        cols["b_ga%d" % i] = inp["b_gate_a"][i]
        cols["lrul%d" % i] = inp["lru_log_param"][i]
    cols["b_ada_kv"] = inp["b_ada_kv"]
    cols["g_kv"] = inp["g_norm_kv"]
    cols["gk2"] = np.concatenate([inp["g_k_norm"], inp["g_k_norm"]])
    for j in range(2):
        cols["gq2_%d" % j] = np.concatenate([inp["g_q_norm"][j], inp["g_q_norm"][j]])
        cols["gsub%d" % j] = inp["g_subln"][j]
    out = np.zeros((128, NV), np.float32)
    for name, n in VENT:
        v = np.asarray(cols[name], np.float32).reshape(n, 128)
        out[:, VBASE[name]:VBASE[name] + n] = v.T
    return out


def _consts():
    c = {}
    c["identf"] = np.eye(128, dtype=np.float32)
    k = np.arange(128)[:, None]
    q = np.arange(128)[None, :]
    c["maskT"] = np.where(k <= q, 0.0, NEG).astype(np.float32)
    bo = np.zeros((128, 128), np.float32)
    bo[:64, :64] = 1.0 / 64
    bo[64:, 64:] = 1.0 / 64
    c["bones"] = bo
    ka = np.zeros((4, 16, 128), np.float32)
    for v in range(16):
        ka[0, v] = np.arange(128)
        ka[1, v] = 1.0
        ka[2, v] = 1.0
        ka[3, v] = 128.0 * (v - 12)
    c["kaug"] = ka
    qa = np.zeros((4, H, 512), np.float32)
    ql = np.arange(512)
    qhi = (ql // 128) * 128
    qlo = ql - qhi
    for h in range(H):
        qa[0, h] = SLOPES[h]
        qa[1, h] = -SLOPES[h] * qhi
        qa[2, h] = -SLOPES[h] * qlo
        qa[3, h] = SLOPES[h]
    c["qaug"] = qa
    al = np.zeros((128, NPG, H), np.float32)
    for pg in range(NPG):
        for h in range(H):
            al[:, pg, h] = -SLOPES[h] * (T - (pg * 128 + np.arange(128)))
    c["alis"] = al
    nm = np.full((128, NS), NEG, np.float32)
    for s in range(NS):
        nm[s, s] = 0.0
    c["newmask"] = nm
    c["iotap"] = np.arange(128, dtype=np.float32).reshape(128, 1)
    bm = np.zeros((16, 1024), np.float32)
    for hc in range(16):
        bm[hc, (hc // 2) * 128:(hc // 2 + 1) * 128] = 1.0
    c["blockm"] = bm
    cm = np.zeros((16, 2), np.float32)
    cm[0::2, 0] = 1.0
    cm[1::2, 1] = 1.0
    c["coefm"] = cm
    return c


def build_program(n_pool):
    nc = bass.Bass("TRN2", target_bir_lowering=False)

    def din(name, shape, dt=F32):
        return nc.dram_tensor(name, list(shape), dt, kind="ExternalInput").ap()

    def dout(name, shape):
        return nc.dram_tensor(name, list(shape), F32, kind="ExternalOutput").ap()

    xT_d = din("xT", [128, 8, TA])
    cT_d = din("cT", [128, 8, 17])
    vecs_d = din("vecs", [128, NV])
    lamv_d = din("lamv", [128, 2, 4, 64])
    h0T_d = din("h0T", [128, N_A, 8, NS])
    c0T_d = din("c0T", [128, N_A, 8, 3, NS])
    f0T_d = din("f0T", [128, DEPTH, 48, 2, NS])
    pt_d = din("ptb", [128, NS * NPG], I32)
    ck_d = din("cache_k", [n_pool * 128, 1024])
    cv_d = din("cache_v", [n_pool * 128, 1024])
    w_ada_d = din("w_ada", [DEPTH, D, 6 * D])
    w_lin_d = din("w_lru_in", [N_A, D, 2 * D])
    w_gx_d = din("w_gate_x", [N_A, 8, 128, 128])
    w_ga_d = din("w_gate_a", [N_A, 8, 128, 128])
    w_lout_d = din("w_lru_out", [N_A, D, D])
    w_adakv_d = din("w_ada_kv", [D, 2 * D])
    w_kv_d = din("w_kv", [D, 2 * D])
    w_q_d = din("w_q", [2, D, D])
    w_o_d = din("w_o", [2, D, D])
    w_up_d = din("w_up", [DEPTH, D, 6 * D])
    w_dn_d = din("w_down", [DEPTH, 3 * D, D])
    gsubrow_d = din("gsubrow", [1, 2, 128])
    cst = _consts()
    cst_d = {k: din("c_" + k, v.shape) for k, v in cst.items()}
    kstok_o = dout("kstok", [NS, 1024])

    yT_o = dout("yT", [128, 8, TA])
    kT_o = dout("kT", [128, 8, TA])
    v_o = dout("vtok", [TA, 1024])
    lh_o = dout("lruh", [128, N_A, 8, 17])
    lcp_o = dout("lrucp", [128, N_A, 8, 3])
    lcs_o = dout("lrucs", [128, N_A, 8, 3, NS])
    fcp_o = dout("ffncp", [128, DEPTH, 48, 2])
    fcs_o = dout("ffncs", [128, DEPTH, 48, 2, NS])

    P = Prog(nc)
    st = contextlib.ExitStack()
    with st:
        def sb(name, shape, dt=F32):
            return st.enter_context(nc.sbuf_tensor("sb_" + name, list(shape), dt))

        xres = sb("xres", [128, 8, TA])
        bx = [Buf("x%d" % i) for i in range(5)]
        vecs = sb("vecs", [128, NV])
        bvec = Buf("vecs")
        mods = sb("mods", [128, 48, 17])
        bmods = Buf("mods")
        gsm = sb("gsm", [128, 3, 8, 17])
        bgsm = Buf("gsm")
        modkv = sb("modkv", [128, 16, 17])
        bmodkv = Buf("modkv")
        scT = sb("scT", [128, 8, 17], BF16)
        bscT = Buf("scT")
        identf = sb("identf", [128, 128])
        identb = sb("identb", [128, 128], BF16)
        maskTb = sb("maskTb", [128, 128], BF16)
        bonesb = sb("bonesb", [128, 128], BF16)
        onesd = sb("onesd", [128, 128], BF16)
        onesh = sb("onesh", [128, 128], BF16)
        ones1 = sb("ones1", [128, 128], BF16)
        ones1f = sb("ones1f", [128, 128])
        kaugb = sb("kaugb", [4, 16, 128], BF16)
        gqs = sb("gqs", [128, 2])
        gsubs = sb("gsubs", [128, 2])
        alis = sb("alis", [128, NPG, H])
        newmask = sb("newmask", [128, NS])
        iotap = sb("iotap", [128, 1])
        cL = sb("cL", [128, N_A * 8])
        neglam = sb("neglam", [128, 2])
        epsc = sb("epsc", [128, 1])
        bconst = Buf("const")
        bcL = Buf("cL")
        blam = Buf("lam")
        ARENA_F = 32000
        arena_t = sb("arena", [128, ARENA_F])
        A = Arena(arena_t, ARENA_F)

        banks = [st.enter_context(nc.psum_tensor("bank%d" % i, [128, 512], F32)) for i in range(8)]
        bbank = [Buf("bank%d" % i) for i in range(8)]
        rot = {"list": list(range(8)), "i": 0}

        def bank():
            i = rot["list"][rot["i"] % len(rot["list"])]
            rot["i"] += 1
            return banks[i], bbank[i]

        def V(name, j=0):
            c = VBASE[name] + j
            return vecs[:, c:c + 1]

        P.dma("sp", vecs[:], vecs_d, writes=[bvec])
        P.dma("sp", identf[:], cst_d["identf"], writes=[bconst])
        P.dma("pool", identb[:], cst_d["identf"], writes=[bconst])
        P.dma("pool", maskTb[:], cst_d["maskT"], writes=[bconst])
        P.dma("pool", bonesb[:], cst_d["bones"], writes=[bconst])
        P.dma("pool", kaugb[:], cst_d["kaug"], writes=[bconst])
        P.dma("sp", alis[:], cst_d["alis"], writes=[bconst])
        P.dma("sp", newmask[:], cst_d["newmask"], writes=[bconst])
        P.dma("sp", iotap[:], cst_d["iotap"], writes=[bconst])
        P.op("dve", lambda e: e.memset(onesd[:], 1.0 / 1024), writes=[bconst])
        P.op("dve", lambda e: e.memset(onesh[:], 1.0 / 128), writes=[bconst])
        P.op("dve", lambda e: e.memset(ones1[:], 1.0), writes=[bconst])
        P.op("dve", lambda e: e.memset(ones1f[:], 1.0), writes=[bconst])
        P.op("dve", lambda e: e.memset(epsc[:], EPS), writes=[bconst])
        for ti, (t0, n) in enumerate(TILES):
            P.dma("sp", xres[:, :, t0:t0 + n], xT_d[:, :, t0:t0 + n], writes=[bx[ti]])

        A.reset()
        cTs = A.alloc([8, 17])
        bcTs = Buf("cTs")
        P.dma("sp", cTs, cT_d, writes=[bcTs])
        P.op("act", lambda e: e.activation(out=scT[:], in_=cTs, func=AF.Silu), reads=[bcTs], writes=[bscT])
        for i in range(N_A):
            src = vecs[:, VBASE["lrul%d" % i]:VBASE["lrul%d" % i] + 8]
            dst = cL[:, i * 8:(i + 1) * 8]
            P.op("act", lambda e, src=src, dst=dst: e.activation(out=dst, in_=src, func=AF.Exp, scale=-1.0), reads=[bvec], writes=[bcL])
            P.op("act", lambda e, dst=dst: e.activation(out=dst, in_=dst, func=AF.Ln, bias=ones1f[:, 0:1], scale=1.0), reads=[bcL, bconst], writes=[bcL])
            P.op("act", lambda e, dst=dst: e.activation(out=dst, in_=dst, func=AF.Copy, scale=-8.0), reads=[bcL], writes=[bcL])
        lamv = A.alloc([2, 4, 64])
        blamv = Buf("lamv")
        lamp = A.alloc([2, 2, 64])
        lams = A.alloc([2, 2])
        P.dma("sp", lamv, lamv_d, writes=[blamv])
        for j in range(2):
            for pr in range(2):
                P.op("dve", lambda e, j=j, pr=pr: e.tensor_tensor(out=lamp[:, j, pr, :], in0=lamv[:, j, 2 * pr, :], in1=lamv[:, j, 2 * pr + 1, :], op=ALU.mult), reads=[blamv], writes=[blam])
                P.op("dve", lambda e, j=j, pr=pr: e.tensor_reduce(out=lams[:, j, pr:pr + 1], in_=lamp[:, j, pr, :], axis=AX.X, op=ALU.add), reads=[blam], writes=[blam])
        P.op("act", lambda e: e.activation(out=lams, in_=lams, func=AF.Exp), reads=[blam], writes=[blam])
        for j in range(2):
            lam_init = 0.8 - 0.6 * math.exp(-0.3 * (j + N_A))
            P.op("dve", lambda e, j=j, li=lam_init: e.scalar_tensor_tensor(out=neglam[:, j:j + 1], in0=lams[:, j, 1:2], scalar=-li, in1=lams[:, j, 0:1], op0=ALU.add, op1=ALU.subtract), reads=[blam], writes=[blam])

        def wview(w2d, c0, ncols):
            return w2d.rearrange("(kc p) n -> p kc n", p=128)[:, :, c0:c0 + ncols]

        mods_dram = nc.dram_tensor("mods_scr", [DEPTH, 128, 48, 17], F32, kind="Internal").ap()
        bmodsd = Buf("mods_dram")

        def ada_all():
            A.reset()
            P.barrier()
            NSL = 4
            slots = [A.alloc([8, 512], BF16) for _ in range(NSL)]
            bsl = [Buf("adaw%d" % i) for i in range(NSL)]
            gc = 0
            jobs = [(w_ada_d[l], "b_ada%d" % l, 48, mods, bmods, l) for l in range(DEPTH)] + [(w_adakv_d, "b_ada_kv", 16, modkv, bmodkv, None)]
            for wd, bname, nj, dst, bdst, l in jobs:
                ngrp = (nj * 128) // 512
                for g in range(ngrp):
                    s = gc % NSL
                    gc += 1
                    P.dma("pool", slots[s], wview(wd, g * 512, 512), writes=[bsl[s]])
                    pt, pb = bank()
                    for jj in range(4):
                        for k in range(8):
                            P.op("pe", lambda e: e.matmul(pt[:, jj * 32:jj * 32 + 17], lhsT=slots[s][:, k, jj * 128:(jj + 1) * 128], rhs=scT[:, k, :], start=(k == 0), stop=(k == 7)), reads=[bsl[s], bscT], writes=[pb])
                    for jj in range(4):
                        jx = g * 4 + jj
                        P.op("dve", lambda e: e.tensor_scalar(out=dst[:, jx, :], in0=pt[:, jj * 32:jj * 32 + 17], scalar1=V(bname, jx), scalar2=None, op0=ALU.add), reads=[pb, bvec], writes=[bdst])
                if l is not None:
                    P.dma("sp", mods_dram[l], mods[:], reads=[bmods], writes=[bmodsd])

        def make_gs(slot, gname, sc_off, src, bsrc):
            for c in range(8):
                P.op("dve", lambda e, c=c: e.tensor_scalar(out=gsm[:, slot, c, :], in0=src[:, sc_off + c, :], scalar1=1.0, scalar2=V(gname, c), op0=ALU.add, op1=ALU.mult), reads=[bsrc, bvec], writes=[bgsm])

        def mod_norm(hT, bh, slot, shift_src, bshift, shift_off, scr, tiles=None, loc=False, tw=256):
            sq = scr[:, 0:2048].bitcast(BF16).rearrange("p (a b) -> p a b", a=8)
            bsq = Buf("sq")
            rstd = scr[:, 2048:2560]
            brstd = Buf("rstd")
            tmps = [scr[:, 2560:2560 + 8 * tw].rearrange("p (a b) -> p a b", a=8), scr[:, 2560 + 8 * tw:2560 + 16 * tw].rearrange("p (a b) -> p a b", a=8)]
            btmps = [Buf("tmp0"), Buf("tmp1")]
            tcnt = 0
            for ti, (t0, n) in enumerate(TILES):
                if tiles is not None and ti not in tiles:
                    continue
                h0c = 0 if loc else t0
                bhh = bh[0] if loc else bh[ti]
                P.op("act", lambda e: e.activation(out=sq[:, :, 0:n], in_=xres[:, :, t0:t0 + n], func=AF.Square), reads=[bx[ti]], writes=[bsq])
                pt, pb = bank()
                for c in range(8):
                    P.op("pe", lambda e: e.matmul(pt[:, 0:n], lhsT=onesd[:], rhs=sq[:, c, 0:n], start=(c == 0), stop=(c == 7)), reads=[bsq, bconst], writes=[pb])
                P.op("act", lambda e: e.activation(out=rstd[:, 0:n], in_=pt[:, 0:n], func=AF.Sqrt, bias=epsc[:, 0:1], scale=1.0), reads=[pb, bconst], writes=[brstd])
                P.op("dve", lambda e: e.reciprocal(out=rstd[:, 0:n], in_=rstd[:, 0:n]), reads=[brstd], writes=[brstd])
                for c0 in range(0, n, tw):
                    m = min(tw, n - c0)
                    tmp = tmps[tcnt % 2]
                    btmp = btmps[tcnt % 2]
                    tcnt += 1
                    P.op("dve", lambda e: e.tensor_tensor(out=tmp[:, :, 0:m], in0=xres[:, :, t0 + c0:t0 + c0 + m], in1=rstd[:, c0:c0 + m].unsqueeze(1).to_broadcast([128, 8, m]), op=ALU.mult), reads=[bx[ti], brstd], writes=[btmp])
                    if ti < 4:
                        for c in range(8):
                            P.op("act", lambda e: e.activation(out=hT[:, c, h0c + c0:h0c + c0 + m], in_=tmp[:, c, 0:m], func=AF.Identity, scale=gsm[:, slot, c, 0:1], bias=shift_src[:, shift_off + c, 0:1]), reads=[btmp, bgsm, bshift], writes=[bhh])
                    else:
                        P.op("dve", lambda e: e.tensor_tensor(out=tmp[:, :, 0:m], in0=tmp[:, :, 0:m], in1=gsm[:, slot, :, 1:17], op=ALU.mult), reads=[btmp, bgsm], writes=[btmp])
                        P.op("dve", lambda e: e.tensor_tensor(out=hT[:, :, h0c:h0c + m], in0=tmp[:, :, 0:m], in1=shift_src[:, shift_off:shift_off + 8, 1:17], op=ALU.add), reads=[btmp, bshift], writes=[bhh])

        tmpg_holder = {}

        def new_tmpg():
            tmpg_holder["ap"] = A.alloc([16])
            tmpg_holder["buf"] = Buf("tmpg")

        def resid_update(ti, t0, n, oc, pt, pb, gate_off):
            if ti < 4:
                P.op("dve", lambda e: e.scalar_tensor_tensor(out=xres[:, oc, t0:t0 + n], in0=pt[:, 0:n], scalar=mods[:, gate_off + oc, 0:1], in1=xres[:, oc, t0:t0 + n], op0=ALU.mult, op1=ALU.add), reads=[pb, bmods, bx[ti]], writes=[bx[ti]])
            else:
                tmpg = tmpg_holder["ap"]
                bt = tmpg_holder["buf"]
                P.op("dve", lambda e: e.tensor_tensor(out=tmpg, in0=pt[:, 0:n], in1=mods[:, gate_off + oc, 1:17], op=ALU.mult), reads=[pb, bmods], writes=[bt])
                P.op("dve", lambda e: e.tensor_tensor(out=xres[:, oc, t0:t0 + n], in0=xres[:, oc, t0:t0 + n], in1=tmpg, op=ALU.add), reads=[bt, bx[ti]], writes=[bx[ti]])

        def lru_layer(i):
            A.reset()
            P.barrier()
            hT = A.alloc([8, TA], BF16)
            bh = [Buf("h%d" % t) for t in range(5)]
            yb = A.alloc([8, TA], BF16)
            by = [Buf("y%d" % t) for t in range(5)]
            wg = A.alloc([2, 8, 128], BF16)
            bwg = Buf("wg")
            P.dma("pool", wg[:, 0], w_gx_d[i].rearrange("n k j -> k n j"), writes=[bwg])
            P.dma("pool", wg[:, 1], w_ga_d[i].rearrange("n k j -> k n j"), writes=[bwg])
            wsl = [A.alloc([8, 256], BF16) for _ in range(2)]
            bws = [Buf("wlin%d" % s) for s in range(2)]
            h0 = A.alloc([8, NS])
            c0 = A.alloc([8, 3, NS])
            bst = Buf("lrustate")
            P.dma("sp", h0, h0T_d[:, i], writes=[bst])
            P.dma("sp", c0, c0T_d[:, i], writes=[bst])
            xbuf = A.alloc([T + 3])
            xs = A.alloc([NS])
            xc = A.alloc([TA])
            off_xc = A.last_off
            xcb = A.alloc([512], BF16)
            ta = A.alloc([TA])
            off_ta = A.last_off
            tom = A.alloc([TA])
            tgx = A.alloc([TA])
            hout = A.alloc([8, 17])
            new_tmpg()
            mod_norm(hT, bh, 0, mods, bmods, 0, arena_t[:, off_xc:off_xc + 6656])
            P.barrier()
            bxb, bxc, bxcb, bta, btom, btgx, bho = (Buf("xb"), Buf("xc"), Buf("xcb"), Buf("ta"), Buf("tom"), Buf("tgx"), Buf("hout"))
            P.op("dve", lambda e: e.memset(xbuf[:, 0:3], 0.0), writes=[bxb])
            for nch in range(8):
                s = nch % 2
                P.dma("pool", wsl[s][:, :, 0:128], wview(w_lin_d[i], nch * 128, 128), writes=[bws[s]])
                P.dma("pool", wsl[s][:, :, 128:256], wview(w_lin_d[i], D + nch * 128, 128), writes=[bws[s]])
                for ti, (t0, n) in enumerate(TILES):
                    pt, pb = bank()
                    for k in range(8):
                        P.op("pe", lambda e, k=k, t0=t0, n=n, pt=pt, s=s: e.matmul(pt[:, 0:n], lhsT=wsl[s][:, k, 0:128], rhs=hT[:, k, t0:t0 + n], start=(k == 0), stop=(k == 7)), reads=[bws[s], bh[ti]], writes=[pb])
                    if ti < 4:
                        P.op("act", lambda e, t0=t0, n=n, pt=pt: e.activation(out=xbuf[:, 3 + t0:3 + t0 + n], in_=pt[:, 0:n], func=AF.Copy), reads=[pb], writes=[bxb])
                    else:
                        P.op("act", lambda e, n=n, pt=pt: e.activation(out=xs, in_=pt[:, 0:n], func=AF.Copy), reads=[pb], writes=[bxb])
                P.dma("sp", lcp_o[:, i, nch, :], xbuf[:, T:T + 3], reads=[bxb])
                P.dma("sp", lcs_o[:, i, nch, 0:2, :], c0[:, nch, 1:3, :], reads=[bst])
                P.dma("sp", lcs_o[:, i, nch, 2, :], xs, reads=[bxb])
                wl = lambda j: V("w_lc%d_%d" % (i, j), nch)
                P.op("act", lambda e: e.activation(out=xc[:, 0:T], in_=xbuf[:, 0:T], func=AF.Identity, scale=wl(0), bias=V("b_lc%d" % i, nch)), reads=[bxb, bvec], writes=[bxc])
                for j in range(1, 4):
                    P.op("dve", lambda e, j=j: e.scalar_tensor_tensor(out=xc[:, 0:T], in0=xbuf[:, j:j + T], scalar=wl(j), in1=xc[:, 0:T], op0=ALU.mult, op1=ALU.add), reads=[bxb, bxc, bvec], writes=[bxc])
                P.op("act", lambda e: e.activation(out=xc[:, T:TA], in_=c0[:, nch, 0, :], func=AF.Identity, scale=wl(0), bias=V("b_lc%d" % i, nch)), reads=[bst, bvec], writes=[bxc])
                for j in range(1, 3):
                    P.op("dve", lambda e, j=j: e.scalar_tensor_tensor(out=xc[:, T:TA], in0=c0[:, nch, j, :], scalar=wl(j), in1=xc[:, T:TA], op0=ALU.mult, op1=ALU.add), reads=[bst, bxc, bvec], writes=[bxc])
                P.op("dve", lambda e: e.scalar_tensor_tensor(out=xc[:, T:TA], in0=xs, scalar=wl(3), in1=xc[:, T:TA], op0=ALU.mult, op1=ALU.add), reads=[bxb, bxc, bvec], writes=[bxc])
                for ti, (t0, n) in enumerate(TILES):
                    P.op("act", lambda e, t0=t0, n=n: e.activation(out=xcb[:, 0:n], in_=xc[:, t0:t0 + n], func=AF.Copy), reads=[bxc], writes=[bxcb])
                    pgx, pbx = bank()
                    P.op("pe", lambda e, n=n, pgx=pgx: e.matmul(pgx[:, 0:n], lhsT=wg[:, 0, nch, :], rhs=xcb[:, 0:n], start=True, stop=True), reads=[bwg, bxcb], writes=[pbx])
                    pga, pba = bank()
                    P.op("pe", lambda e, n=n, pga=pga: e.matmul(pga[:, 0:n], lhsT=wg[:, 1, nch, :], rhs=xcb[:, 0:n], start=True, stop=True), reads=[bwg, bxcb], writes=[pba])
                    P.op("act", lambda e, t0=t0, n=n, pgx=pgx: e.activation(out=tgx[:, t0:t0 + n], in_=pgx[:, 0:n], func=AF.Sigmoid, bias=V("b_gx%d" % i, nch)), reads=[pbx, bvec], writes=[btgx])
                    P.op("act", lambda e, t0=t0, n=n, pga=pga: e.activation(out=ta[:, t0:t0 + n], in_=pga[:, 0:n], func=AF.Sigmoid, bias=V("b_ga%d" % i, nch)), reads=[pba, bvec], writes=[bta])
                P.op("act", lambda e: e.activation(out=ta, in_=ta, func=AF.Exp, scale=cL[:, i * 8 + nch:i * 8 + nch + 1]), reads=[bta, bcL], writes=[bta])
                P.op("dve", lambda e: e.tensor_tensor(out=tom, in0=ta, in1=ta, op=ALU.mult), reads=[bta], writes=[btom])
                P.op("act", lambda e: e.activation(out=tom, in_=tom, func=AF.Sqrt, scale=-1.0, bias=ones1f[:, 0:1]), reads=[btom, bconst], writes=[btom])
                P.op("dve", lambda e: e.tensor_tensor(out=tgx, in0=tgx, in1=tom, op=ALU.mult), reads=[btgx, btom], writes=[btgx])
                P.op("dve", lambda e: e.tensor_tensor(out=tgx, in0=tgx, in1=xc, op=ALU.mult), reads=[btgx, bxc], writes=[btgx])
                P.op("dve", lambda e: e.tensor_tensor_scan(out=tom[:, 0:T], data0=ta[:, 0:T], data1=tgx[:, 0:T], initial=0.0, op0=ALU.mult, op1=ALU.add), reads=[bta, btgx, btom], writes=[btom])
                P.op("dve", lambda e: e.tensor_tensor(out=tom[:, T:TA], in0=ta[:, T:TA], in1=h0[:, nch, :], op=ALU.mult), reads=[bta, bst, btom], writes=[btom])
                P.op("dve", lambda e: e.tensor_tensor(out=tom[:, T:TA], in0=tom[:, T:TA], in1=tgx[:, T:TA], op=ALU.add), reads=[btgx, btom], writes=[btom])
                P.op("act", lambda e: e.activation(out=hout[:, nch, :], in_=tom[:, T - 1:TA], func=AF.Copy), reads=[btom], writes=[bho])
                for ti, (t0, n) in enumerate(TILES):
                    pt, pb = bank()
                    for k in range(8):
                        P.op("pe", lambda e, k=k, t0=t0, n=n, pt=pt, s=s: e.matmul(pt[:, 0:n], lhsT=wsl[s][:, k, 128:256], rhs=hT[:, k, t0:t0 + n], start=(k == 0), stop=(k == 7)), reads=[bws[s], bh[ti]], writes=[pb])
                    P.op("act", lambda e, t0=t0, n=n, pt=pt: e.activation(out=xc[:, t0:t0 + n], in_=pt[:, 0:n], func=AF.Gelu_apprx_tanh), reads=[pb, bxc], writes=[bxc])
                    P.op("dve", lambda e, t0=t0, n=n: e.tensor_tensor(out=yb[:, nch, t0:t0 + n], in0=xc[:, t0:t0 + n], in1=tom[:, t0:t0 + n], op=ALU.mult), reads=[bxc, btom], writes=[by[ti]])
            P.dma("sp", lh_o[:, i], hout, reads=[bho])
            wo = [A.alloc([8, 128], BF16) for _ in range(2)]
            bwo = [Buf("wlo%d" % s) for s in range(2)]
            for oc in range(8):
                s = oc % 2
                P.dma("pool", wo[s], wview(w_lout_d[i], oc * 128, 128), writes=[bwo[s]])
                for ti, (t0, n) in enumerate(TILES):
                    pt, pb = bank()
                    for k in range(8):
                        P.op("pe", lambda e, k=k, t0=t0, n=n, pt=pt, s=s: e.matmul(pt[:, 0:n], lhsT=wo[s][:, k, :], rhs=yb[:, k, t0:t0 + n], start=(k == 0), stop=(k == 7)), reads=[bwo[s], by[ti]], writes=[pb])
                    resid_update(ti, t0, n, oc, pt, pb, 16)

        def ffn_layer(l):
            A.reset()
            P.barrier()
            hT = A.alloc([8, TA], BF16)
            bh = [Buf("hf%d" % t) for t in range(5)]
            NG = 8
            GC = 3
            actb = [A.alloc([GC, TA], BF16) for _ in range(2)]
            bact = [[Buf("act%d_%d" % (s, t)) for t in range(5)] for s in range(2)]
            wup = [A.alloc([8, 256], BF16) for _ in range(3)]
            bwup = [Buf("wup%d" % s) for s in range(3)]
            wcnt = 0
            wdn = [A.alloc([GC, 1024], BF16) for _ in range(2)]
            bwdn = [Buf("wdn%d" % s) for s in range(2)]
            f0 = A.alloc([48, 2, NS])
            bf0 = Buf("f0")
            P.dma("sp", f0, f0T_d[:, l], writes=[bf0])
            P.dma("sp", fcs_o[:, l, :, 0, :], f0[:, :, 1, :], reads=[bf0])
            ub = [A.alloc([T + 2]) for _ in range(2)]
            off_ub = A.last_off - (T + 2 + 7) // 8 * 8
            us = [A.alloc([NS]) for _ in range(2)]
            bub = [Buf("ub%d" % s) for s in range(2)]
            uc = [A.alloc([TA]) for _ in range(2)]
            off_uc = A.last_off - TA
            assert off_uc % 8 == 0
            buc = [Buf("uc%d" % s) for s in range(2)]
            new_tmpg()
            mod_norm(hT, bh, 1, mods, bmods, 24, arena_t[:, off_ub:off_ub + 6656])
            P.barrier()
            fout = A.alloc([48, 2])
            fouts = A.alloc([48, NS])
            bfo = Buf("fout")
            for s in range(2):
                P.op("dve", lambda e, s=s: e.memset(ub[s][:, 0:2], 0.0), writes=[bub[s]])
            for g in range(NG):
                s = g % 2
                P.dma("pool", wdn[s], w_dn_d[l].rearrange("(kc p) n -> p kc n", p=128)[:, g * GC:(g + 1) * GC, :], writes=[bwdn[s]])
                for jj in range(GC):
                    ws = wcnt % 3
                    wcnt += 1
                    P.dma("pool", wup[ws][:, :, 0:128], wview(w_up_d[l], (g * GC + jj) * 128, 128), writes=[bwup[ws]])
                    P.dma("pool", wup[ws][:, :, 128:256], wview(w_up_d[l], 3 * D + (g * GC + jj) * 128, 128), writes=[bwup[ws]])
                    for part in range(2):
                        ch = part * 24 + g * GC + jj
                        wc0 = part * 128
                        for ti, (t0, n) in enumerate(TILES):
                            pt, pb = bank()
                            for k in range(8):
                                P.op("pe", lambda e, k=k, t0=t0, n=n, pt=pt, s=s, wc0=wc0: e.matmul(pt[:, 0:n], lhsT=wup[ws][:, k, wc0:wc0 + 128], rhs=hT[:, k, t0:t0 + n], start=(k == 0), stop=(k == 7)), reads=[bwup[ws], bh[ti]], writes=[pb])
                            if ti < 4:
                                P.op("act", lambda e, t0=t0, n=n, pt=pt, part=part: e.activation(out=ub[part][:, 2 + t0:2 + t0 + n], in_=pt[:, 0:n], func=AF.Copy), reads=[pb], writes=[bub[part]])
                            else:
                                P.op("act", lambda e, n=n, pt=pt, part=part: e.activation(out=us[part], in_=pt[:, 0:n], func=AF.Copy), reads=[pb], writes=[bub[part]])
                        P.op("act", lambda e, ch=ch, part=part: e.activation(out=fout[:, ch, :], in_=ub[part][:, T:T + 2], func=AF.Copy), reads=[bub[part]], writes=[bfo])
                        P.op("act", lambda e, ch=ch, part=part: e.activation(out=fouts[:, ch, :], in_=us[part], func=AF.Copy), reads=[bub[part]], writes=[bfo])
                        wf = lambda j, ch=ch: V("w_fc%d_%d" % (l, j), ch)
                        bfc = V("b_fc%d" % l, ch)
                        P.op("act", lambda e, part=part, wf=wf, bfc=bfc: e.activation(out=uc[part][:, 0:T], in_=ub[part][:, 0:T], func=AF.Identity, scale=wf(0), bias=bfc), reads=[bub[part], bvec], writes=[buc[part]])
                        for j in range(1, 3):
                            P.op("dve", lambda e, j=j, part=part, wf=wf: e.scalar_tensor_tensor(out=uc[part][:, 0:T], in0=ub[part][:, j:j + T], scalar=wf(j), in1=uc[part][:, 0:T], op0=ALU.mult, op1=ALU.add), reads=[bub[part], buc[part], bvec], writes=[buc[part]])
                        P.op("act", lambda e, part=part, wf=wf, bfc=bfc, ch=ch: e.activation(out=uc[part][:, T:TA], in_=f0[:, ch, 0, :], func=AF.Identity, scale=wf(0), bias=bfc), reads=[bf0, bvec], writes=[buc[part]])
                        P.op("dve", lambda e, part=part, wf=wf, ch=ch: e.scalar_tensor_tensor(out=uc[part][:, T:TA], in0=f0[:, ch, 1, :], scalar=wf(1), in1=uc[part][:, T:TA], op0=ALU.mult, op1=ALU.add), reads=[bf0, buc[part], bvec], writes=[buc[part]])
                        P.op("dve", lambda e, part=part, wf=wf: e.scalar_tensor_tensor(out=uc[part][:, T:TA], in0=us[part], scalar=wf(2), in1=uc[part][:, T:TA], op0=ALU.mult, op1=ALU.add), reads=[bub[part], buc[part], bvec], writes=[buc[part]])
                    P.op("act", lambda e: e.activation(out=uc[0], in_=uc[0], func=AF.Gelu_apprx_tanh), reads=[buc[0]], writes=[buc[0]])
                    for ti, (t0, n) in enumerate(TILES):
                        P.op("dve", lambda e, t0=t0, n=n, jj=jj, s=s: e.tensor_tensor(out=actb[s][:, jj, t0:t0 + n], in0=uc[0][:, t0:t0 + n], in1=uc[1][:, t0:t0 + n], op=ALU.mult), reads=[buc[0], buc[1]], writes=[bact[s][ti]])
                for oc in range(8):
                    for ti, (t0, n) in enumerate(TILES):
                        pt, pb = bank()
                        for k in range(GC):
                            P.op("pe", lambda e, k=k, t0=t0, n=n, pt=pt, s=s, oc=oc: e.matmul(pt[:, 0:n], lhsT=wdn[s][:, k, oc * 128:(oc + 1) * 128], rhs=actb[s][:, k, t0:t0 + n], start=(k == 0), stop=(k == GC - 1)), reads=[bwdn[s], bact[s][ti]], writes=[pb])
                        resid_update(ti, t0, n, oc, pt, pb, 40)
            P.dma("sp", fcp_o[:, l], fout, reads=[bfo])
            P.dma("sp", fcs_o[:, l, :, 1, :], fouts, reads=[bfo])


        bkTd = Buf("kT_dram")
        bvd = Buf("v_dram")

        def group_rstd(pt, pb, n, sqb, bsqb, rr, brr, ones_ap, p2b=None):
            P.op("act", lambda e: e.activation(out=sqb[:, 0:n], in_=pt[:, 0:n], func=AF.Square), reads=[pb], writes=[bsqb])
            p2, pb2 = p2b if p2b is not None else bank()
            P.op("pe", lambda e: e.matmul(p2[:, 0:n], lhsT=ones_ap, rhs=sqb[:, 0:n], start=True, stop=True), reads=[bsqb, bconst], writes=[pb2])
            P.op("act", lambda e: e.activation(out=rr[:, 0:n], in_=p2[:, 0:n], func=AF.Sqrt, bias=epsc[:, 0:1], scale=1.0), reads=[pb2, bconst], writes=[brr])
            P.op("dve", lambda e: e.reciprocal(out=rr[:, 0:n], in_=rr[:, 0:n]), reads=[brr], writes=[brr])

        def kv_phase():
            make_gs(2, "g_kv", 8, modkv, bmodkv)
            A.reset()
            P.barrier()
            hT = A.alloc([8, TA], BF16)
            bh = [Buf("hkv%d" % t) for t in range(5)]
            scr = A.alloc([6656])
            mod_norm(hT, bh, 2, modkv, bmodkv, 0, scr)
            wk = [A.alloc([8, 128], BF16) for _ in range(2)]
            bwk = [Buf("wk%d" % s) for s in range(2)]
            wv = A.alloc([8, 1024], BF16)
            bwv = Buf("wv")
            P.dma("pool", wv, wview(w_kv_d, D, 1024), writes=[bwv])
            sqb = A.alloc([512], BF16)
            bsqb = Buf("sqb")
            rr = A.alloc([512])
            brr = Buf("rr")
            kst = [A.alloc([512]) for _ in range(2)]
            bkst = [Buf("kst%d" % s) for s in range(2)]
            vst = [A.alloc([1024]) for _ in range(2)]
            bvst = [Buf("vst%d" % s) for s in range(2)]
            kstk = A.alloc([1024])
            bkstk = Buf("kstk")
            cnt = 0
            for hc in range(8):
                s = hc % 2
                P.dma("pool", wk[s], wview(w_kv_d, hc * 128, 128), writes=[bwk[s]])
                for ti, (t0, n) in enumerate(TILES):
                    pt, pb = bank()
                    for k in range(8):
                        P.op("pe", lambda e: e.matmul(pt[:, 0:n], lhsT=wk[s][:, k, :], rhs=hT[:, k, t0:t0 + n], start=(k == 0), stop=(k == 7)), reads=[bwk[s], bh[ti]], writes=[pb])
                    group_rstd(pt, pb, n, sqb, bsqb, rr, brr, bonesb[:])
                    ks = cnt % 2
                    cnt += 1
                    P.op("dve", lambda e: e.scalar_tensor_tensor(out=kst[ks][:, 0:n], in0=pt[:, 0:n], scalar=V("gk2"), in1=rr[:, 0:n], op0=ALU.mult, op1=ALU.mult), reads=[pb, brr, bvec], writes=[bkst[ks]])
                    P.dma("sp", kT_o[:, hc, t0:t0 + n], kst[ks][:, 0:n], reads=[bkst[ks]], writes=[bkTd])
                    if ti == 4:
                        ptt, pbt = bank()
                        P.op("pe", lambda e: e.transpose(out=ptt[0:NS, 0:128], in_=kst[ks][:, 0:NS], identity=identf[:]), reads=[bkst[ks], bconst], writes=[pbt])
                        P.op("act", lambda e: e.activation(out=kstk[0:NS, hc * 128:(hc + 1) * 128], in_=ptt[0:NS, 0:128], func=AF.Copy), reads=[pbt], writes=[bkstk])
            P.dma("sp", kstok_o, kstk[0:NS, :], reads=[bkstk], writes=[bkTd])
            vtiles = [(t * 128, 128) for t in range(16)] + [(T, NS)]
            for vi, (t0, m) in enumerate(vtiles):
                ti = min(t0 // 512, 4)
                vs = vi % 2
                for half in range(2):
                    pt, pb = bank()
                    for k in range(8):
                        P.op("pe", lambda e: e.matmul(pt[0:m, :], lhsT=hT[:, k, t0:t0 + m], rhs=wv[:, k, half * 512:(half + 1) * 512], start=(k == 0), stop=(k == 7)), reads=[bwv, bh[ti]], writes=[pb])
                    P.op("act", lambda e: e.activation(out=vst[vs][0:m, half * 512:(half + 1) * 512], in_=pt[0:m, :], func=AF.Copy), reads=[pb], writes=[bvst[vs]])
                P.dma("sp", v_o[t0:t0 + m, :], vst[vs][0:m, :], reads=[bvst[vs]], writes=[bvd])

        def attn_layer(j):
            A.reset()
            P.barrier()
            KT = A.alloc([8, TA], BF16)
            bKT = Buf("KT")
            P.dma("pool", KT, kT_o, reads=[bkTd], writes=[bKT])
            Vb = A.alloc([16, 1024], BF16)
            bVb = Buf("Vb")
            P.dma("pool", Vb, v_o[0:T, :].rearrange("(kt p) f -> p kt f", p=128), reads=[bvd], writes=[bVb])
            qaugb = A.alloc([H, 512], BF16, parts=4)
            bqa = Buf("qaug")
            P.dma("pool", qaugb, cst_d["qaug"], writes=[bqa])
            hT = A.alloc([8, 512], BF16)
            bh = [Buf("hat")]
            scr = A.alloc([3584])
            to = [A.alloc([512]) for _ in range(2)]
            bto = [Buf("to%d" % s) for s in range(2)]
            sqb2 = A.alloc([512], BF16)
            bsqb2 = Buf("sqb2")
            rr2 = A.alloc([512])
            brr2 = Buf("rr2")
            wq = [A.alloc([8, 128], BF16) for _ in range(2)]
            bwq = [Buf("wq%d" % s) for s in range(2)]
            wo = wq
            bwo = bwq
            qz = [[A.alloc([512], BF16) for _ in range(2)] for _ in range(2)]
            bqz = [Buf("qz0"), Buf("qz1")]
            pbuf = [A.alloc([512], BF16) for _ in range(4)]
            bpb = [Buf("pbuf%d" % s) for s in range(4)]
            oT = A.alloc([8, 512], BF16)
            boT = Buf("oT")
            sqb = A.alloc([512], BF16)
            bsqb = Buf("sqb")
            rr = scr[:, 0:512]
            brr = Buf("rr")
            rc = [scr[:, 512:1024], scr[:, 1024:1536]]
            brc = [Buf("rc%d" % s) for s in range(2)]
            tc = [scr[:, 1536:2048], scr[:, 2048:2560]]
            btc = [Buf("tc%d" % s) for s in range(2)]
            od = scr[:, 2560:3072]
            bod = Buf("od")
            new_tmpg()
            for sl in range(2):
                P.op("dve", lambda e: e.memset(qz[sl][0][64:128, :], 0.0), writes=[bqz[sl]])
                P.op("dve", lambda e: e.memset(qz[sl][1][0:64, :], 0.0), writes=[bqz[sl]])
            lam_init = 0.8 - 0.6 * math.exp(-0.3 * (j + N_A))
            P.op("dve", lambda e: e.tensor_scalar(out=gqs[:, j:j + 1], in0=V("gq2_%d" % j), scalar1=0.125, scalar2=None, op0=ALU.mult), reads=[bvec], writes=[bconst])
            P.op("dve", lambda e: e.tensor_scalar(out=gsubs[:, j:j + 1], in0=V("gsub%d" % j), scalar1=1.0 - lam_init, scalar2=None, op0=ALU.mult), reads=[bvec], writes=[bconst])
            rot["list"] = [0, 1, 2, 3]
            accO = [(banks[4], bbank[4]), (banks[5], bbank[5])]
            accD = [(banks[6], bbank[6]), (banks[7], bbank[7])]
            cnts = {"p": 0, "w": 0}

            def qproj(h, sl):
                s = cnts["w"] % 2
                cnts["w"] += 1
                P.dma("pool", wq[s], wview(w_q_d[j], h * 128, 128), writes=[bwq[s]])
                pq, pbq = banks[1], bbank[1]
                for k in range(8):
                    P.op("pe", lambda e: e.matmul(pq[:, :], lhsT=wq[s][:, k, :], rhs=hT[:, k, :], start=(k == 0), stop=(k == 7)), reads=[bwq[s], bh[0]], writes=[pbq])
                group_rstd(pq, pbq, 512, sqb, bsqb, rr, brr, bonesb[:], p2b=(banks[2], bbank[2]))
                P.op("dve", lambda e: e.scalar_tensor_tensor(out=qz[sl][0][0:64, :], in0=pq[0:64, :], scalar=gqs[0:64, j:j + 1], in1=rr[0:64, :], op0=ALU.mult, op1=ALU.mult), reads=[pbq, brr, bconst], writes=[bqz[sl]])
                P.op("dve", lambda e: e.scalar_tensor_tensor(out=qz[sl][1][64:128, :], in0=pq[64:128, :], scalar=gqs[64:128, j:j + 1], in1=rr[64:128, :], op0=ALU.mult, op1=ALU.mult), reads=[pbq, brr, bconst], writes=[bqz[sl]])

            for qb in range(4):
                P.barrier()
                mod_norm(hT, bh, 0, mods, bmods, 0, scr, tiles=[qb], loc=True, tw=64)
                P.barrier()
                qproj(0, 0)
                for h in range(H):
                    sl = h % 2
                    nkt = 4 * qb + 4
                    Sb = {}

                    def QK(kt):
                        jd = kt - 4 * qb
                        clo = 128 * jd if jd > 0 else 0
                        n = 512 - clo
                        var = jd + 12
                        for c in range(2):
                            bi = 2 * (kt % 2) + c
                            ps, pbs = banks[bi], bbank[bi]
                            Sb[(kt, c)] = (ps, pbs)
                            P.op("pe", lambda e: e.matmul(ps[:, 0:n], lhsT=KT[:, h, kt * 128:(kt + 1) * 128], rhs=qz[sl][c][:, clo:512], start=True, stop=False, skip_group_check=True), reads=[bKT, bqz[sl]], writes=[pbs])
                            P.op("pe", lambda e: e.matmul(ps[:, 0:n], lhsT=kaugb[0:4, var, :], rhs=qaugb[0:4, h, clo:512], start=False, stop=(jd < 0), skip_group_check=True), reads=[bconst, bqa], writes=[pbs])
                            if jd >= 0:
                                P.op("pe", lambda e: e.matmul(ps[:, 0:128], lhsT=identb[:], rhs=maskTb[:], start=False, stop=True, skip_group_check=True), reads=[bconst], writes=[pbs])

                    def PVD(kt):
                        jd = kt - 4 * qb
                        clo = 128 * jd if jd > 0 else 0
                        n = 512 - clo
                        for c in range(2):
                            ps, pbs = Sb[(kt, c)]
                            pi = cnts["p"] % 4
                            cnts["p"] += 1
                            P.op("act", lambda e: e.activation(out=pbuf[pi][:, 0:n], in_=ps[:, 0:n], func=AF.Exp), reads=[pbs], writes=[bpb[pi]])
                            P.op("pe", lambda e: e.matmul(accO[c][0][:, clo:512], lhsT=Vb[:, kt, h * 128:(h + 1) * 128], rhs=pbuf[pi][:, 0:n], start=(kt == 0), stop=(kt == nkt - 1), skip_group_check=True), reads=[bVb, bpb[pi]], writes=[accO[c][1]])
                            P.op("pe", lambda e: e.matmul(accD[c][0][:, clo:512], lhsT=ones1[:], rhs=pbuf[pi][:, 0:n], start=(kt == 0), stop=(kt == nkt - 1), skip_group_check=True), reads=[bconst, bpb[pi]], writes=[accD[c][1]])

                    QK(0)
                    for kt in range(nkt):
                        if kt + 1 < nkt:
                            QK(kt + 1)
                        PVD(kt)
                    if h + 1 < H:
                        qproj(h + 1, 1 - sl)
                    for c in range(2):
                        P.op("dve", lambda e: e.reciprocal(out=rc[c], in_=accD[c][0][:, :]), reads=[accD[c][1]], writes=[brc[c]])
                        P.op("act", lambda e: e.activation(out=to[c], in_=accO[c][0][:, :], func=AF.Copy), reads=[accO[c][1]], writes=[bto[c]])
                    for c in range(2):
                        P.op("dve", lambda e: e.tensor_tensor(out=tc[c], in0=to[c], in1=rc[c], op=ALU.mult), reads=[bto[c], brc[c]], writes=[btc[c]])
                    P.op("dve", lambda e: e.scalar_tensor_tensor(out=od, in0=tc[1], scalar=neglam[:, j:j + 1], in1=tc[0], op0=ALU.mult, op1=ALU.add), reads=[btc[0], btc[1], blam], writes=[bod])
                    P.op("act", lambda e: e.activation(out=sqb2, in_=od, func=AF.Square), reads=[bod], writes=[bsqb2])
                    p2, pb2 = banks[3], bbank[3]
                    P.op("pe", lambda e: e.matmul(p2[:, :], lhsT=onesh[:], rhs=sqb2, start=True, stop=True), reads=[bsqb2, bconst], writes=[pb2])
                    P.op("act", lambda e: e.activation(out=rr2, in_=p2[:, :], func=AF.Sqrt, bias=epsc[:, 0:1], scale=1.0), reads=[pb2, bconst], writes=[brr2])
                    P.op("dve", lambda e: e.reciprocal(out=rr2, in_=rr2), reads=[brr2], writes=[brr2])
                    P.op("dve", lambda e: e.scalar_tensor_tensor(out=oT[:, h, :], in0=od, scalar=gsubs[:, j:j + 1], in1=rr2, op0=ALU.mult, op1=ALU.mult), reads=[bod, brr2, bconst], writes=[boT])
                t0 = 512 * qb
                for oc in range(8):
                    s = cnts["w"] % 2
                    cnts["w"] += 1
                    P.dma("pool", wo[s], wview(w_o_d[j], oc * 128, 128), writes=[bwo[s]])
                    pt, pb = bank()
                    for k in range(8):
                        P.op("pe", lambda e: e.matmul(pt[:, :], lhsT=wo[s][:, k, :], rhs=oT[:, k, :], start=(k == 0), stop=(k == 7)), reads=[bwo[s], boT], writes=[pb])
                    resid_update(qb, t0, 512, oc, pt, pb, 16)
            rot["list"] = list(range(8))

            A.reset()
            P.barrier()
            hTs = A.alloc([8, NS], BF16)
            bhs = [Buf("hs")]
            scr = A.alloc([6656])
            wq = [A.alloc([8, 128], BF16) for _ in range(2)]
            bwq = [Buf("wqs%d" % s) for s in range(2)]
            wo = [A.alloc([8, 128], BF16) for _ in range(2)]
            bwo = [Buf("wos%d" % s) for s in range(2)]
            qs_all = A.alloc([8, NS])
            bqs = Buf("qs_all")
            sqs = A.alloc([NS], BF16)
            bsqs = Buf("sqs")
            rrs = A.alloc([NS])
            brrs = Buf("rrs")
            Rm = A.alloc([8, 128], BF16)
            bRm = Buf("Rm")
            qbc = A.alloc([1024], BF16)
            bqbc = Buf("qbc")
            NPF = 4
            kpg = [A.alloc([1024]) for _ in range(NPF)]
            bkpg = [Buf("kpg%d" % s) for s in range(NPF)]
            vpg = [A.alloc([1024]) for _ in range(NPF)]
            bvpg = [Buf("vpg%d" % s) for s in range(NPF)]
            vpb = [A.alloc([1024], BF16) for _ in range(NPF)]
            bvpb = [Buf("vpb%d" % s) for s in range(NPF)]
            prod = A.alloc([1024])
            bprod = Buf("prod")
            idxf = A.alloc([NS * NPG])
            idx = A.alloc([NS * NPG], I32)
            ptl = A.alloc([NS * NPG], I32)
            bidx = Buf("idx")
            Knew = A.alloc([1024])
            Vnewb = A.alloc([1024], BF16)
            Vnewf = prod
            bnew = Buf("new")
            On = A.alloc([1024])
            bOn = Buf("On")
            ods = A.alloc([1024])
            bods = Buf("ods")
            sq2 = prod
            bsq2 = bprod
            ms8 = A.alloc([8])
            bms8 = Buf("ms8")
            gsr = A.alloc([128])
            bgsr = Buf("gsr")
            oTs = A.alloc([8, NS], BF16)
            boTs = Buf("oTs")
            new_tmpg()
            P.dma("sp", ptl, pt_d, writes=[bidx])
            P.op("dve", lambda e: e.tensor_copy(out=idxf, in_=ptl), reads=[bidx], writes=[bidx])
            P.op("dve", lambda e: e.tensor_scalar(out=idxf, in0=idxf, scalar1=128.0, scalar2=iotap[:, 0:1], op0=ALU.mult, op1=ALU.add), reads=[bidx, bconst], writes=[bidx])
            P.op("dve", lambda e: e.tensor_copy(out=idx, in_=idxf), reads=[bidx], writes=[bidx])
            P.op("dve", lambda e: e.memset(Knew, 0.0), writes=[bnew])
            P.op("dve", lambda e: e.memset(Vnewf, 0.0), writes=[bprod])
            P.dma("sp", Knew[0:NS, :], kstok_o, reads=[bkTd], writes=[bnew])
            P.dma("sp", Vnewf[0:NS, :], v_o[T:TA, :], reads=[bvd], writes=[bprod])
            P.op("act", lambda e: e.activation(out=Vnewb, in_=Vnewf, func=AF.Copy), reads=[bprod], writes=[bnew])
            P.dma("sp", gsr[0:1, :], gsubrow_d[:, j, :], writes=[bgsr])
            P.op("dve", lambda e: e.tensor_scalar(out=gsr[0:1, :], in0=gsr[0:1, :], scalar1=1.0 - lam_init, scalar2=None, op0=ALU.mult), reads=[bgsr], writes=[bgsr])
            mod_norm(hTs, bhs, 0, mods, bmods, 0, scr, tiles=[4], loc=True)
            for h in range(H):
                s = h % 2
                P.dma("pool", wq[s], wview(w_q_d[j], h * 128, 128), writes=[bwq[s]])
                pq, pbq = bank()
                for k in range(8):
                    P.op("pe", lambda e: e.matmul(pq[:, 0:NS], lhsT=wq[s][:, k, :], rhs=hTs[:, k, :], start=(k == 0), stop=(k == 7)), reads=[bwq[s], bhs[0]], writes=[pbq])
                group_rstd(pq, pbq, NS, sqs, bsqs, rrs, brrs, bonesb[:])
                P.op("dve", lambda e: e.scalar_tensor_tensor(out=qs_all[:, h, :], in0=pq[:, 0:NS], scalar=gqs[:, j:j + 1], in1=rrs, op0=ALU.mult, op1=ALU.mult), reads=[pbq, brrs, bconst], writes=[bqs])
            rot["list"] = [0, 1]
            blockm = A.alloc([1024])
            coefm = A.alloc([2])
            coefc = A.alloc([1])
            bepi = Buf("epi")
            P.dma("sp", blockm[0:16, :], cst_d["blockm"], writes=[bepi])
            P.dma("sp", coefm[0:16, :], cst_d["coefm"], writes=[bepi])
            P.op("dve", lambda e: e.scalar_tensor_tensor(out=coefc[0:16, :], in0=coefm[0:16, 1:2], scalar=neglam[0:16, j:j + 1], in1=coefm[0:16, 0:1], op0=ALU.mult, op1=ALU.add), reads=[bepi, blam], writes=[bepi])
            qbcs = [qbc, A.alloc([1024], BF16)]
            bqbcs = [bqbc, Buf("qbc1")]
            Sp = [A.alloc([16]) for _ in range(3)]
            bSp = [Buf("Sp%d" % i) for i in range(3)]
            Pp = [A.alloc([16], BF16) for _ in range(3)]
            bPp = [Buf("Pp%d" % i) for i in range(3)]
            wcol = A.alloc([1])
            bwcol = Buf("wcol")
            masked = A.alloc([1024])
            bmasked = Buf("masked")
            Oacc = [[(banks[2], bbank[2]), (banks[3], bbank[3])], [(banks[4], bbank[4]), (banks[5], bbank[5])]]
            rowb = [(banks[6], bbank[6]), (banks[7], bbank[7])]
            gcnt = 0
            rcnt = 0
            for sm in range(NS):
                qb_ = qbcs[sm % 2]
                bqb_ = bqbcs[sm % 2]
                for h in range(H):
                    P.op("dve", lambda e: e.tensor_scalar(out=Rm[:, h, :], in0=identb[:], scalar1=qs_all[:, h, sm:sm + 1], scalar2=None, op0=ALU.mult), reads=[bconst, bqs], writes=[bRm])
                for half in range(2):
                    pb_, pbb_ = bank()
                    P.op("pe", lambda e: e.matmul(pb_[:, :], lhsT=ones1[:], rhs=Rm[:, 4 * half:4 * half + 4, :], start=True, stop=True), reads=[bRm, bconst], writes=[pbb_])
                    P.op("act", lambda e: e.activation(out=qb_[:, half * 512:(half + 1) * 512], in_=pb_[:, :], func=AF.Copy), reads=[pbb_], writes=[bqb_])
                Oa = Oacc[sm % 2]
                pden, pbden = bank()
                for pg in range(NPG + 1):
                    r = rcnt % 3
                    rcnt += 1
                    if pg < NPG:
                        g = gcnt % NPF
                        gcnt += 1
                        col = sm * NPG + pg
                        P.dma("pool", None, None, reads=[bidx], writes=[bkpg[g]], fn=(lambda e, g=g, col=col: e.indirect_dma_start(out=kpg[g], out_offset=None, in_=ck_d, in_offset=bass.IndirectOffsetOnAxis(ap=idx[:, col:col + 1], axis=0))))
                        P.dma("pool", None, None, reads=[bidx], writes=[bvpg[g]], fn=(lambda e, g=g, col=col: e.indirect_dma_start(out=vpg[g], out_offset=None, in_=cv_d, in_offset=bass.IndirectOffsetOnAxis(ap=idx[:, col:col + 1], axis=0))))
                        vb_ = vpb[g]
                        bvb_ = bvpb[g]
                        P.op("act", lambda e: e.activation(out=vb_, in_=vpg[g], func=AF.Copy), reads=[bvpg[g]], writes=[bvb_])
                        ksrc, bks_ = kpg[g], bkpg[g]
                    else:
                        ksrc, bks_ = Knew, bnew
                        vb_, bvb_ = Vnewb, bnew
                    P.op("dve", lambda e: e.tensor_tensor(out=prod, in0=ksrc, in1=qb_, op=ALU.mult), reads=[bks_, bqb_], writes=[bprod])
                    P.op("dve", lambda e: e.tensor_reduce(out=Sp[r], in_=prod.rearrange("p (g d) -> p g d", d=64), axis=AX.X, op=ALU.add), reads=[bprod], writes=[bSp[r]])
                    if pg < NPG:
                        P.op("dve", lambda e: e.tensor_tensor(out=Sp[r].rearrange("p (h c) -> p h c", c=2), in0=Sp[r].rearrange("p (h c) -> p h c", c=2), in1=alis[:, pg, :].unsqueeze(2).to_broadcast([128, H, 2]), op=ALU.add), reads=[bSp[r], bconst], writes=[bSp[r]])
                    else:
                        P.op("dve", lambda e: e.tensor_scalar(out=Sp[r], in0=Sp[r], scalar1=newmask[:, sm:sm + 1], scalar2=None, op0=ALU.add), reads=[bSp[r], bconst], writes=[bSp[r]])
                    P.op("act", lambda e: e.activation(out=Pp[r], in_=Sp[r], func=AF.Exp), reads=[bSp[r]], writes=[bPp[r]])
                    for half in range(2):
                        P.op("pe", lambda e: e.matmul(Oa[half][0][0:16, :], lhsT=Pp[r], rhs=vb_[:, half * 512:(half + 1) * 512], start=(pg == 0), stop=(pg == NPG)), reads=[bPp[r], bvb_], writes=[Oa[half][1]])
                    P.op("pe", lambda e: e.matmul(pden[0:16, 0:1], lhsT=Pp[r], rhs=ones1[:, 0:1], start=(pg == 0), stop=(pg == NPG)), reads=[bPp[r], bconst], writes=[pbden])
                P.op("dve", lambda e: e.reciprocal(out=wcol[0:16, :], in_=pden[0:16, 0:1]), reads=[pbden], writes=[bwcol])
                P.op("dve", lambda e: e.tensor_tensor(out=wcol[0:16, :], in0=wcol[0:16, :], in1=coefc[0:16, :], op=ALU.mult), reads=[bwcol, bepi], writes=[bwcol])
                for half in range(2):
                    P.op("dve", lambda e: e.tensor_tensor(out=masked[0:16, half * 512:(half + 1) * 512], in0=Oa[half][0][0:16, :], in1=blockm[0:16, half * 512:(half + 1) * 512], op=ALU.mult), reads=[Oa[half][1], bepi], writes=[bmasked])
                for half in range(2):
                    P.op("pe", lambda e: e.matmul(rowb[half][0][0:1, :], lhsT=wcol[0:16, 0:1], rhs=masked[0:16, half * 512:(half + 1) * 512], start=True, stop=True), reads=[bwcol, bmasked], writes=[rowb[half][1]])
                    P.op("act", lambda e: e.activation(out=ods[0:1, half * 512:(half + 1) * 512], in_=rowb[half][0][0:1, :], func=AF.Copy), reads=[rowb[half][1]], writes=[bods])
                P.op("dve", lambda e: e.tensor_tensor(out=On[0:1, 0:1024], in0=ods[0:1, :], in1=ods[0:1, :], op=ALU.mult), reads=[bods], writes=[bOn])
                P.op("dve", lambda e: e.tensor_reduce(out=ms8[0:1, :], in_=On[0:1, 0:1024].rearrange("p (h e) -> p h e", e=128), axis=AX.X, op=ALU.add), reads=[bOn], writes=[bms8])
                P.op("act", lambda e: e.activation(out=ms8[0:1, :], in_=ms8[0:1, :], func=AF.Sqrt, bias=epsc[0:1, 0:1], scale=1.0 / 128), reads=[bms8, bconst], writes=[bms8])
                P.op("dve", lambda e: e.reciprocal(out=ms8[0:1, :], in_=ms8[0:1, :]), reads=[bms8], writes=[bms8])
                P.op("dve", lambda e: e.tensor_tensor(out=ods[0:1, :].rearrange("p (h e) -> p h e", e=128), in0=ods[0:1, :].rearrange("p (h e) -> p h e", e=128), in1=ms8[0:1, :].unsqueeze(2).to_broadcast([1, 8, 128]), op=ALU.mult), reads=[bods, bms8], writes=[bods])
                P.op("dve", lambda e: e.tensor_tensor(out=ods[0:1, :].rearrange("p (h e) -> p h e", e=128), in0=ods[0:1, :].rearrange("p (h e) -> p h e", e=128), in1=gsr[0:1, :].unsqueeze(1).to_broadcast([1, 8, 128]), op=ALU.mult), reads=[bods, bgsr], writes=[bods])
                pc, pbc = bank()
                for h in range(H):
                    P.op("pe", lambda e: e.matmul(pc[:, h:h + 1], lhsT=ods[0:1, h * 128:(h + 1) * 128], rhs=ones1f[0:1, 0:1], start=True, stop=True, skip_group_check=True), reads=[bods, bconst], writes=[pbc])
                P.op("act", lambda e: e.activation(out=oTs[:, :, sm], in_=pc[:, 0:8], func=AF.Copy), reads=[pbc], writes=[boTs])
            rot["list"] = list(range(8))
            for oc in range(8):
                s = oc % 2
                P.dma("pool", wo[s], wview(w_o_d[j], oc * 128, 128), writes=[bwo[s]])
                pt, pb = bank()
                for k in range(8):
                    P.op("pe", lambda e: e.matmul(pt[:, 0:NS], lhsT=wo[s][:, k, :], rhs=oTs[:, k, :], start=(k == 0), stop=(k == 7)), reads=[bwo[s], boTs], writes=[pb])
                resid_update(4, T, NS, oc, pt, pb, 16)

        ada_all()
        for l in range(DEPTH):
            P.dma("sp", mods[:], mods_dram[l], reads=[bmodsd], writes=[bmods])
            make_gs(0, "g_mix%d" % l, 8, mods, bmods)
            make_gs(1, "g_ffn%d" % l, 32, mods, bmods)
            if l < N_A:
                lru_layer(l)
            else:
                attn_layer(l - N_A)
            ffn_layer(l)
            if l == N_A - 1:
                kv_phase()

        for ti, (t0, n) in enumerate(TILES):
            P.dma("sp", yT_o[:, :, t0:t0 + n], xres[:, :, t0:t0 + n], reads=[bx[ti]])
        P.finish()
        P.emit()
    return nc


_CACHE = {}


def kernel(**inp):
    inp = {k: np.asarray(v) for k, v in inp.items()}
    n_pool = inp["cache_k"].shape[0]
    if n_pool not in _CACHE:
        _CACHE[n_pool] = build_program(n_pool)
    nc = _CACHE[n_pool]
    vecs = _pack_vecs(inp)
    cst = _consts()
    lamv = np.stack([np.stack([inp["lam_q1"][j], inp["lam_k1"][j], inp["lam_q2"][j], inp["lam_k2"][j]]) for j in range(2)])
    lamv = np.ascontiguousarray(np.broadcast_to(lamv[None], (128, 2, 4, 64))).astype(np.float32)
    ck = np.ascontiguousarray(inp["cache_k"]).reshape(n_pool * 128, 1024)
    cv = np.ascontiguousarray(inp["cache_v"]).reshape(n_pool * 128, 1024)

    def fm(a):
        rows, F = a.shape
        return np.ascontiguousarray(a.reshape(rows, F // 128, 128).transpose(2, 1, 0))

    in_maps = []
    for i in range(NCORES):
        ss = slice(NS * i, NS * (i + 1))
        xa = np.concatenate([inp["x_prompt"][i], inp["x_sample"][ss, 0]], axis=0)
        ca = np.concatenate([inp["c_prompt"][i:i + 1], inp["c_sample"][ss]], axis=0)
        h0 = np.stack([fm(inp["state_lru_h"][a, ss]) for a in range(N_A)], axis=1)
        c0 = np.stack([np.stack([fm(inp["state_lru_conv"][a, ss, j]) for j in range(3)], axis=2) for a in range(N_A)], axis=1)
        f0 = np.stack([np.stack([fm(inp["state_ffn_conv"][l, ss, j]) for j in range(2)], axis=2) for l in range(DEPTH)], axis=1)
        ptb = np.ascontiguousarray(np.broadcast_to(inp["page_table"][ss].reshape(1, NS * NPG), (128, NS * NPG))).astype(np.int32)
        m = {
            "xT": fm(xa), "cT": fm(ca), "vecs": vecs, "lamv": lamv,
            "h0T": np.ascontiguousarray(h0), "c0T": np.ascontiguousarray(c0), "f0T": np.ascontiguousarray(f0),
            "ptb": ptb, "cache_k": ck, "cache_v": cv, "gsubrow": np.ascontiguousarray(inp["g_subln"].reshape(1, 2, 128)).astype(np.float32),
            "w_ada": inp["w_ada"], "w_lru_in": inp["w_lru_in"], "w_gate_x": inp["w_gate_x"], "w_gate_a": inp["w_gate_a"],
            "w_lru_out": inp["w_lru_out"], "w_ada_kv": inp["w_ada_kv"], "w_kv": inp["w_kv"], "w_q": inp["w_q"], "w_o": inp["w_o"],
            "w_up": inp["w_up"], "w_down": inp["w_down"],
        }
        for k, v in cst.items():
            m["c_" + k] = v
        in_maps.append(m)
    res = run_bass_kernel_spmd(nc, in_maps, core_ids=list(range(NCORES)))
    R = res.results

    def tm(a):
        return np.ascontiguousarray(a.transpose(2, 1, 0).reshape(a.shape[2], a.shape[1] * 128))

    B = NCORES
    y_p = np.zeros((B, T, D), np.float32)
    y_s = np.zeros((B * NS, 1, D), np.float32)
    k_p = np.zeros((B, T, H, 2, 64), np.float32)
    v_p = np.zeros((B, T, H, 128), np.float32)
    k_s = np.zeros((B * NS, 1, H, 2, 64), np.float32)
    v_s = np.zeros((B * NS, 1, H, 128), np.float32)
    lh_p = np.zeros((N_A, B, D), np.float32)
    lh_s = np.zeros((N_A, B * NS, D), np.float32)
    lc_p = np.zeros((N_A, B, 3, D), np.float32)
    lc_s = np.zeros((N_A, B * NS, 3, D), np.float32)
    fc_p = np.zeros((DEPTH, B, 2, 6 * D), np.float32)
    fc_s = np.zeros((DEPTH, B * NS, 2, 6 * D), np.float32)
    for i in range(NCORES):
        r = R[i]
        ss = slice(NS * i, NS * (i + 1))
        yt = tm(r["yT"])
        y_p[i] = yt[:T]
        y_s[ss, 0] = yt[T:]
        kt = tm(r["kT"])
        k_p[i] = kt[:T].reshape(T, H, 2, 64)
        k_s[ss, 0] = kt[T:].reshape(NS, H, 2, 64)
        v_p[i] = r["vtok"][:T].reshape(T, H, 128)
        v_s[ss, 0] = r["vtok"][T:].reshape(NS, H, 128)
        for a in range(N_A):
            hh = tm(r["lruh"][:, a])
            lh_p[a, i] = hh[0]
            lh_s[a, ss] = hh[1:]
            lc_p[a, i] = tm(r["lrucp"][:, a])
            for j in range(3):
                lc_s[a, ss, j] = tm(r["lrucs"][:, a, :, j, :])
        for l in range(DEPTH):
            fc_p[l, i] = tm(r["ffncp"][:, l])
            for j in range(2):
                fc_s[l, ss, j] = tm(r["ffncs"][:, l, :, j, :])
    return (y_p, y_s, k_p, v_p, k_s, v_s, lh_p, lh_s, lc_p, lc_s, fc_p, fc_s)
```

```python
import math
import contextlib
import numpy as np
import concourse.bass as bass
import concourse.mybir as mybir
from concourse.bass_utils import run_bass_kernel_spmd

F32 = mybir.dt.float32
BF16 = mybir.dt.bfloat16
I32 = mybir.dt.int32
AF = mybir.ActivationFunctionType
ALU = mybir.AluOpType
AX = mybir.AxisListType

NCORES = 8
D = 1024
T = 2048
NS = 16
TA = T + NS
H = 8
DEPTH = 4
N_A = 2
NPG = 16
TILES = [(0, 512), (512, 512), (1024, 512), (1536, 512), (2048, 16)]
EPS = 1e-6
NEG = -30000.0
SLOPES = [2.0 ** (-(h + 1)) for h in range(H)]

ENGS = ["pe", "act", "dve", "pool", "sp"]
EPOCH = 20000


class Buf:
    __slots__ = ("name", "w", "r", "dkey", "dcount")

    def __init__(self, name=""):
        self.name = name
        self.w = None
        self.r = {}
        self.dkey = None
        self.dcount = 0


class _Rec:
    def __getattr__(self, name):
        def f(*a, **k):
            self.call = (name, a, k)
            return self
        return f


class Prog:
    def __init__(self, nc):
        self.nc = nc
        self.q = {e: [] for e in ENGS}
        self.cnt = {e: 0 for e in ENGS}
        self.waited = {e: {} for e in ENGS}
        self.keys = []
        self.keyset = set()
        self.final = {}
        self.nd = 0
        self.free_dkeys = []
        self.dbufs = []

    def _key(self, k):
        if k not in self.keyset:
            self.keyset.add(k)
            self.keys.append(k)
        return k

    def _collect(self, eng, reads, writes):
        need = {}

        def add(ev, raw):
            if ev is None:
                return
            k, v = ev
            if (not raw) and k[0] == eng:
                return
            if need.get(k, 0) < v:
                need[k] = v

        for b in reads:
            add(b.w, True)
        for b in writes:
            add(b.w, False)
            for k, v in b.r.items():
                add((k, v), False)
        waits = []
        wd = self.waited[eng]
        for k, v in need.items():
            if wd.get(k, 0) < v:
                wd[k] = v
                waits.append((k, v))
        return waits

    def _commit(self, ev, reads, writes):
        k, v = ev
        for b in reads:
            if b.r.get(k, 0) < v:
                b.r[k] = v
        for b in writes:
            b.w = ev
            b.r = {}
        if self.final.get(k, 0) < v:
            self.final[k] = v

    def op(self, eng, fn, reads=(), writes=()):
        rec = _Rec()
        fn(rec)
        name, a, k = rec.call

        def fn(e, name=name, a=a, k=k):
            return getattr(e, name)(*a, **k)
        waits = self._collect(eng, reads, writes)
        self.cnt[eng] += 1
        c = self.cnt[eng]
        key = self._key((eng, (c - 1) // EPOCH))
        val = (c - 1) % EPOCH + 1
        self._commit((key, val), reads, writes)
        self.q[eng].append((waits, fn, key, 1))

    def I(self, eng, method, reads, writes, *args, **kw):
        def fn(e, method=method, args=args, kw=kw):
            return getattr(e, method)(*args, **kw)
        self.op(eng, fn, reads=reads, writes=writes)

    def dma(self, eng, out, in_, reads=(), writes=(), fn=None):
        waits = self._collect(eng, reads, writes)
        prim = writes[0] if len(writes) else reads[0]
        if prim.dkey is None:
            if self.free_dkeys:
                prim.dkey = self.free_dkeys.pop()
            else:
                prim.dkey = self._key(("d", self.nd))
                self.nd += 1
            prim.dcount = self.final.get(prim.dkey, 0)
            self.dbufs.append(prim)
        prim.dcount += 16
        ev = (prim.dkey, prim.dcount)
        self._commit(ev, reads, writes)
        if fn is None:
            def fn(e, out=out, in_=in_):
                return e.dma_start(out=out, in_=in_)
        self.q[eng].append((waits, fn, prim.dkey, 16))

    def barrier(self):
        for b in self.dbufs:
            if b.dkey is not None:
                self.free_dkeys.append(b.dkey)
                b.dkey = None
        self.dbufs = []
        snap = dict(self.final)
        for eng in ENGS:
            wd = self.waited[eng]
            waits = []
            for k, v in snap.items():
                if wd.get(k, 0) < v:
                    wd[k] = v
                    waits.append((k, v))
            if waits:
                self.q[eng].append((waits, None, None, 0))

    def finish(self):
        wd = self.waited["sp"]
        waits = []
        for k, v in self.final.items():
            if wd.get(k, 0) < v:
                wd[k] = v
                waits.append((k, v))
        self.q["sp"].append((waits, None, None, 0))

    def emit(self):
        nc = self.nc
        with contextlib.ExitStack() as st:
            sems = {}
            for i, k in enumerate(self.keys):
                sems[k] = st.enter_context(nc.semaphore("s%d" % i))
            block = st.enter_context(nc.Block())

            def run(e, lst):
                for waits, fn, key, inc in lst:
                    for k, v in waits:
                        e.wait_ge(sems[k], v)
                    if fn is not None:
                        fn(e).then_inc(sems[key], inc)

            @block.tensor
            def _(e):
                run(e, self.q["pe"])

            @block.scalar
            def _(e):
                run(e, self.q["act"])

            @block.vector
            def _(e):
                run(e, self.q["dve"])

            @block.gpsimd
            def _(e):
                run(e, self.q["pool"])

            @block.sync
            def _(e):
                run(e, self.q["sp"])


class Arena:
    def __init__(self, tensor, nfloats):
        self.t = tensor
        self.n = nfloats
        self.off = 0

    def reset(self):
        self.off = 0

    def alloc(self, shape, dt=F32, parts=128):
        n = 1
        for s in shape:
            n *= s
        nbytes = n * (2 if dt == BF16 else 4)
        nf = (nbytes + 3) // 4
        nf = (nf + 7) // 8 * 8
        assert self.off + nf <= self.n, ("arena overflow", self.off, nf, self.n)
        ap = self.t[0:parts, self.off:self.off + nf]
        if dt != F32:
            ap = ap.bitcast(dt)
        ap = ap[:, 0:n]
        if len(shape) == 2:
            ap = ap.rearrange("p (a b) -> p a b", a=shape[0])
        elif len(shape) == 3:
            ap = ap.rearrange("p (a b c) -> p a b c", a=shape[0], b=shape[1])
        self.last_off = self.off
        self.off += nf
        return ap


def _vec_layout():
    ent = []

    def add(name, ncols):
        ent.append((name, ncols))

    for l in range(DEPTH):
        add("b_ada%d" % l, 48)
        add("g_mix%d" % l, 8)
        add("g_ffn%d" % l, 8)
        for j in range(3):
            add("w_fc%d_%d" % (l, j), 48)
        add("b_fc%d" % l, 48)
    for i in range(N_A):
        for j in range(4):
            add("w_lc%d_%d" % (i, j), 8)
        add("b_lc%d" % i, 8)
        add("b_gx%d" % i, 8)
        add("b_ga%d" % i, 8)
        add("lrul%d" % i, 8)
    add("b_ada_kv", 16)
    add("g_kv", 8)
    add("gk2", 1)
    for j in range(2):
        add("gq2_%d" % j, 1)
        add("gsub%d" % j, 1)
    base = {}
    off = 0
    for name, n in ent:
        base[name] = off
        off += n
    return ent, base, off


VENT, VBASE, NV = _vec_layout()


def _pack_vecs(inp):
    cols = {}
    for l in range(DEPTH):
        cols["b_ada%d" % l] = inp["b_ada"][l]
        cols["g_mix%d" % l] = inp["g_norm_mix"][l]
        cols["g_ffn%d" % l] = inp["g_norm_ffn"][l]
        for j in range(3):
            cols["w_fc%d_%d" % (l, j)] = inp["w_ffn_conv"][l, j]
        cols["b_fc%d" % l] = inp["b_ffn_conv"][l]
    for i in range(N_A):
        for j in range(4):
            cols["w_lc%d_%d" % (i, j)] = inp["w_lru_conv"][i, j]
        cols["b_lc%d" % i] = inp["b_lru_conv"][i]
        cols["b_gx%d" % i] = inp["b_gate_x"][i]
        cols["b_ga%d" % i] = inp["b_gate_a"][i]
        cols["lrul%d" % i] = inp["lru_log_param"][i]
    cols["b_ada_kv"] = inp["b_ada_kv"]
    cols["g_kv"] = inp["g_norm_kv"]
    cols["gk2"] = np.concatenate([inp["g_k_norm"], inp["g_k_norm"]])
    for j in range(2):
        cols["gq2_%d" % j] = np.concatenate([inp["g_q_norm"][j], inp["g_q_norm"][j]])
        cols["gsub%d" % j] = inp["g_subln"][j]
    out = np.zeros((128, NV), np.float32)
    for name, n in VENT:
        v = np.asarray(cols[name], np.float32).reshape(n, 128)
        out[:, VBASE[name]:VBASE[name] + n] = v.T
    return out


def _consts():
    c = {}
    c["identf"] = np.eye(128, dtype=np.float32)
    k = np.arange(128)[:, None]
    q = np.arange(128)[None, :]
    c["maskT"] = np.where(k <= q, 0.0, NEG).astype(np.float32)
    bo = np.zeros((128, 128), np.float32)
    bo[:64, :64] = 1.0 / 64
    bo[64:, 64:] = 1.0 / 64
    c["bones"] = bo
    ka = np.zeros((4, 16, 128), np.float32)
    for v in range(16):
        ka[0, v] = np.arange(128)
        ka[1, v] = 1.0
        ka[2, v] = 1.0
        ka[3, v] = 128.0 * (v - 12)
    c["kaug"] = ka
    qa = np.zeros((4, H, 512), np.float32)
    ql = np.arange(512)
    qhi = (ql // 128) * 128
    qlo = ql - qhi
    for h in range(H):
        qa[0, h] = SLOPES[h]
        qa[1, h] = -SLOPES[h] * qhi
        qa[2, h] = -SLOPES[h] * qlo
        qa[3, h] = SLOPES[h]
    c["qaug"] = qa
    al = np.zeros((128, NPG, H), np.float32)
    for pg in range(NPG):
        for h in range(H):
            al[:, pg, h] = -SLOPES[h] * (T - (pg * 128 + np.arange(128)))
    c["alis"] = al
    nm = np.full((128, NS), NEG, np.float32)
    for s in range(NS):
        nm[s, s] = 0.0
    c["newmask"] = nm
    c["iotap"] = np.arange(128, dtype=np.float32).reshape(128, 1)
    bm = np.zeros((16, 1024), np.float32)
    for hc in range(16):
        bm[hc, (hc // 2) * 128:(hc // 2 + 1) * 128] = 1.0
    c["blockm"] = bm
    cm = np.zeros((16, 2), np.float32)
    cm[0::2, 0] = 1.0
    cm[1::2, 1] = 1.0
    c["coefm"] = cm
    return c


def build_program(n_pool):
    nc = bass.Bass("TRN2", target_bir_lowering=False)

    def din(name, shape, dt=F32):
        return nc.dram_tensor(name, list(shape), dt, kind="ExternalInput").ap()

    def dout(name, shape):
        return nc.dram_tensor(name, list(shape), F32, kind="ExternalOutput").ap()

    xT_d = din("xT", [128, 8, TA])
    cT_d = din("cT", [128, 8, 17])
    vecs_d = din("vecs", [128, NV])
    lamv_d = din("lamv", [128, 2, 4, 64])
    h0T_d = din("h0T", [128, N_A, 8, NS])
    c0T_d = din("c0T", [128, N_A, 8, 3, NS])
    f0T_d = din("f0T", [128, DEPTH, 48, 2, NS])
    pt_d = din("ptb", [128, NS * NPG], I32)
    ck_d = din("cache_k", [n_pool * 128, 1024])
    cv_d = din("cache_v", [n_pool * 128, 1024])
    w_ada_d = din("w_ada", [DEPTH, D, 6 * D])
    w_lin_d = din("w_lru_in", [N_A, D, 2 * D])
    w_gx_d = din("w_gate_x", [N_A, 8, 128, 128])
    w_ga_d = din("w_gate_a", [N_A, 8, 128, 128])
    w_lout_d = din("w_lru_out", [N_A, D, D])
    w_adakv_d = din("w_ada_kv", [D, 2 * D])
    w_kv_d = din("w_kv", [D, 2 * D])
    w_q_d = din("w_q", [2, D, D])
    w_o_d = din("w_o", [2, D, D])
    w_up_d = din("w_up", [DEPTH, D, 6 * D])
    w_dn_d = din("w_down", [DEPTH, 3 * D, D])
    gsubrow_d = din("gsubrow", [1, 2, 128])
    cst = _consts()
    cst_d = {k: din("c_" + k, v.shape) for k, v in cst.items()}
    kstok_o = dout("kstok", [NS, 1024])

    yT_o = dout("yT", [128, 8, TA])
    kT_o = dout("kT", [128, 8, TA])
    v_o = dout("vtok", [TA, 1024])
    lh_o = dout("lruh", [128, N_A, 8, 17])
    lcp_o = dout("lrucp", [128, N_A, 8, 3])
    lcs_o = dout("lrucs", [128, N_A, 8, 3, NS])
    fcp_o = dout("ffncp", [128, DEPTH, 48, 2])
    fcs_o = dout("ffncs", [128, DEPTH, 48, 2, NS])

    P = Prog(nc)
    st = contextlib.ExitStack()
    with st:
        def sb(name, shape, dt=F32):
            return st.enter_context(nc.sbuf_tensor("sb_" + name, list(shape), dt))

        xres = sb("xres", [128, 8, TA])
        bx = [Buf("x%d" % i) for i in range(5)]
        vecs = sb("vecs", [128, NV])
        bvec = Buf("vecs")
        mods = sb("mods", [128, 48, 17])
        bmods = Buf("mods")
        gsm = sb("gsm", [128, 3, 8, 17])
        bgsm = Buf("gsm")
        modkv = sb("modkv", [128, 16, 17])
        bmodkv = Buf("modkv")
        scT = sb("scT", [128, 8, 17], BF16)
        bscT = Buf("scT")
        identf = sb("identf", [128, 128])
        identb = sb("identb", [128, 128], BF16)
        maskTb = sb("maskTb", [128, 128], BF16)
        bonesb = sb("bonesb", [128, 128], BF16)
        onesd = sb("onesd", [128, 128], BF16)
        onesh = sb("onesh", [128, 128], BF16)
        ones1 = sb("ones1", [128, 128], BF16)
        ones1f = sb("ones1f", [128, 128])
        kaugb = sb("kaugb", [4, 16, 128], BF16)
        gqs = sb("gqs", [128, 2])
        gsubs = sb("gsubs", [128, 2])
        alis = sb("alis", [128, NPG, H])
        newmask = sb("newmask", [128, NS])
        iotap = sb("iotap", [128, 1])
        cL = sb("cL", [128, N_A * 8])
        neglam = sb("neglam", [128, 2])
        epsc = sb("epsc", [128, 1])
        bconst = Buf("const")
        bcL = Buf("cL")
        blam = Buf("lam")
        ARENA_F = 32000
        arena_t = sb("arena", [128, ARENA_F])
        A = Arena(arena_t, ARENA_F)

        banks = [st.enter_context(nc.psum_tensor("bank%d" % i, [128, 512], F32)) for i in range(8)]
        bbank = [Buf("bank%d" % i) for i in range(8)]
        rot = {"list": list(range(8)), "i": 0}

        def bank():
            i = rot["list"][rot["i"] % len(rot["list"])]
            rot["i"] += 1
            return banks[i], bbank[i]

        def V(name, j=0):
            c = VBASE[name] + j
            return vecs[:, c:c + 1]

        P.dma("sp", vecs[:], vecs_d, writes=[bvec])
        P.dma("sp", identf[:], cst_d["identf"], writes=[bconst])
        P.dma("pool", identb[:], cst_d["identf"], writes=[bconst])
        P.dma("pool", maskTb[:], cst_d["maskT"], writes=[bconst])
        P.dma("pool", bonesb[:], cst_d["bones"], writes=[bconst])
        P.dma("pool", kaugb[:], cst_d["kaug"], writes=[bconst])
        P.dma("sp", alis[:], cst_d["alis"], writes=[bconst])
        P.dma("sp", newmask[:], cst_d["newmask"], writes=[bconst])
        P.dma("sp", iotap[:], cst_d["iotap"], writes=[bconst])
        P.op("dve", lambda e: e.memset(onesd[:], 1.0 / 1024), writes=[bconst])
        P.op("dve", lambda e: e.memset(onesh[:], 1.0 / 128), writes=[bconst])
        P.op("dve", lambda e: e.memset(ones1[:], 1.0), writes=[bconst])
        P.op("dve", lambda e: e.memset(ones1f[:], 1.0), writes=[bconst])
        P.op("dve", lambda e: e.memset(epsc[:], EPS), writes=[bconst])
        for ti, (t0, n) in enumerate(TILES):
            P.dma("sp", xres[:, :, t0:t0 + n], xT_d[:, :, t0:t0 + n], writes=[bx[ti]])

        A.reset()
        cTs = A.alloc([8, 17])
        bcTs = Buf("cTs")
        P.dma("sp", cTs, cT_d, writes=[bcTs])
        P.op("act", lambda e: e.activation(out=scT[:], in_=cTs, func=AF.Silu), reads=[bcTs], writes=[bscT])
        for i in range(N_A):
            src = vecs[:, VBASE["lrul%d" % i]:VBASE["lrul%d" % i] + 8]
            dst = cL[:, i * 8:(i + 1) * 8]
            P.op("act", lambda e, src=src, dst=dst: e.activation(out=dst, in_=src, func=AF.Exp, scale=-1.0), reads=[bvec], writes=[bcL])
            P.op("act", lambda e, dst=dst: e.activation(out=dst, in_=dst, func=AF.Ln, bias=ones1f[:, 0:1], scale=1.0), reads=[bcL, bconst], writes=[bcL])
            P.op("act", lambda e, dst=dst: e.activation(out=dst, in_=dst, func=AF.Copy, scale=-8.0), reads=[bcL], writes=[bcL])
        lamv = A.alloc([2, 4, 64])
        blamv = Buf("lamv")
        lamp = A.alloc([2, 2, 64])
        lams = A.alloc([2, 2])
        P.dma("sp", lamv, lamv_d, writes=[blamv])
        for j in range(2):
            for pr in range(2):
                P.op("dve", lambda e, j=j, pr=pr: e.tensor_tensor(out=lamp[:, j, pr, :], in0=lamv[:, j, 2 * pr, :], in1=lamv[:, j, 2 * pr + 1, :], op=ALU.mult), reads=[blamv], writes=[blam])
                P.op("dve", lambda e, j=j, pr=pr: e.tensor_reduce(out=lams[:, j, pr:pr + 1], in_=lamp[:, j, pr, :], axis=AX.X, op=ALU.add), reads=[blam], writes=[blam])
        P.op("act", lambda e: e.activation(out=lams, in_=lams, func=AF.Exp), reads=[blam], writes=[blam])
        for j in range(2):
            lam_init = 0.8 - 0.6 * math.exp(-0.3 * (j + N_A))
            P.op("dve", lambda e, j=j, li=lam_init: e.scalar_tensor_tensor(out=neglam[:, j:j + 1], in0=lams[:, j, 1:2], scalar=-li, in1=lams[:, j, 0:1], op0=ALU.add, op1=ALU.subtract), reads=[blam], writes=[blam])

        def wview(w2d, c0, ncols):
            return w2d.rearrange("(kc p) n -> p kc n", p=128)[:, :, c0:c0 + ncols]

        mods_dram = nc.dram_tensor("mods_scr", [DEPTH, 128, 48, 17], F32, kind="Internal").ap()
        bmodsd = Buf("mods_dram")

        def ada_all():
            A.reset()
            P.barrier()
            NSL = 4
            slots = [A.alloc([8, 512], BF16) for _ in range(NSL)]
            bsl = [Buf("adaw%d" % i) for i in range(NSL)]
            gc = 0
            jobs = [(w_ada_d[l], "b_ada%d" % l, 48, mods, bmods, l) for l in range(DEPTH)] + [(w_adakv_d, "b_ada_kv", 16, modkv, bmodkv, None)]
            for wd, bname, nj, dst, bdst, l in jobs:
                ngrp = (nj * 128) // 512
                for g in range(ngrp):
                    s = gc % NSL
                    gc += 1
                    P.dma("pool", slots[s], wview(wd, g * 512, 512), writes=[bsl[s]])
                    pt, pb = bank()
                    for jj in range(4):
                        for k in range(8):
                            P.op("pe", lambda e: e.matmul(pt[:, jj * 32:jj * 32 + 17], lhsT=slots[s][:, k, jj * 128:(jj + 1) * 128], rhs=scT[:, k, :], start=(k == 0), stop=(k == 7)), reads=[bsl[s], bscT], writes=[pb])
                    for jj in range(4):
                        jx = g * 4 + jj
                        P.op("dve", lambda e: e.tensor_scalar(out=dst[:, jx, :], in0=pt[:, jj * 32:jj * 32 + 17], scalar1=V(bname, jx), scalar2=None, op0=ALU.add), reads=[pb, bvec], writes=[bdst])
                if l is not None:
                    P.dma("sp", mods_dram[l], mods[:], reads=[bmods], writes=[bmodsd])

        def make_gs(slot, gname, sc_off, src, bsrc):
            for c in range(8):
                P.op("dve", lambda e, c=c: e.tensor_scalar(out=gsm[:, slot, c, :], in0=src[:, sc_off + c, :], scalar1=1.0, scalar2=V(gname, c), op0=ALU.add, op1=ALU.mult), reads=[bsrc, bvec], writes=[bgsm])

        def mod_norm(hT, bh, slot, shift_src, bshift, shift_off, scr, tiles=None, loc=False, tw=256):
            sq = scr[:, 0:2048].bitcast(BF16).rearrange("p (a b) -> p a b", a=8)
            bsq = Buf("sq")
            rstd = scr[:, 2048:2560]
            brstd = Buf("rstd")
            tmps = [scr[:, 2560:2560 + 8 * tw].rearrange("p (a b) -> p a b", a=8), scr[:, 2560 + 8 * tw:2560 + 16 * tw].rearrange("p (a b) -> p a b", a=8)]
            btmps = [Buf("tmp0"), Buf("tmp1")]
            tcnt = 0
            for ti, (t0, n) in enumerate(TILES):
                if tiles is not None and ti not in tiles:
                    continue
                h0c = 0 if loc else t0
                bhh = bh[0] if loc else bh[ti]
                P.op("act", lambda e: e.activation(out=sq[:, :, 0:n], in_=xres[:, :, t0:t0 + n], func=AF.Square), reads=[bx[ti]], writes=[bsq])
                pt, pb = bank()
                for c in range(8):
                    P.op("pe", lambda e: e.matmul(pt[:, 0:n], lhsT=onesd[:], rhs=sq[:, c, 0:n], start=(c == 0), stop=(c == 7)), reads=[bsq, bconst], writes=[pb])
                P.op("act", lambda e: e.activation(out=rstd[:, 0:n], in_=pt[:, 0:n], func=AF.Sqrt, bias=epsc[:, 0:1], scale=1.0), reads=[pb, bconst], writes=[brstd])
                P.op("dve", lambda e: e.reciprocal(out=rstd[:, 0:n], in_=rstd[:, 0:n]), reads=[brstd], writes=[brstd])
                for c0 in range(0, n, tw):
                    m = min(tw, n - c0)
                    tmp = tmps[tcnt % 2]
                    btmp = btmps[tcnt % 2]
                    tcnt += 1
                    P.op("dve", lambda e: e.tensor_tensor(out=tmp[:, :, 0:m], in0=xres[:, :, t0 + c0:t0 + c0 + m], in1=rstd[:, c0:c0 + m].unsqueeze(1).to_broadcast([128, 8, m]), op=ALU.mult), reads=[bx[ti], brstd], writes=[btmp])
                    if ti < 4:
                        for c in range(8):
                            P.op("act", lambda e: e.activation(out=hT[:, c, h0c + c0:h0c + c0 + m], in_=tmp[:, c, 0:m], func=AF.Identity, scale=gsm[:, slot, c, 0:1], bias=shift_src[:, shift_off + c, 0:1]), reads=[btmp, bgsm, bshift], writes=[bhh])
                    else:
                        P.op("dve", lambda e: e.tensor_tensor(out=tmp[:, :, 0:m], in0=tmp[:, :, 0:m], in1=gsm[:, slot, :, 1:17], op=ALU.mult), reads=[btmp, bgsm], writes=[btmp])
                        P.op("dve", lambda e: e.tensor_tensor(out=hT[:, :, h0c:h0c + m], in0=tmp[:, :, 0:m], in1=shift_src[:, shift_off:shift_off + 8, 1:17], op=ALU.add), reads=[btmp, bshift], writes=[bhh])

        tmpg_holder = {}

        def new_tmpg():
            tmpg_holder["ap"] = A.alloc([16])
            tmpg_holder["buf"] = Buf("tmpg")

        def resid_update(ti, t0, n, oc, pt, pb, gate_off):
            if ti < 4:
                P.op("dve", lambda e: e.scalar_tensor_tensor(out=xres[:, oc, t0:t0 + n], in0=pt[:, 0:n], scalar=mods[:, gate_off + oc, 0:1], in1=xres[:, oc, t0:t0 + n], op0=ALU.mult, op1=ALU.add), reads=[pb, bmods, bx[ti]], writes=[bx[ti]])
            else:
                tmpg = tmpg_holder["ap"]
                bt = tmpg_holder["buf"]
                P.op("dve", lambda e: e.tensor_tensor(out=tmpg, in0=pt[:, 0:n], in1=mods[:, gate_off + oc, 1:17], op=ALU.mult), reads=[pb, bmods], writes=[bt])
                P.op("dve", lambda e: e.tensor_tensor(out=xres[:, oc, t0:t0 + n], in0=xres[:, oc, t0:t0 + n], in1=tmpg, op=ALU.add), reads=[bt, bx[ti]], writes=[bx[ti]])

        def lru_layer(i):
            A.reset()
            P.barrier()
            hT = A.alloc([8, TA], BF16)
            bh = [Buf("h%d" % t) for t in range(5)]
            yb = A.alloc([8, TA], BF16)
            by = [Buf("y%d" % t) for t in range(5)]
            wg = A.alloc([2, 8, 128], BF16)
            bwg = Buf("wg")
            P.dma("pool", wg[:, 0], w_gx_d[i].rearrange("n k j -> k n j"), writes=[bwg])
            P.dma("pool", wg[:, 1], w_ga_d[i].rearrange("n k j -> k n j"), writes=[bwg])
            wsl = [A.alloc([8, 256], BF16) for _ in range(2)]
            bws = [Buf("wlin%d" % s) for s in range(2)]
            h0 = A.alloc([8, NS])
            c0 = A.alloc([8, 3, NS])
            bst = Buf("lrustate")
            P.dma("sp", h0, h0T_d[:, i], writes=[bst])
            P.dma("sp", c0, c0T_d[:, i], writes=[bst])
            xbuf = A.alloc([T + 3])
            xs = A.alloc([NS])
            xc = A.alloc([TA])
            off_xc = A.last_off
            xcb = A.alloc([512], BF16)
            ta = A.alloc([TA])
            off_ta = A.last_off
            tom = A.alloc([TA])
            tgx = A.alloc([TA])
            hout = A.alloc([8, 17])
            new_tmpg()
            mod_norm(hT, bh, 0, mods, bmods, 0, arena_t[:, off_xc:off_xc + 6656])
            P.barrier()
            bxb, bxc, bxcb, bta, btom, btgx, bho = (Buf("xb"), Buf("xc"), Buf("xcb"), Buf("ta"), Buf("tom"), Buf("tgx"), Buf("hout"))
            P.op("dve", lambda e: e.memset(xbuf[:, 0:3], 0.0), writes=[bxb])
            for nch in range(8):
                s = nch % 2
                P.dma("pool", wsl[s][:, :, 0:128], wview(w_lin_d[i], nch * 128, 128), writes=[bws[s]])
                P.dma("pool", wsl[s][:, :, 128:256], wview(w_lin_d[i], D + nch * 128, 128), writes=[bws[s]])
                for ti, (t0, n) in enumerate(TILES):
                    pt, pb = bank()
                    for k in range(8):
                        P.op("pe", lambda e, k=k, t0=t0, n=n, pt=pt, s=s: e.matmul(pt[:, 0:n], lhsT=wsl[s][:, k, 0:128], rhs=hT[:, k, t0:t0 + n], start=(k == 0), stop=(k == 7)), reads=[bws[s], bh[ti]], writes=[pb])
                    if ti < 4:
                        P.op("act", lambda e, t0=t0, n=n, pt=pt: e.activation(out=xbuf[:, 3 + t0:3 + t0 + n], in_=pt[:, 0:n], func=AF.Copy), reads=[pb], writes=[bxb])
                    else:
                        P.op("act", lambda e, n=n, pt=pt: e.activation(out=xs, in_=pt[:, 0:n], func=AF.Copy), reads=[pb], writes=[bxb])
                P.dma("sp", lcp_o[:, i, nch, :], xbuf[:, T:T + 3], reads=[bxb])
                P.dma("sp", lcs_o[:, i, nch, 0:2, :], c0[:, nch, 1:3, :], reads=[bst])
                P.dma("sp", lcs_o[:, i, nch, 2, :], xs, reads=[bxb])
                wl = lambda j: V("w_lc%d_%d" % (i, j), nch)
                P.op("act", lambda e: e.activation(out=xc[:, 0:T], in_=xbuf[:, 0:T], func=AF.Identity, scale=wl(0), bias=V("b_lc%d" % i, nch)), reads=[bxb, bvec], writes=[bxc])
                for j in range(1, 4):
                    P.op("dve", lambda e, j=j: e.scalar_tensor_tensor(out=xc[:, 0:T], in0=xbuf[:, j:j + T], scalar=wl(j), in1=xc[:, 0:T], op0=ALU.mult, op1=ALU.add), reads=[bxb, bxc, bvec], writes=[bxc])
                P.op("act", lambda e: e.activation(out=xc[:, T:TA], in_=c0[:, nch, 0, :], func=AF.Identity, scale=wl(0), bias=V("b_lc%d" % i, nch)), reads=[bst, bvec], writes=[bxc])
                for j in range(1, 3):
                    P.op("dve", lambda e, j=j: e.scalar_tensor_tensor(out=xc[:, T:TA], in0=c0[:, nch, j, :], scalar=wl(j), in1=xc[:, T:TA], op0=ALU.mult, op1=ALU.add), reads=[bst, bxc, bvec], writes=[bxc])
                P.op("dve", lambda e: e.scalar_tensor_tensor(out=xc[:, T:TA], in0=xs, scalar=wl(3), in1=xc[:, T:TA], op0=ALU.mult, op1=ALU.add), reads=[bxb, bxc, bvec], writes=[bxc])
                for ti, (t0, n) in enumerate(TILES):
                    P.op("act", lambda e, t0=t0, n=n: e.activation(out=xcb[:, 0:n], in_=xc[:, t0:t0 + n], func=AF.Copy), reads=[bxc], writes=[bxcb])
                    pgx, pbx = bank()
                    P.op("pe", lambda e, n=n, pgx=pgx: e.matmul(pgx[:, 0:n], lhsT=wg[:, 0, nch, :], rhs=xcb[:, 0:n], start=True, stop=True), reads=[bwg, bxcb], writes=[pbx])
                    pga, pba = bank()
                    P.op("pe", lambda e, n=n, pga=pga: e.matmul(pga[:, 0:n], lhsT=wg[:, 1, nch, :], rhs=xcb[:, 0:n], start=True, stop=True), reads=[bwg, bxcb], writes=[pba])
                    P.op("act", lambda e, t0=t0, n=n, pgx=pgx: e.activation(out=tgx[:, t0:t0 + n], in_=pgx[:, 0:n], func=AF.Sigmoid, bias=V("b_gx%d" % i, nch)), reads=[pbx, bvec], writes=[btgx])
                    P.op("act", lambda e, t0=t0, n=n, pga=pga: e.activation(out=ta[:, t0:t0 + n], in_=pga[:, 0:n], func=AF.Sigmoid, bias=V("b_ga%d" % i, nch)), reads=[pba, bvec], writes=[bta])
                P.op("act", lambda e: e.activation(out=ta, in_=ta, func=AF.Exp, scale=cL[:, i * 8 + nch:i * 8 + nch + 1]), reads=[bta, bcL], writes=[bta])
                P.op("act", lambda e: e.activation(out=tom, in_=ta, func=AF.Square), reads=[bta], writes=[btom])
                P.op("act", lambda e: e.activation(out=tom, in_=tom, func=AF.Sqrt, scale=-1.0, bias=ones1f[:, 0:1]), reads=[btom, bconst], writes=[btom])
                P.op("dve", lambda e: e.tensor_tensor(out=tgx, in0=tgx, in1=tom, op=ALU.mult), reads=[btgx, btom], writes=[btgx])
                P.op("dve", lambda e: e.tensor_tensor(out=tgx, in0=tgx, in1=xc, op=ALU.mult), reads=[btgx, bxc], writes=[btgx])
                P.op("dve", lambda e: e.tensor_tensor_scan(out=tom[:, 0:T], data0=ta[:, 0:T], data1=tgx[:, 0:T], initial=0.0, op0=ALU.mult, op1=ALU.add), reads=[bta, btgx, btom], writes=[btom])
                P.op("dve", lambda e: e.tensor_tensor(out=tom[:, T:TA], in0=ta[:, T:TA], in1=h0[:, nch, :], op=ALU.mult), reads=[bta, bst, btom], writes=[btom])
                P.op("dve", lambda e: e.tensor_tensor(out=tom[:, T:TA], in0=tom[:, T:TA], in1=tgx[:, T:TA], op=ALU.add), reads=[btgx, btom], writes=[btom])
                P.op("act", lambda e: e.activation(out=hout[:, nch, :], in_=tom[:, T - 1:TA], func=AF.Copy), reads=[btom], writes=[bho])
                for ti, (t0, n) in enumerate(TILES):
                    pt, pb = bank()
                    for k in range(8):
                        P.op("pe", lambda e, k=k, t0=t0, n=n, pt=pt, s=s: e.matmul(pt[:, 0:n], lhsT=wsl[s][:, k, 128:256], rhs=hT[:, k, t0:t0 + n], start=(k == 0), stop=(k == 7)), reads=[bws[s], bh[ti]], writes=[pb])
                    P.op("act", lambda e, t0=t0, n=n, pt=pt: e.activation(out=xc[:, t0:t0 + n], in_=pt[:, 0:n], func=AF.Gelu_apprx_tanh), reads=[pb, bxc], writes=[bxc])
                    P.op("dve", lambda e, t0=t0, n=n: e.tensor_tensor(out=yb[:, nch, t0:t0 + n], in0=xc[:, t0:t0 + n], in1=tom[:, t0:t0 + n], op=ALU.mult), reads=[bxc, btom], writes=[by[ti]])
            P.dma("sp", lh_o[:, i], hout, reads=[bho])
            wo = [A.alloc([8, 128], BF16) for _ in range(2)]
            bwo = [Buf("wlo%d" % s) for s in range(2)]
            for oc in range(8):
                s = oc % 2
                P.dma("pool", wo[s], wview(w_lout_d[i], oc * 128, 128), writes=[bwo[s]])
                for ti, (t0, n) in enumerate(TILES):
                    pt, pb = bank()
                    for k in range(8):
                        P.op("pe", lambda e, k=k, t0=t0, n=n, pt=pt, s=s: e.matmul(pt[:, 0:n], lhsT=wo[s][:, k, :], rhs=yb[:, k, t0:t0 + n], start=(k == 0), stop=(k == 7)), reads=[bwo[s], by[ti]], writes=[pb])
                    resid_update(ti, t0, n, oc, pt, pb, 16)

        def ffn_layer(l):
            A.reset()
            P.barrier()
            hT = A.alloc([8, TA], BF16)
            bh = [Buf("hf%d" % t) for t in range(5)]
            NG = 8
            GC = 3
            actb = [A.alloc([GC, TA], BF16) for _ in range(2)]
            bact = [[Buf("act%d_%d" % (s, t)) for t in range(5)] for s in range(2)]
            wup = [A.alloc([8, 256], BF16) for _ in range(3)]
            bwup = [Buf("wup%d" % s) for s in range(3)]
            wcnt = 0
            wdn = [A.alloc([GC, 1024], BF16) for _ in range(2)]
            bwdn = [Buf("wdn%d" % s) for s in range(2)]
            f0 = A.alloc([48, 2, NS])
            bf0 = Buf("f0")
            P.dma("sp", f0, f0T_d[:, l], writes=[bf0])
            P.dma("sp", fcs_o[:, l, :, 0, :], f0[:, :, 1, :], reads=[bf0])
            ub = [A.alloc([T + 2]) for _ in range(2)]
            off_ub = A.last_off - (T + 2 + 7) // 8 * 8
            us = [A.alloc([NS]) for _ in range(2)]
            bub = [Buf("ub%d" % s) for s in range(2)]
            uc = [A.alloc([TA]) for _ in range(2)]
            off_uc = A.last_off - TA
            assert off_uc % 8 == 0
            buc = [Buf("uc%d" % s) for s in range(2)]
            new_tmpg()
            mod_norm(hT, bh, 1, mods, bmods, 24, arena_t[:, off_ub:off_ub + 6656])
            P.barrier()
            fout = A.alloc([48, 2])
            fouts = A.alloc([48, NS])
            bfo = Buf("fout")
            for s in range(2):
                P.op("dve", lambda e, s=s: e.memset(ub[s][:, 0:2], 0.0), writes=[bub[s]])
            for g in range(NG):
                s = g % 2
                P.dma("pool", wdn[s], w_dn_d[l].rearrange("(kc p) n -> p kc n", p=128)[:, g * GC:(g + 1) * GC, :], writes=[bwdn[s]])
                for jj in range(GC):
                    ws = wcnt % 3
                    wcnt += 1
                    P.dma("pool", wup[ws][:, :, 0:128], wview(w_up_d[l], (g * GC + jj) * 128, 128), writes=[bwup[ws]])
                    P.dma("pool", wup[ws][:, :, 128:256], wview(w_up_d[l], 3 * D + (g * GC + jj) * 128, 128), writes=[bwup[ws]])
                    for part in range(2):
                        ch = part * 24 + g * GC + jj
                        wc0 = part * 128
                        for ti, (t0, n) in enumerate(TILES):
                            pt, pb = bank()
                            for k in range(8):
                                P.op("pe", lambda e, k=k, t0=t0, n=n, pt=pt, s=s, wc0=wc0: e.matmul(pt[:, 0:n], lhsT=wup[ws][:, k, wc0:wc0 + 128], rhs=hT[:, k, t0:t0 + n], start=(k == 0), stop=(k == 7)), reads=[bwup[ws], bh[ti]], writes=[pb])
                            if ti < 4:
                                P.op("act", lambda e, t0=t0, n=n, pt=pt, part=part: e.activation(out=ub[part][:, 2 + t0:2 + t0 + n], in_=pt[:, 0:n], func=AF.Copy), reads=[pb], writes=[bub[part]])
                            else:
                                P.op("act", lambda e, n=n, pt=pt, part=part: e.activation(out=us[part], in_=pt[:, 0:n], func=AF.Copy), reads=[pb], writes=[bub[part]])
                        P.op("act", lambda e, ch=ch, part=part: e.activation(out=fout[:, ch, :], in_=ub[part][:, T:T + 2], func=AF.Copy), reads=[bub[part]], writes=[bfo])
                        P.op("act", lambda e, ch=ch, part=part: e.activation(out=fouts[:, ch, :], in_=us[part], func=AF.Copy), reads=[bub[part]], writes=[bfo])
                        wf = lambda j, ch=ch: V("w_fc%d_%d" % (l, j), ch)
                        bfc = V("b_fc%d" % l, ch)
                        P.op("act", lambda e, part=part, wf=wf, bfc=bfc: e.activation(out=uc[part][:, 0:T], in_=ub[part][:, 0:T], func=AF.Identity, scale=wf(0), bias=bfc), reads=[bub[part], bvec], writes=[buc[part]])
                        for j in range(1, 3):
                            P.op("dve", lambda e, j=j, part=part, wf=wf: e.scalar_tensor_tensor(out=uc[part][:, 0:T], in0=ub[part][:, j:j + T], scalar=wf(j), in1=uc[part][:, 0:T], op0=ALU.mult, op1=ALU.add), reads=[bub[part], buc[part], bvec], writes=[buc[part]])
                        P.op("act", lambda e, part=part, wf=wf, bfc=bfc, ch=ch: e.activation(out=uc[part][:, T:TA], in_=f0[:, ch, 0, :], func=AF.Identity, scale=wf(0), bias=bfc), reads=[bf0, bvec], writes=[buc[part]])
                        P.op("dve", lambda e, part=part, wf=wf, ch=ch: e.scalar_tensor_tensor(out=uc[part][:, T:TA], in0=f0[:, ch, 1, :], scalar=wf(1), in1=uc[part][:, T:TA], op0=ALU.mult, op1=ALU.add), reads=[bf0, buc[part], bvec], writes=[buc[part]])
                        P.op("dve", lambda e, part=part, wf=wf: e.scalar_tensor_tensor(out=uc[part][:, T:TA], in0=us[part], scalar=wf(2), in1=uc[part][:, T:TA], op0=ALU.mult, op1=ALU.add), reads=[bub[part], buc[part], bvec], writes=[buc[part]])
                    P.op("act", lambda e: e.activation(out=uc[0], in_=uc[0], func=AF.Gelu_apprx_tanh), reads=[buc[0]], writes=[buc[0]])
                    for ti, (t0, n) in enumerate(TILES):
                        P.op("dve", lambda e, t0=t0, n=n, jj=jj, s=s: e.tensor_tensor(out=actb[s][:, jj, t0:t0 + n], in0=uc[0][:, t0:t0 + n], in1=uc[1][:, t0:t0 + n], op=ALU.mult), reads=[buc[0], buc[1]], writes=[bact[s][ti]])
                for oc in range(8):
                    for ti, (t0, n) in enumerate(TILES):
                        pt, pb = bank()
                        for k in range(GC):
                            P.op("pe", lambda e, k=k, t0=t0, n=n, pt=pt, s=s, oc=oc: e.matmul(pt[:, 0:n], lhsT=wdn[s][:, k, oc * 128:(oc + 1) * 128], rhs=actb[s][:, k, t0:t0 + n], start=(k == 0), stop=(k == GC - 1)), reads=[bwdn[s], bact[s][ti]], writes=[pb])
                        resid_update(ti, t0, n, oc, pt, pb, 40)
            P.dma("sp", fcp_o[:, l], fout, reads=[bfo])
            P.dma("sp", fcs_o[:, l, :, 1, :], fouts, reads=[bfo])


        bkTd = Buf("kT_dram")
        bvd = Buf("v_dram")

        def group_rstd(pt, pb, n, sqb, bsqb, rr, brr, ones_ap, p2b=None):
            P.op("act", lambda e: e.activation(out=sqb[:, 0:n], in_=pt[:, 0:n], func=AF.Square), reads=[pb], writes=[bsqb])
            p2, pb2 = p2b if p2b is not None else bank()
            P.op("pe", lambda e: e.matmul(p2[:, 0:n], lhsT=ones_ap, rhs=sqb[:, 0:n], start=True, stop=True), reads=[bsqb, bconst], writes=[pb2])
            P.op("act", lambda e: e.activation(out=rr[:, 0:n], in_=p2[:, 0:n], func=AF.Sqrt, bias=epsc[:, 0:1], scale=1.0), reads=[pb2, bconst], writes=[brr])
            P.op("dve", lambda e: e.reciprocal(out=rr[:, 0:n], in_=rr[:, 0:n]), reads=[brr], writes=[brr])

        def kv_phase():
            make_gs(2, "g_kv", 8, modkv, bmodkv)
            A.reset()
            P.barrier()
            hT = A.alloc([8, TA], BF16)
            bh = [Buf("hkv%d" % t) for t in range(5)]
            scr = A.alloc([6656])
            mod_norm(hT, bh, 2, modkv, bmodkv, 0, scr)
            wk = [A.alloc([8, 128], BF16) for _ in range(2)]
            bwk = [Buf("wk%d" % s) for s in range(2)]
            wv = A.alloc([8, 1024], BF16)
            bwv = Buf("wv")
            P.dma("pool", wv, wview(w_kv_d, D, 1024), writes=[bwv])
            sqb = A.alloc([512], BF16)
            bsqb = Buf("sqb")
            rr = A.alloc([512])
            brr = Buf("rr")
            kst = [A.alloc([512]) for _ in range(2)]
            bkst = [Buf("kst%d" % s) for s in range(2)]
            vst = [A.alloc([1024]) for _ in range(2)]
            bvst = [Buf("vst%d" % s) for s in range(2)]
            kstk = A.alloc([1024])
            bkstk = Buf("kstk")
            cnt = 0
            for hc in range(8):
                s = hc % 2
                P.dma("pool", wk[s], wview(w_kv_d, hc * 128, 128), writes=[bwk[s]])
                for ti, (t0, n) in enumerate(TILES):
                    pt, pb = bank()
                    for k in range(8):
                        P.op("pe", lambda e: e.matmul(pt[:, 0:n], lhsT=wk[s][:, k, :], rhs=hT[:, k, t0:t0 + n], start=(k == 0), stop=(k == 7)), reads=[bwk[s], bh[ti]], writes=[pb])
                    group_rstd(pt, pb, n, sqb, bsqb, rr, brr, bonesb[:])
                    ks = cnt % 2
                    cnt += 1
                    P.op("dve", lambda e: e.scalar_tensor_tensor(out=kst[ks][:, 0:n], in0=pt[:, 0:n], scalar=V("gk2"), in1=rr[:, 0:n], op0=ALU.mult, op1=ALU.mult), reads=[pb, brr, bvec], writes=[bkst[ks]])
                    P.dma("sp", kT_o[:, hc, t0:t0 + n], kst[ks][:, 0:n], reads=[bkst[ks]], writes=[bkTd])
                    if ti == 4:
                        ptt, pbt = bank()
                        P.op("pe", lambda e: e.transpose(out=ptt[0:NS, 0:128], in_=kst[ks][:, 0:NS], identity=identf[:]), reads=[bkst[ks], bconst], writes=[pbt])
                        P.op("act", lambda e: e.activation(out=kstk[0:NS, hc * 128:(hc + 1) * 128], in_=ptt[0:NS, 0:128], func=AF.Copy), reads=[pbt], writes=[bkstk])
            P.dma("sp", kstok_o, kstk[0:NS, :], reads=[bkstk], writes=[bkTd])
            vtiles = [(t * 128, 128) for t in range(16)] + [(T, NS)]
            for vi, (t0, m) in enumerate(vtiles):
                ti = min(t0 // 512, 4)
                vs = vi % 2
                for half in range(2):
                    pt, pb = bank()
                    for k in range(8):
                        P.op("pe", lambda e: e.matmul(pt[0:m, :], lhsT=hT[:, k, t0:t0 + m], rhs=wv[:, k, half * 512:(half + 1) * 512], start=(k == 0), stop=(k == 7)), reads=[bwv, bh[ti]], writes=[pb])
                    P.op("act", lambda e: e.activation(out=vst[vs][0:m, half * 512:(half + 1) * 512], in_=pt[0:m, :], func=AF.Copy), reads=[pb], writes=[bvst[vs]])
                P.dma("sp", v_o[t0:t0 + m, :], vst[vs][0:m, :], reads=[bvst[vs]], writes=[bvd])

        def attn_layer(j):
            A.reset()
            P.barrier()
            KT = A.alloc([8, TA], BF16)
            bKT = Buf("KT")
            P.dma("pool", KT, kT_o, reads=[bkTd], writes=[bKT])
            Vb = A.alloc([16, 1024], BF16)
            bVb = Buf("Vb")
            P.dma("pool", Vb, v_o[0:T, :].rearrange("(kt p) f -> p kt f", p=128), reads=[bvd], writes=[bVb])
            qaugb = A.alloc([H, 512], BF16, parts=4)
            bqa = Buf("qaug")
            P.dma("pool", qaugb, cst_d["qaug"], writes=[bqa])
            hT = A.alloc([8, 512], BF16)
            bh = [Buf("hat")]
            scr = A.alloc([3584])
            to = [A.alloc([512]) for _ in range(2)]
            bto = [Buf("to%d" % s) for s in range(2)]
            sqb2 = A.alloc([512], BF16)
            bsqb2 = Buf("sqb2")
            rr2 = A.alloc([512])
            brr2 = Buf("rr2")
            wq = [A.alloc([8, 128], BF16) for _ in range(2)]
            bwq = [Buf("wq%d" % s) for s in range(2)]
            wo = wq
            bwo = bwq
            qz = [[A.alloc([512], BF16) for _ in range(2)] for _ in range(2)]
            bqz = [Buf("qz0"), Buf("qz1")]
            pbuf = [A.alloc([512], BF16) for _ in range(4)]
            bpb = [Buf("pbuf%d" % s) for s in range(4)]
            oT = A.alloc([8, 512], BF16)
            boT = Buf("oT")
            sqb = A.alloc([512], BF16)
            bsqb = Buf("sqb")
            rr = scr[:, 0:512]
            brr = Buf("rr")
            rc = [scr[:, 512:1024], scr[:, 1024:1536]]
            brc = [Buf("rc%d" % s) for s in range(2)]
            tc = [scr[:, 1536:2048], scr[:, 2048:2560]]
            btc = [Buf("tc%d" % s) for s in range(2)]
            od = scr[:, 2560:3072]
            bod = Buf("od")
            new_tmpg()
            for sl in range(2):
                P.op("dve", lambda e: e.memset(qz[sl][0][64:128, :], 0.0), writes=[bqz[sl]])
                P.op("dve", lambda e: e.memset(qz[sl][1][0:64, :], 0.0), writes=[bqz[sl]])
            lam_init = 0.8 - 0.6 * math.exp(-0.3 * (j + N_A))
            P.op("dve", lambda e: e.tensor_scalar(out=gqs[:, j:j + 1], in0=V("gq2_%d" % j), scalar1=0.125, scalar2=None, op0=ALU.mult), reads=[bvec], writes=[bconst])
            P.op("dve", lambda e: e.tensor_scalar(out=gsubs[:, j:j + 1], in0=V("gsub%d" % j), scalar1=1.0 - lam_init, scalar2=None, op0=ALU.mult), reads=[bvec], writes=[bconst])
            rot["list"] = [0, 1, 2, 3]
            accO = [(banks[4], bbank[4]), (banks[5], bbank[5])]
            accD = [(banks[6], bbank[6]), (banks[7], bbank[7])]
            cnts = {"p": 0, "w": 0}

            def qproj(h, sl):
                s = cnts["w"] % 2
                cnts["w"] += 1
                P.dma("pool", wq[s], wview(w_q_d[j], h * 128, 128), writes=[bwq[s]])
                pq, pbq = banks[1], bbank[1]
                for k in range(8):
                    P.op("pe", lambda e: e.matmul(pq[:, :], lhsT=wq[s][:, k, :], rhs=hT[:, k, :], start=(k == 0), stop=(k == 7)), reads=[bwq[s], bh[0]], writes=[pbq])
                group_rstd(pq, pbq, 512, sqb, bsqb, rr, brr, bonesb[:], p2b=(banks[2], bbank[2]))
                P.op("dve", lambda e: e.scalar_tensor_tensor(out=qz[sl][0][0:64, :], in0=pq[0:64, :], scalar=gqs[0:64, j:j + 1], in1=rr[0:64, :], op0=ALU.mult, op1=ALU.mult), reads=[pbq, brr, bconst], writes=[bqz[sl]])
                P.op("dve", lambda e: e.scalar_tensor_tensor(out=qz[sl][1][64:128, :], in0=pq[64:128, :], scalar=gqs[64:128, j:j + 1], in1=rr[64:128, :], op0=ALU.mult, op1=ALU.mult), reads=[pbq, brr, bconst], writes=[bqz[sl]])

            for qb in range(4):
                P.barrier()
                mod_norm(hT, bh, 0, mods, bmods, 0, scr, tiles=[qb], loc=True, tw=64)
                P.barrier()
                qproj(0, 0)
                for h in range(H):
                    sl = h % 2
                    nkt = 4 * qb + 4
                    Sb = {}

                    def QK(kt):
                        jd = kt - 4 * qb
                        clo = 128 * jd if jd > 0 else 0
                        n = 512 - clo
                        var = jd + 12
                        for c in range(2):
                            bi = 2 * (kt % 2) + c
                            ps, pbs = banks[bi], bbank[bi]
                            Sb[(kt, c)] = (ps, pbs)
                            P.op("pe", lambda e: e.matmul(ps[:, 0:n], lhsT=KT[:, h, kt * 128:(kt + 1) * 128], rhs=qz[sl][c][:, clo:512], start=True, stop=False, skip_group_check=True), reads=[bKT, bqz[sl]], writes=[pbs])
                            P.op("pe", lambda e: e.matmul(ps[:, 0:n], lhsT=kaugb[0:4, var, :], rhs=qaugb[0:4, h, clo:512], start=False, stop=(jd < 0), skip_group_check=True), reads=[bconst, bqa], writes=[pbs])
                            if jd >= 0:
                                P.op("pe", lambda e: e.matmul(ps[:, 0:128], lhsT=identb[:], rhs=maskTb[:], start=False, stop=True, skip_group_check=True), reads=[bconst], writes=[pbs])

                    def PVD(kt):
                        jd = kt - 4 * qb
                        clo = 128 * jd if jd > 0 else 0
                        n = 512 - clo
                        for c in range(2):
                            ps, pbs = Sb[(kt, c)]
                            pi = cnts["p"] % 4
                            cnts["p"] += 1
                            P.op("act", lambda e: e.activation(out=pbuf[pi][:, 0:n], in_=ps[:, 0:n], func=AF.Exp), reads=[pbs], writes=[bpb[pi]])
                            P.op("pe", lambda e: e.matmul(accO[c][0][:, clo:512], lhsT=Vb[:, kt, h * 128:(h + 1) * 128], rhs=pbuf[pi][:, 0:n], start=(kt == 0), stop=(kt == nkt - 1), skip_group_check=True), reads=[bVb, bpb[pi]], writes=[accO[c][1]])
                            P.op("pe", lambda e: e.matmul(accD[c][0][:, clo:512], lhsT=ones1[:], rhs=pbuf[pi][:, 0:n], start=(kt == 0), stop=(kt == nkt - 1), skip_group_check=True), reads=[bconst, bpb[pi]], writes=[accD[c][1]])

                    QK(0)
                    for kt in range(nkt):
                        if kt + 1 < nkt:
                            QK(kt + 1)
                        PVD(kt)
                    if h + 1 < H:
                        qproj(h + 1, 1 - sl)
                    for c in range(2):
                        P.op("dve", lambda e: e.reciprocal(out=rc[c], in_=accD[c][0][:, :]), reads=[accD[c][1]], writes=[brc[c]])
                        P.op("act", lambda e: e.activation(out=to[c], in_=accO[c][0][:, :], func=AF.Copy), reads=[accO[c][1]], writes=[bto[c]])
                    for c in range(2):
                        P.op("dve", lambda e: e.tensor_tensor(out=tc[c], in0=to[c], in1=rc[c], op=ALU.mult), reads=[bto[c], brc[c]], writes=[btc[c]])
                    P.op("dve", lambda e: e.scalar_tensor_tensor(out=od, in0=tc[1], scalar=neglam[:, j:j + 1], in1=tc[0], op0=ALU.mult, op1=ALU.add), reads=[btc[0], btc[1], blam], writes=[bod])
                    P.op("act", lambda e: e.activation(out=sqb2, in_=od, func=AF.Square), reads=[bod], writes=[bsqb2])
                    p2, pb2 = banks[3], bbank[3]
                    P.op("pe", lambda e: e.matmul(p2[:, :], lhsT=onesh[:], rhs=sqb2, start=True, stop=True), reads=[bsqb2, bconst], writes=[pb2])
                    P.op("act", lambda e: e.activation(out=rr2, in_=p2[:, :], func=AF.Sqrt, bias=epsc[:, 0:1], scale=1.0), reads=[pb2, bconst], writes=[brr2])
                    P.op("dve", lambda e: e.reciprocal(out=rr2, in_=rr2), reads=[brr2], writes=[brr2])
                    P.op("dve", lambda e: e.scalar_tensor_tensor(out=oT[:, h, :], in0=od, scalar=gsubs[:, j:j + 1], in1=rr2, op0=ALU.mult, op1=ALU.mult), reads=[bod, brr2, bconst], writes=[boT])
                t0 = 512 * qb
                for oc in range(8):
                    s = cnts["w"] % 2
                    cnts["w"] += 1
                    P.dma("pool", wo[s], wview(w_o_d[j], oc * 128, 128), writes=[bwo[s]])
                    pt, pb = bank()
                    for k in range(8):
                        P.op("pe", lambda e: e.matmul(pt[:, :], lhsT=wo[s][:, k, :], rhs=oT[:, k, :], start=(k == 0), stop=(k == 7)), reads=[bwo[s], boT], writes=[pb])
                    resid_update(qb, t0, 512, oc, pt, pb, 16)
            rot["list"] = list(range(8))

            A.reset()
            P.barrier()
            hTs = A.alloc([8, NS], BF16)
            bhs = [Buf("hs")]
            scr = A.alloc([6656])
            wq = [A.alloc([8, 128], BF16) for _ in range(2)]
            bwq = [Buf("wqs%d" % s) for s in range(2)]
            wo = [A.alloc([8, 128], BF16) for _ in range(2)]
            bwo = [Buf("wos%d" % s) for s in range(2)]
            qs_all = A.alloc([8, NS])
            bqs = Buf("qs_all")
            sqs = A.alloc([NS], BF16)
            bsqs = Buf("sqs")
            rrs = A.alloc([NS])
            brrs = Buf("rrs")
            Rm = A.alloc([8, 128], BF16)
            bRm = Buf("Rm")
            qbc = A.alloc([1024], BF16)
            bqbc = Buf("qbc")
            NPF = 5
            kpg = [A.alloc([1024]) for _ in range(NPF)]
            bkpg = [Buf("kpg%d" % s) for s in range(NPF)]
            vpg = [A.alloc([1024]) for _ in range(NPF)]
            bvpg = [Buf("vpg%d" % s) for s in range(NPF)]
            vpb = [A.alloc([1024], BF16) for _ in range(NPF)]
            bvpb = [Buf("vpb%d" % s) for s in range(NPF)]
            prod = A.alloc([1024])
            bprod = Buf("prod")
            idxf = A.alloc([NS * NPG])
            idx = A.alloc([NS * NPG], I32)
            ptl = A.alloc([NS * NPG], I32)
            bidx = Buf("idx")
            Knew = A.alloc([1024])
            Vnewb = A.alloc([1024], BF16)
            Vnewf = prod
            bnew = Buf("new")
            On = A.alloc([1024])
            bOn = Buf("On")
            ods = A.alloc([1024])
            bods = Buf("ods")
            sq2 = prod
            bsq2 = bprod
            ms8 = A.alloc([8])
            bms8 = Buf("ms8")
            gsr = A.alloc([128])
            bgsr = Buf("gsr")
            oTs = A.alloc([8, NS], BF16)
            boTs = Buf("oTs")
            new_tmpg()
            P.dma("sp", ptl, pt_d, writes=[bidx])
            P.op("dve", lambda e: e.tensor_copy(out=idxf, in_=ptl), reads=[bidx], writes=[bidx])
            P.op("dve", lambda e: e.tensor_scalar(out=idxf, in0=idxf, scalar1=128.0, scalar2=iotap[:, 0:1], op0=ALU.mult, op1=ALU.add), reads=[bidx, bconst], writes=[bidx])
            P.op("dve", lambda e: e.tensor_copy(out=idx, in_=idxf), reads=[bidx], writes=[bidx])
            P.op("dve", lambda e: e.memset(Knew, 0.0), writes=[bnew])
            P.op("dve", lambda e: e.memset(Vnewf, 0.0), writes=[bprod])
            P.dma("sp", Knew[0:NS, :], kstok_o, reads=[bkTd], writes=[bnew])
            P.dma("sp", Vnewf[0:NS, :], v_o[T:TA, :], reads=[bvd], writes=[bprod])
            P.op("act", lambda e: e.activation(out=Vnewb, in_=Vnewf, func=AF.Copy), reads=[bprod], writes=[bnew])
            P.dma("sp", gsr[0:1, :], gsubrow_d[:, j, :], writes=[bgsr])
            P.op("dve", lambda e: e.tensor_scalar(out=gsr[0:1, :], in0=gsr[0:1, :], scalar1=1.0 - lam_init, scalar2=None, op0=ALU.mult), reads=[bgsr], writes=[bgsr])
            mod_norm(hTs, bhs, 0, mods, bmods, 0, scr, tiles=[4], loc=True)
            for h in range(H):
                s = h % 2
                P.dma("pool", wq[s], wview(w_q_d[j], h * 128, 128), writes=[bwq[s]])
                pq, pbq = bank()
                for k in range(8):
                    P.op("pe", lambda e: e.matmul(pq[:, 0:NS], lhsT=wq[s][:, k, :], rhs=hTs[:, k, :], start=(k == 0), stop=(k == 7)), reads=[bwq[s], bhs[0]], writes=[pbq])
                group_rstd(pq, pbq, NS, sqs, bsqs, rrs, brrs, bonesb[:])
                P.op("dve", lambda e: e.scalar_tensor_tensor(out=qs_all[:, h, :], in0=pq[:, 0:NS], scalar=gqs[:, j:j + 1], in1=rrs, op0=ALU.mult, op1=ALU.mult), reads=[pbq, brrs, bconst], writes=[bqs])
            rot["list"] = [0, 1]
            blockm = A.alloc([1024])
            coefm = A.alloc([2])
            coefc = A.alloc([1])
            bepi = Buf("epi")
            P.dma("sp", blockm[0:16, :], cst_d["blockm"], writes=[bepi])
            P.dma("sp", coefm[0:16, :], cst_d["coefm"], writes=[bepi])
            P.op("dve", lambda e: e.scalar_tensor_tensor(out=coefc[0:16, :], in0=coefm[0:16, 1:2], scalar=neglam[0:16, j:j + 1], in1=coefm[0:16, 0:1], op0=ALU.mult, op1=ALU.add), reads=[bepi, blam], writes=[bepi])
            qbcs = [qbc, A.alloc([1024], BF16)]
            bqbcs = [bqbc, Buf("qbc1")]
            Sp = [A.alloc([16]) for _ in range(3)]
            bSp = [Buf("Sp%d" % i) for i in range(3)]
            Pp = [A.alloc([16], BF16) for _ in range(3)]
            bPp = [Buf("Pp%d" % i) for i in range(3)]
            wcol = A.alloc([1])
            bwcol = Buf("wcol")
            masked = A.alloc([1024])
            bmasked = Buf("masked")
            Oacc = [[(banks[2], bbank[2]), (banks[3], bbank[3])], [(banks[4], bbank[4]), (banks[5], bbank[5])]]
            rowb = [(banks[6], bbank[6]), (banks[7], bbank[7])]
            gcnt = 0
            rcnt = 0
            for sm in range(NS):
                qb_ = qbcs[sm % 2]
                bqb_ = bqbcs[sm % 2]
                for h in range(H):
                    P.op("dve", lambda e: e.tensor_scalar(out=Rm[:, h, :], in0=identb[:], scalar1=qs_all[:, h, sm:sm + 1], scalar2=None, op0=ALU.mult), reads=[bconst, bqs], writes=[bRm])
                for half in range(2):
                    pb_, pbb_ = bank()
                    P.op("pe", lambda e: e.matmul(pb_[:, :], lhsT=ones1[:], rhs=Rm[:, 4 * half:4 * half + 4, :], start=True, stop=True), reads=[bRm, bconst], writes=[pbb_])
                    P.op("act", lambda e: e.activation(out=qb_[:, half * 512:(half + 1) * 512], in_=pb_[:, :], func=AF.Copy), reads=[pbb_], writes=[bqb_])
                Oa = Oacc[sm % 2]
                pden, pbden = bank()
                for pg in range(NPG + 1):
                    r = rcnt % 3
                    rcnt += 1
                    if pg < NPG:
                        g = gcnt % NPF
                        gcnt += 1
                        col = sm * NPG + pg
                        P.dma("pool", None, None, reads=[bidx], writes=[bkpg[g]], fn=(lambda e, g=g, col=col: e.indirect_dma_start(out=kpg[g], out_offset=None, in_=ck_d, in_offset=bass.IndirectOffsetOnAxis(ap=idx[:, col:col + 1], axis=0))))
                        P.dma("pool", None, None, reads=[bidx], writes=[bvpg[g]], fn=(lambda e, g=g, col=col: e.indirect_dma_start(out=vpg[g], out_offset=None, in_=cv_d, in_offset=bass.IndirectOffsetOnAxis(ap=idx[:, col:col + 1], axis=0))))
                        vb_ = vpb[g]
                        bvb_ = bvpb[g]
                        P.op("act", lambda e: e.activation(out=vb_, in_=vpg[g], func=AF.Copy), reads=[bvpg[g]], writes=[bvb_])
                        ksrc, bks_ = kpg[g], bkpg[g]
                    else:
                        ksrc, bks_ = Knew, bnew
                        vb_, bvb_ = Vnewb, bnew
                    P.op("dve", lambda e: e.tensor_tensor(out=prod, in0=ksrc, in1=qb_, op=ALU.mult), reads=[bks_, bqb_], writes=[bprod])
                    P.op("dve", lambda e: e.tensor_reduce(out=Sp[r], in_=prod.rearrange("p (g d) -> p g d", d=64), axis=AX.X, op=ALU.add), reads=[bprod], writes=[bSp[r]])
                    if pg < NPG:
                        P.op("dve", lambda e: e.tensor_tensor(out=Sp[r].rearrange("p (h c) -> p h c", c=2), in0=Sp[r].rearrange("p (h c) -> p h c", c=2), in1=alis[:, pg, :].unsqueeze(2).to_broadcast([128, H, 2]), op=ALU.add), reads=[bSp[r], bconst], writes=[bSp[r]])
                    else:
                        P.op("dve", lambda e: e.tensor_scalar(out=Sp[r], in0=Sp[r], scalar1=newmask[:, sm:sm + 1], scalar2=None, op0=ALU.add), reads=[bSp[r], bconst], writes=[bSp[r]])
                    P.op("act", lambda e: e.activation(out=Pp[r], in_=Sp[r], func=AF.Exp), reads=[bSp[r]], writes=[bPp[r]])
                    for half in range(2):
                        P.op("pe", lambda e: e.matmul(Oa[half][0][0:16, :], lhsT=Pp[r], rhs=vb_[:, half * 512:(half + 1) * 512], start=(pg == 0), stop=(pg == NPG)), reads=[bPp[r], bvb_], writes=[Oa[half][1]])
                    P.op("pe", lambda e: e.matmul(pden[0:16, 0:1], lhsT=Pp[r], rhs=ones1[:, 0:1], start=(pg == 0), stop=(pg == NPG)), reads=[bPp[r], bconst], writes=[pbden])
                P.op("dve", lambda e: e.reciprocal(out=wcol[0:16, :], in_=pden[0:16, 0:1]), reads=[pbden], writes=[bwcol])
                P.op("dve", lambda e: e.tensor_tensor(out=wcol[0:16, :], in0=wcol[0:16, :], in1=coefc[0:16, :], op=ALU.mult), reads=[bwcol, bepi], writes=[bwcol])
                for half in range(2):
                    P.op("dve", lambda e: e.tensor_tensor(out=masked[0:16, half * 512:(half + 1) * 512], in0=Oa[half][0][0:16, :], in1=blockm[0:16, half * 512:(half + 1) * 512], op=ALU.mult), reads=[Oa[half][1], bepi], writes=[bmasked])
                for half in range(2):
                    P.op("pe", lambda e: e.matmul(rowb[half][0][0:1, :], lhsT=wcol[0:16, 0:1], rhs=masked[0:16, half * 512:(half + 1) * 512], start=True, stop=True), reads=[bwcol, bmasked], writes=[rowb[half][1]])
                    P.op("act", lambda e: e.activation(out=ods[0:1, half * 512:(half + 1) * 512], in_=rowb[half][0][0:1, :], func=AF.Copy), reads=[rowb[half][1]], writes=[bods])
                P.op("dve", lambda e: e.tensor_tensor(out=On[0:1, 0:1024], in0=ods[0:1, :], in1=ods[0:1, :], op=ALU.mult), reads=[bods], writes=[bOn])
                P.op("dve", lambda e: e.tensor_reduce(out=ms8[0:1, :], in_=On[0:1, 0:1024].rearrange("p (h e) -> p h e", e=128), axis=AX.X, op=ALU.add), reads=[bOn], writes=[bms8])
                P.op("act", lambda e: e.activation(out=ms8[0:1, :], in_=ms8[0:1, :], func=AF.Sqrt, bias=epsc[0:1, 0:1], scale=1.0 / 128), reads=[bms8, bconst], writes=[bms8])
                P.op("dve", lambda e: e.reciprocal(out=ms8[0:1, :], in_=ms8[0:1, :]), reads=[bms8], writes=[bms8])
                P.op("dve", lambda e: e.tensor_tensor(out=ods[0:1, :].rearrange("p (h e) -> p h e", e=128), in0=ods[0:1, :].rearrange("p (h e) -> p h e", e=128), in1=ms8[0:1, :].unsqueeze(2).to_broadcast([1, 8, 128]), op=ALU.mult), reads=[bods, bms8], writes=[bods])
                P.op("dve", lambda e: e.tensor_tensor(out=ods[0:1, :].rearrange("p (h e) -> p h e", e=128), in0=ods[0:1, :].rearrange("p (h e) -> p h e", e=128), in1=gsr[0:1, :].unsqueeze(1).to_broadcast([1, 8, 128]), op=ALU.mult), reads=[bods, bgsr], writes=[bods])
                pc, pbc = bank()
                for h in range(H):
                    P.op("pe", lambda e: e.matmul(pc[:, h:h + 1], lhsT=ods[0:1, h * 128:(h + 1) * 128], rhs=ones1f[0:1, 0:1], start=True, stop=True, skip_group_check=True), reads=[bods, bconst], writes=[pbc])
                P.op("act", lambda e: e.activation(out=oTs[:, :, sm], in_=pc[:, 0:8], func=AF.Copy), reads=[pbc], writes=[boTs])
            rot["list"] = list(range(8))
            for oc in range(8):
                s = oc % 2
                P.dma("pool", wo[s], wview(w_o_d[j], oc * 128, 128), writes=[bwo[s]])
                pt, pb = bank()
                for k in range(8):
                    P.op("pe", lambda e: e.matmul(pt[:, 0:NS], lhsT=wo[s][:, k, :], rhs=oTs[:, k, :], start=(k == 0), stop=(k == 7)), reads=[bwo[s], boTs], writes=[pb])
                resid_update(4, T, NS, oc, pt, pb, 16)

        ada_all()
        for l in range(DEPTH):
            P.dma("sp", mods[:], mods_dram[l], reads=[bmodsd], writes=[bmods])
            make_gs(0, "g_mix%d" % l, 8, mods, bmods)
            make_gs(1, "g_ffn%d" % l, 32, mods, bmods)
            if l < N_A:
                lru_layer(l)
            else:
                attn_layer(l - N_A)
            ffn_layer(l)
            if l == N_A - 1:
                kv_phase()

        for ti, (t0, n) in enumerate(TILES):
            P.dma("sp", yT_o[:, :, t0:t0 + n], xres[:, :, t0:t0 + n], reads=[bx[ti]])
        P.finish()
        P.emit()
    return nc


_CACHE = {}


def kernel(**inp):
    inp = {k: np.asarray(v) for k, v in inp.items()}
    n_pool = inp["cache_k"].shape[0]
    if n_pool not in _CACHE:
        _CACHE[n_pool] = build_program(n_pool)
    nc = _CACHE[n_pool]
    vecs = _pack_vecs(inp)
    cst = _consts()
    lamv = np.stack([np.stack([inp["lam_q1"][j], inp["lam_k1"][j], inp["lam_q2"][j], inp["lam_k2"][j]]) for j in range(2)])
    lamv = np.ascontiguousarray(np.broadcast_to(lamv[None], (128, 2, 4, 64))).astype(np.float32)
    ck = np.ascontiguousarray(inp["cache_k"]).reshape(n_pool * 128, 1024)
    cv = np.ascontiguousarray(inp["cache_v"]).reshape(n_pool * 128, 1024)

    def fm(a):
        rows, F = a.shape
        return np.ascontiguousarray(a.reshape(rows, F // 128, 128).transpose(2, 1, 0))

    in_maps = []
    for i in range(NCORES):
        ss = slice(NS * i, NS * (i + 1))
        xa = np.concatenate([inp["x_prompt"][i], inp["x_sample"][ss, 0]], axis=0)
        ca = np.concatenate([inp["c_prompt"][i:i + 1], inp["c_sample"][ss]], axis=0)
        h0 = np.stack([fm(inp["state_lru_h"][a, ss]) for a in range(N_A)], axis=1)
        c0 = np.stack([np.stack([fm(inp["state_lru_conv"][a, ss, j]) for j in range(3)], axis=2) for a in range(N_A)], axis=1)
        f0 = np.stack([np.stack([fm(inp["state_ffn_conv"][l, ss, j]) for j in range(2)], axis=2) for l in range(DEPTH)], axis=1)
        ptb = np.ascontiguousarray(np.broadcast_to(inp["page_table"][ss].reshape(1, NS * NPG), (128, NS * NPG))).astype(np.int32)
        m = {
            "xT": fm(xa), "cT": fm(ca), "vecs": vecs, "lamv": lamv,
            "h0T": np.ascontiguousarray(h0), "c0T": np.ascontiguousarray(c0), "f0T": np.ascontiguousarray(f0),
            "ptb": ptb, "cache_k": ck, "cache_v": cv, "gsubrow": np.ascontiguousarray(inp["g_subln"].reshape(1, 2, 128)).astype(np.float32),
            "w_ada": inp["w_ada"], "w_lru_in": inp["w_lru_in"], "w_gate_x": inp["w_gate_x"], "w_gate_a": inp["w_gate_a"],
            "w_lru_out": inp["w_lru_out"], "w_ada_kv": inp["w_ada_kv"], "w_kv": inp["w_kv"], "w_q": inp["w_q"], "w_o": inp["w_o"],
            "w_up": inp["w_up"], "w_down": inp["w_down"],
        }
        for k, v in cst.items():
            m["c_" + k] = v
        in_maps.append(m)
    res = run_bass_kernel_spmd(nc, in_maps, core_ids=list(range(NCORES)))
    R = res.results

    def tm(a):
        return np.ascontiguousarray(a.transpose(2, 1, 0).reshape(a.shape[2], a.shape[1] * 128))

    B = NCORES
    y_p = np.zeros((B, T, D), np.float32)
    y_s = np.zeros((B * NS, 1, D), np.float32)
    k_p = np.zeros((B, T, H, 2, 64), np.float32)
    v_p = np.zeros((B, T, H, 128), np.float32)
    k_s = np.zeros((B * NS, 1, H, 2, 64), np.float32)
    v_s = np.zeros((B * NS, 1, H, 128), np.float32)
    lh_p = np.zeros((N_A, B, D), np.float32)
    lh_s = np.zeros((N_A, B * NS, D), np.float32)
    lc_p = np.zeros((N_A, B, 3, D), np.float32)
    lc_s = np.zeros((N_A, B * NS, 3, D), np.float32)
    fc_p = np.zeros((DEPTH, B, 2, 6 * D), np.float32)
    fc_s = np.zeros((DEPTH, B * NS, 2, 6 * D), np.float32)
    for i in range(NCORES):
        r = R[i]
        ss = slice(NS * i, NS * (i + 1))
        yt = tm(r["yT"])
        y_p[i] = yt[:T]
        y_s[ss, 0] = yt[T:]
        kt = tm(r["kT"])
        k_p[i] = kt[:T].reshape(T, H, 2, 64)
        k_s[ss, 0] = kt[T:].reshape(NS, H, 2, 64)
        v_p[i] = r["vtok"][:T].reshape(T, H, 128)
        v_s[ss, 0] = r["vtok"][T:].reshape(NS, H, 128)
        for a in range(N_A):
            hh = tm(r["lruh"][:, a])
            lh_p[a, i] = hh[0]
            lh_s[a, ss] = hh[1:]
            lc_p[a, i] = tm(r["lrucp"][:, a])
            for j in range(3):
                lc_s[a, ss, j] = tm(r["lrucs"][:, a, :, j, :])
        for l in range(DEPTH):
            fc_p[l, i] = tm(r["ffncp"][:, l])
            for j in range(2):
                fc_s[l, ss, j] = tm(r["ffncs"][:, l, :, j, :])
    return (y_p, y_s, k_p, v_p, k_s, v_s, lh_p, lh_s, lc_p, lc_s, fc_p, fc_s)
```
